# Optimizing a Trainium2 kernel written in Bass

```python
import math
import jax, jax.numpy as jnp
from jax import lax
import numpy as np

D_MODEL = 2048
BATCH = 4
SEQ = 2048
DEPTH = 4

GRID_W = 64
CTX_LEN = 256
N_MIXERS = 2
S5_GROUP_WIDTH = 16
S5_GROUPS = D_MODEL // S5_GROUP_WIDTH
S5_STATE = 64
S5_DIRS = 2
DT_MIN = 1e-3
DT_MAX = 1e-1
CONV_WIDTH = 31
N_EXPERTS = 16
EC_CAPACITY_FACTOR = 2
EXPERT_FF = D_MODEL // 2
N_S5_LAYERS = (DEPTH + 1) // 2
N_CONV_LAYERS = DEPTH // 2
DEEPNORM_ALPHA = (2.0 * DEPTH) ** 0.25
DEEPNORM_BETA = (8.0 * DEPTH) ** -0.25
LN_EPS = 1e-5

kernel_name = 'hybrid_s5_conformer_ec_moe_diffusion'


def _layer_norm(x, g, b):
    xf = x.astype(jnp.float32)
    mu = jnp.mean(xf, axis=-1, keepdims=True)
    var = jnp.mean(jnp.square(xf - mu), axis=-1, keepdims=True)
    y = (xf - mu) * lax.rsqrt(var + LN_EPS)
    return (y * g.astype(jnp.float32) + b.astype(jnp.float32)).astype(x.dtype)


def _post_norm(x, y, g, b):
    return _layer_norm(DEEPNORM_ALPHA * x + y, g, b)


def _modulate(x, shift, scale):
    return x * (1.0 + scale) + shift


def _scan_combine(left, right):
    a_l, b_l = left
    a_r, b_r = right
    return a_l * a_r, a_r * b_l + b_r


def _s5_states(u, a_bar, b_bar, reverse, h0=None):
    bsz, length, _ = u.shape
    ug = u.astype(jnp.float32).reshape(bsz, length, S5_GROUPS, S5_GROUP_WIDTH).astype(jnp.complex64)
    bu = jnp.einsum('blgh,gph->blgp', ug, b_bar)
    if h0 is not None:
        edge = length - 1 if reverse else 0
        bu = bu.at[:, edge].add(a_bar * h0)
    a = jnp.broadcast_to(a_bar, bu.shape)
    _, h = lax.associative_scan(_scan_combine, (a, bu), axis=1, reverse=reverse)
    return h


def _s5_readout(h, c_mat):
    bsz, length = h.shape[:2]
    y = jnp.einsum('blgp,ghp->blgh', h, c_mat)
    return jnp.real(y).reshape(bsz, length, D_MODEL)


def _s5_output(u, y_fwd, y_bwd, d_skip, w_glu, b_glu):
    y = y_fwd + y_bwd + d_skip.astype(jnp.float32) * u.astype(jnp.float32)
    z = jax.nn.gelu(y).astype(u.dtype)
    val, gate = jnp.split(z @ w_glu + b_glu, 2, axis=-1)
    return val * jax.nn.sigmoid(gate)


def _s5_mixer(u_lat, u_ctx, a_re, a_im, log_dt, b_re, b_im, c_re, c_im, d_skip, w_glu, b_glu,
              ctx_out_needed):
    f32 = jnp.float32
    lam = lax.complex(a_re.astype(f32), a_im.astype(f32))
    dt = jnp.exp(log_dt.astype(f32))[..., None]
    a_bar = jnp.exp(lam * dt)
    b_bar = ((a_bar - 1.0) / lam)[..., None] * lax.complex(b_re.astype(f32), b_im.astype(f32))
    c_mat = lax.complex(c_re.astype(f32), c_im.astype(f32))
    h_ctx_f = _s5_states(u_ctx, a_bar[0], b_bar[0], reverse=False)
    h_ctx_b = _s5_states(u_ctx, a_bar[1], b_bar[1], reverse=True)
    y_lat_f = _s5_readout(_s5_states(u_lat, a_bar[0], b_bar[0], False, h_ctx_f[:, -1]), c_mat[0])
    y_lat_b = _s5_readout(_s5_states(u_lat, a_bar[1], b_bar[1], True, h_ctx_b[:, 0]), c_mat[1])
    y_lat = _s5_output(u_lat, y_lat_f, y_lat_b, d_skip, w_glu, b_glu)
    y_ctx = None
    if ctx_out_needed:
        y_ctx = _s5_output(u_ctx, _s5_readout(h_ctx_f, c_mat[0]), _s5_readout(h_ctx_b, c_mat[1]),
                           d_skip, w_glu, b_glu)
    return y_lat, y_ctx


def _depthwise_conv(h, w_dw, b_dw, rows):
    width = w_dw.shape[0]
    pad = width // 2
    chans = h.shape[-1]
    w = w_dw.astype(h.dtype)
    if rows is None:
        out = lax.conv_general_dilated(h, w[:, None, :], window_strides=(1,), padding=[(pad, pad)],
                                       dimension_numbers=('NWC', 'WIO', 'NWC'),
                                       feature_group_count=chans)
    else:
        bsz, length, _ = h.shape
        grid = h.reshape(bsz, rows, GRID_W, chans)
        out = lax.conv_general_dilated(grid, w[:, None, None, :], window_strides=(1, 1),
                                       padding=[(pad, pad), (0, 0)],
                                       dimension_numbers=('NHWC', 'HWIO', 'NHWC'),
                                       feature_group_count=chans).reshape(bsz, length, chans)
    return out + b_dw


def _conformer_conv(u, w_pw1, b_pw1, w_dw, b_dw, ln_g, ln_b, w_pw2, b_pw2, rows):
    a, g = jnp.split(u @ w_pw1 + b_pw1, 2, axis=-1)
    h = _depthwise_conv(a * jax.nn.sigmoid(g), w_dw, b_dw, rows)
    h = jax.nn.silu(_layer_norm(h, ln_g, ln_b))
    return h @ w_pw2 + b_pw2


def _expert_choice_ffn(u, w_router, w_in, w_out):
    bsz, n_tok, d = u.shape
    cap = EC_CAPACITY_FACTOR * n_tok // N_EXPERTS
    aff = jax.nn.softmax(jnp.einsum('btd,de->bte', u, w_router).astype(jnp.float32), axis=-1)
    gates, idx = lax.top_k(jnp.swapaxes(aff, 1, 2), cap)
    x_sel = jax.vmap(lambda ub, ib: ub[ib])(u, idx)
    g, up = jnp.split(jnp.einsum('becd,edf->becf', x_sel, w_in), 2, axis=-1)
    y = jnp.einsum('becf,efd->becd', jax.nn.silu(g) * up, w_out) * gates[..., None].astype(u.dtype)

    def combine(ib, yb):
        return jnp.zeros((n_tok, d), yb.dtype).at[ib.reshape(-1)].add(yb.reshape(-1, d))

    return jax.vmap(combine)(idx, y)


def setup_inputs(seed: int = 0) -> dict:
    key = jax.random.key(seed)
    ks = jax.random.split(key, 32)
    f32 = jnp.float32
    D, E, F = D_MODEL, N_EXPERTS, EXPERT_FF
    G, P, H, K = S5_GROUPS, S5_STATE, S5_GROUP_WIDTH, CONV_WIDTH
    NS, NC = N_S5_LAYERS, N_CONV_LAYERS

    def nrm(k, shape, s):
        return jax.random.normal(k, shape, f32) * s

    glu_col_scale = jnp.concatenate([jnp.full((D,), DEEPNORM_BETA, f32), jnp.ones((D,), f32)])
    n_idx = jnp.arange(P, dtype=f32)
    return {
        'x': nrm(ks[0], (BATCH, SEQ, D), 1.0),
        'c': nrm(ks[1], (BATCH, D), 1.0),
        'ctx': nrm(ks[2], (BATCH, CTX_LEN, D), 1.0),
        'c_ctx': nrm(ks[3], (D,), 1.0),
        'ada_w': nrm(ks[4], (DEPTH, D, 6 * D), 0.5 * D ** -0.5),
        'ada_b': nrm(ks[5], (DEPTH, 6 * D), 0.01),
        'ln_g': 1.0 + nrm(ks[6], (DEPTH, 2, D), 0.02),
        'ln_b': nrm(ks[7], (DEPTH, 2, D), 0.02),
        's5_a_re': -0.5 + nrm(ks[8], (NS, S5_DIRS, G, P), 0.01),
        's5_a_im': math.pi * n_idx + nrm(ks[9], (NS, S5_DIRS, G, P), 0.01),
        's5_log_dt': jax.random.uniform(ks[10], (NS, S5_DIRS, G), f32, math.log(DT_MIN), math.log(DT_MAX)),
        's5_b_re': nrm(ks[11], (NS, S5_DIRS, G, P, H), (2.0 * H) ** -0.5),
        's5_b_im': nrm(ks[12], (NS, S5_DIRS, G, P, H), (2.0 * H) ** -0.5),
        's5_c_re': nrm(ks[13], (NS, S5_DIRS, G, H, P), P ** -0.5),
        's5_c_im': nrm(ks[14], (NS, S5_DIRS, G, H, P), P ** -0.5),
        's5_d': nrm(ks[15], (NS, D), 1.0),
        's5_w_glu': nrm(ks[16], (NS, D, 2 * D), D ** -0.5) * glu_col_scale,
        's5_b_glu': nrm(ks[17], (NS, 2 * D), 0.01),
        'cv_w_pw1': nrm(ks[18], (NC, D, 2 * D), D ** -0.5),
        'cv_b_pw1': nrm(ks[19], (NC, 2 * D), 0.01),
        'cv_w_dw': nrm(ks[20], (NC, K, D), K ** -0.5),
        'cv_b_dw': nrm(ks[21], (NC, D), 0.01),
        'cv_ln_g': 1.0 + nrm(ks[22], (NC, D), 0.02),
        'cv_ln_b': nrm(ks[23], (NC, D), 0.02),
        'cv_w_pw2': nrm(ks[24], (NC, D, D), DEEPNORM_BETA * D ** -0.5),
        'cv_b_pw2': nrm(ks[25], (NC, D), 0.01),
        'moe_w_router': nrm(ks[26], (DEPTH, D, E), D ** -0.5),
        'moe_w_in': nrm(ks[27], (DEPTH, E, D, 2 * F), D ** -0.5),
        'moe_w_out': nrm(ks[28], (DEPTH, E, F, D), DEEPNORM_BETA * F ** -0.5),
    }


def reference(x, c, ctx, c_ctx, ada_w, ada_b, ln_g, ln_b, s5_a_re, s5_a_im, s5_log_dt, s5_b_re,
              s5_b_im, s5_c_re, s5_c_im, s5_d, s5_w_glu, s5_b_glu, cv_w_pw1, cv_b_pw1, cv_w_dw,
              cv_b_dw, cv_ln_g, cv_ln_b, cv_w_pw2, cv_b_pw2, moe_w_router, moe_w_in, moe_w_out):
    rows = x.shape[1] // GRID_W
    cond_lat = jax.nn.silu(c)
    cond_ctx = jax.nn.silu(c_ctx)
    x_lat, x_ctx = x, ctx
    for i in range(DEPTH):
        is_s5 = (i % N_MIXERS) == 0
        j = i // N_MIXERS
        ctx_out_needed = any((k % N_MIXERS) == 0 for k in range(i + 1, DEPTH))
        ctx_read = is_s5 or ctx_out_needed

        m_lat = (cond_lat @ ada_w[i] + ada_b[i])[:, None, :]
        sh1, sc1, g1, sh2, sc2, g2 = jnp.split(m_lat, 6, axis=-1)
        u_lat = _modulate(x_lat, sh1, sc1)
        if ctx_read:
            cm = jnp.split(cond_ctx @ ada_w[i] + ada_b[i], 6)
            u_ctx = _modulate(x_ctx, cm[0], cm[1])

        if is_s5:
            y_lat, y_ctx = _s5_mixer(u_lat, u_ctx, s5_a_re[j], s5_a_im[j], s5_log_dt[j], s5_b_re[j],
                                     s5_b_im[j], s5_c_re[j], s5_c_im[j], s5_d[j], s5_w_glu[j],
                                     s5_b_glu[j], ctx_out_needed)
        else:
            conv_args = (cv_w_pw1[j], cv_b_pw1[j], cv_w_dw[j], cv_b_dw[j], cv_ln_g[j], cv_ln_b[j],
                         cv_w_pw2[j], cv_b_pw2[j])
            y_lat = _conformer_conv(u_lat, *conv_args, rows)
            y_ctx = _conformer_conv(u_ctx, *conv_args, None) if ctx_out_needed else None

        x_lat = _post_norm(x_lat, g1 * y_lat, ln_g[i, 0], ln_b[i, 0])
        f_lat = _expert_choice_ffn(_modulate(x_lat, sh2, sc2), moe_w_router[i], moe_w_in[i], moe_w_out[i])
        x_lat = _post_norm(x_lat, g2 * f_lat, ln_g[i, 1], ln_b[i, 1])

        if ctx_out_needed:
            x_ctx = _post_norm(x_ctx, cm[2] * y_ctx, ln_g[i, 0], ln_b[i, 0])
            f_ctx = _expert_choice_ffn(_modulate(x_ctx, cm[3], cm[4]), moe_w_router[i], moe_w_in[i],
                                       moe_w_out[i])
            x_ctx = _post_norm(x_ctx, cm[5] * f_ctx, ln_g[i, 1], ln_b[i, 1])
    return x_lat
```

```python
import math
from contextlib import ExitStack

import numpy as np
import concourse.bass as bass
import concourse.mybir as mybir
from concourse.ap import AP
from concourse.bass_utils import run_bass_kernel_spmd

F32 = mybir.dt.float32
BF16 = mybir.dt.bfloat16
I32 = mybir.dt.int32
U32 = mybir.dt.uint32
AF = mybir.ActivationFunctionType
ALU = mybir.AluOpType

COMPUTE = ("pe", "act", "dve", "pool")
DMAQ = ("sp", "act", "pool")
NDMASEM = 8


class Op:
    __slots__ = ("eng", "fn", "deps", "is_dma", "needed", "semval")

    def __init__(self, eng, fn, is_dma):
        self.eng = eng
        self.fn = fn
        self.deps = []
        self.is_dma = is_dma
        self.needed = False
        self.semval = None


class FW:
    def __init__(self, nc):
        self.nc = nc
        self.streams = {e: [] for e in ("pe", "act", "dve", "pool", "sp")}
        self.last_w = {}
        self.readers = {}
        self.bar_idx = {e: 0 for e in self.streams}

    def op(self, eng, fn, reads=(), writes=(), dma=False):
        o = Op(eng, fn, dma)
        deps = {}
        for k in reads:
            lw = self.last_w.get(k)
            if lw is not None:
                deps[id(lw)] = lw
        for k in writes:
            lw = self.last_w.get(k)
            if lw is not None:
                deps[id(lw)] = lw
            for r in self.readers.get(k, ()):
                deps[id(r)] = r
        for d in deps.values():
            if (not d.is_dma) and (not dma) and d.eng == eng and eng == "pe":
                continue
            o.deps.append(d)
            d.needed = True
        for k in writes:
            self.last_w[k] = o
            self.readers[k] = []
        for k in reads:
            if k in writes:
                continue
            lst = self.readers.setdefault(k, [])
            if not dma:
                lst[:] = [r for r in lst if not (r.eng == eng and not r.is_dma)]
            lst.append(o)
        self.streams[eng].append(o)
        return o

    def barrier(self):
        lastops = []
        for e, st in self.streams.items():
            for o in reversed(st):
                if (not o.is_dma) and o.fn is not None:
                    lastops.append(o)
                    break
            lastops += [o for o in st[self.bar_idx[e]:] if o.is_dma]
        for e in self.streams:
            o = Op(e, None, False)
            o.deps = list(lastops)
            self.streams[e].append(o)
        for d in lastops:
            d.needed = True
        for e in self.streams:
            self.bar_idx[e] = len(self.streams[e])
        self.last_w = {}
        self.readers = {}

    def emit(self, final_wait_ops=()):
        nc = self.nc
        with ExitStack() as es:
            csem = {e: es.enter_context(nc.semaphore("c_" + e)) for e in COMPUTE}
            dsems = {e: [es.enter_context(nc.semaphore("d_%s%d" % (e, i))) for i in range(NDMASEM)]
                     for e in DMAQ}
            for e in COMPUTE:
                cnt = 0
                for o in self.streams[e]:
                    if o.is_dma or o.fn is None:
                        continue
                    if o.needed:
                        cnt += 1
                        o.semval = (csem[e], cnt)
            for e in DMAQ:
                dcount = [0] * NDMASEM
                rr = 0
                for o in self.streams[e]:
                    if not o.is_dma:
                        continue
                    dcount[rr] += 1
                    o.semval = (dsems[e][rr], 16 * dcount[rr])
                    rr = (rr + 1) % NDMASEM
            block = es.enter_context(nc.Block())
            handles = {"pe": block.tensor, "act": block.scalar, "dve": block.vector,
                       "pool": block.gpsimd, "sp": block.sync}
            for e in ("sp", "pool", "act", "dve", "pe"):
                ops = self.streams[e]
                finals = list(final_wait_ops) if e == "sp" else []

                def body(eng, ops=ops, finals=finals):
                    waited = {}

                    def wait(sem, val):
                        k = id(sem)
                        if waited.get(k, 0) >= val:
                            return
                        waited[k] = val
                        eng.wait_ge(sem, val)

                    for o in ops:
                        for d in o.deps:
                            wait(*d.semval)
                        if o.fn is None:
                            continue
                        if o.is_dma:
                            sem, val = o.semval
                            if val > 16:
                                wait(sem, val - 16)
                            o.fn(eng).then_inc(sem, 16)
                        else:
                            ins = o.fn(eng)
                            if o.needed:
                                ins.then_inc(o.semval[0], 1)
                    for o in finals:
                        wait(*o.semval)

                handles[e](body)


class Cfg:
    def __init__(self, D=2048, L=2048, LC=256, DEPTH=4):
        self.D, self.L, self.LC, self.DEPTH = D, L, LC, DEPTH
        self.E = 16
        self.GW = 64
        self.KW = 31
        self.NK = D // 128
        self.G = D // 16
        self.NB = self.G // 16
        self.F = D // 2
        self.NF = self.F // 128
        self.NT = min(512, D)
        self.TALL = L + LC
        self.NLT = L // 128
        self.NCT = LC // 128
        self.NTT = self.TALL // 128
        self.CAPL = 2 * L // 16
        self.CAPC = 2 * LC // 16
        self.NCL = L // 8
        self.NCC = LC // 8
        self.NC = self.NCL + self.NCC
        self.ALPHA = (2.0 * DEPTH) ** 0.25
        self.NS5 = (DEPTH + 1) // 2
        self.NCV = DEPTH // 2


LN_EPS = 1e-5
TWO_PI = 2.0 * math.pi


def fview(t, off, dims, p0=0, pn=None):
    base = t[:]
    pstep, pcount = base.ap[0]
    if pn is None:
        pn = pcount - p0
    return AP(base.tensor, base.offset + p0 * pstep + off, [[pstep, pn]] + [list(d) for d in dims])


class KB:
    def __init__(self, cfg, debug_outs=()):
        self.cfg = cfg
        self.nc = bass.Bass("TRN2", target_bir_lowering=False)
        self.f = FW(self.nc)
        self.debug_outs = set(debug_outs)
        self.uid = 0
        self.out_stores = []

    def din(self, name, shape, dt=F32):
        return self.nc.dram_tensor(name, list(shape), dt, kind="ExternalInput").ap()

    def dscr(self, name, shape, dt=F32):
        kind = "ExternalOutput" if name in self.debug_outs else "Internal"
        return self.nc.dram_tensor(name, list(shape), dt, kind=kind).ap()

    def sb(self, es, name, shape, dt=F32):
        self.uid += 1
        return es.enter_context(self.nc.sbuf_tensor("%s_%d" % (name, self.uid), list(shape), dt))

    def ps(self, es, name, shape, dt=F32):
        self.uid += 1
        return es.enter_context(self.nc.psum_tensor("%s_%d" % (name, self.uid), list(shape), dt))

    SCRATCH = ("XR", "TS", "U2", "ZS", "CV", "MOD", "YG", "OUTF")

    def dma(self, q, out, in_, r, w):
        w2 = []
        for k in w:
            if k in self.SCRATCH:
                self.uid += 1
                k = "%s#%d" % (k, self.uid)
            w2.append(k)
        return self.f.op(q, lambda e: e.dma_start(out=out, in_=in_), r, w2, dma=True)

    def tt(self, eng, out, in0, in1, op, r, w):
        return self.f.op(eng, lambda e: e.tensor_tensor(out=out, in0=in0, in1=in1, op=op), r, w)

    def ts(self, eng, out, in0, s1, s2, op0, op1, r, w):
        if s2 is None:
            return self.f.op(eng, lambda e: e.tensor_scalar(out=out, in0=in0, scalar1=s1, scalar2=None, op0=op0), r, w)
        return self.f.op(eng, lambda e: e.tensor_scalar(out=out, in0=in0, scalar1=s1, scalar2=s2, op0=op0, op1=op1), r, w)

    def stt(self, out, in0, scalar, in1, op0, op1, r, w):
        return self.f.op("dve", lambda e: e.scalar_tensor_tensor(out=out, in0=in0, scalar=scalar, in1=in1, op0=op0, op1=op1), r, w)

    def cp(self, eng, out, in_, r, w):
        if eng == "act":
            return self.f.op(eng, lambda e: e.copy(out=out, in_=in_), r, w)
        return self.f.op(eng, lambda e: e.tensor_copy(out=out, in_=in_), r, w)

    def act(self, out, in_, func, r, w, scale=1.0, bias=None):
        if bias is None:
            return self.f.op("act", lambda e: e.activation(out=out, in_=in_, func=func, scale=scale), r, w)
        return self.f.op("act", lambda e: e.activation(out=out, in_=in_, func=func, scale=scale, bias=bias), r, w)

    def mm(self, out, lhsT, rhs, start, stop, r, w):
        return self.f.op("pe", lambda e: e.matmul(out=out, lhsT=lhsT, rhs=rhs, start=start, stop=stop), r, w)

    def tr(self, out, in_, ident, r, w):
        return self.f.op("pe", lambda e: e.transpose(out=out, in_=in_, identity=ident), r, w)

    def memset(self, eng, ap, val, r, w):
        return self.f.op(eng, lambda e: e.memset(ap, val), r, w)


def s5_paramgen_gen(kb, es, j, io, SW, A8T, A8D):
    cfg, f = kb.cfg, kb.f
    GL = 8
    if True:
        sb = lambda n, s, dt=F32: kb.sb(es, n, s, dt)
        kt = sb("kt", [128, 2, 4, 2, 8])
        msk = sb("msk", [128, 2, 128])
        idf = sb("idf", [128, 128])
        kb.dma("sp", kt[:], io["s5_kt"], [], ["kt"])
        kb.dma("sp", msk[:], io["s5_mask"], [], ["msk"])
        kb.dma("sp", idf[:], io["ident"], [], ["idf_pg"])
        LM = sb("LM", [128, 3, GL]); Bt = sb("Bt", [128, 2, GL, 16]); Ct = sb("Ct", [128, 2, GL, 16])
        dtv = sb("dtv", [128, GL]); XRI = sb("XRI", [128, 2, GL])
        marg = sb("marg", [128, GL, 8]); mag = sb("mag", [128, GL, 8]); y = sb("y", [128, GL, 8])
        yi = sb("yi", [128, GL, 8], I32); yf = sb("yf", [128, GL, 8]); m1 = sb("m1", [128, GL, 8])
        sinv = sb("sinv", [128, GL, 8]); cosv = sb("cosv", [128, GL, 8])
        P = sb("P", [128, 4, 2, GL, 8])
        sm = sb("sm", [128, 8, GL])
        Bb = sb("Bb", [128, 2, GL, 16]); tb = sb("tb", [128, GL, 16])
        WP = sb("WP", [128, 2, GL, 128]); WE = sb("WE", [128, 2, GL, 128]); CM = sb("CM", [128, 2, GL, 128])
        tw = sb("tw", [128, GL, 128])
        OUT = sb("OUT", [128, GL, 6, 128], BF16)
        pT = [kb.ps(es, "pT%d" % i, [128, 512]) for i in range(2)]
        pM = [kb.ps(es, "pM%d" % i, [128, 512]) for i in range(2)]
        V = "dve"

        for d in range(2):
            for blk in range(cfg.NB):
                kb.dma("sp", LM[:], io["s5_lam"][j, d, blk], [], ["LM"])
                kb.dma("sp", Bt[:], io["s5_B"][j, d, blk], [], ["Bt"])
                kb.dma("sp", Ct[:], io["s5_C"][j, d, blk], [], ["Ct"])
                kb.act(dtv[:], LM[:, 2, :], AF.Exp, ["LM"], ["dtv"])
                kb.tt(V, XRI[:, 0, :], LM[:, 0, :], dtv[:], ALU.mult, ["LM", "dtv"], ["XRI"])
                kb.tt(V, XRI[:, 1, :], LM[:, 1, :], dtv[:], ALU.mult, ["LM", "dtv"], ["XRI"])
                for ty in range(4):
                    ktab = fview(kt, ((d * 4 + ty) * 2 + 0) * 8, [(0, GL), (1, 8)])
                    ktab2 = fview(kt, ((d * 4 + ty) * 2 + 1) * 8, [(0, GL), (1, 8)])
                    xr_b = fview(XRI, 0, [(1, GL), (0, 8)])
                    xi_b = fview(XRI, GL, [(1, GL), (0, 8)])
                    kb.tt(V, marg[:], xr_b, ktab, ALU.mult, ["XRI", "kt"], ["marg"])
                    kb.act(mag[:], marg[:], AF.Exp, ["marg"], ["mag"])
                    kb.tt(V, y[:], xi_b, ktab2, ALU.mult, ["XRI", "kt"], ["y"])
                    kb.cp(V, yi[:], y[:], ["y"], ["yi"])
                    kb.cp(V, yf[:], yi[:], ["yi"], ["yf"])
                    kb.tt(V, y[:], y[:], yf[:], ALU.subtract, ["y", "yf"], ["y"])
                    kb.ts(V, m1[:], y[:], 0.5, None, ALU.is_gt, None, ["y"], ["m1"])
                    kb.tt(V, y[:], y[:], m1[:], ALU.subtract, ["y", "m1"], ["y"])
                    kb.ts(V, m1[:], y[:], -0.5, None, ALU.is_lt, None, ["y"], ["m1"])
                    kb.tt(V, y[:], y[:], m1[:], ALU.add, ["y", "m1"], ["y"])
                    kb.act(sinv[:], y[:], AF.Sin, ["y"], ["sinv"], scale=TWO_PI)
                    kb.ts(V, yf[:], y[:], 0.25, None, ALU.add, None, ["y"], ["yf"])
                    kb.ts(V, m1[:], yf[:], 0.5, None, ALU.is_gt, None, ["yf"], ["m1"])
                    kb.tt(V, yf[:], yf[:], m1[:], ALU.subtract, ["yf", "m1"], ["yf"])
                    kb.act(cosv[:], yf[:], AF.Sin, ["yf"], ["cosv"], scale=TWO_PI)
                    kb.tt(V, P[:, ty, 0], mag[:], cosv[:], ALU.mult, ["mag", "cosv"], ["P"])
                    kb.tt(V, P[:, ty, 1], mag[:], sinv[:], ALU.mult, ["mag", "sinv"], ["P"])
                a1re = P[:, 3, 0, :, 0]; a1im = P[:, 3, 1, :, 0]
                a8re = P[:, 3, 0, :, 1]; a8im = P[:, 3, 1, :, 1]
                lre = LM[:, 0, :]; lim = LM[:, 1, :]
                S = lambda i: sm[:, i, :]
                kb.ts(V, S(0), a1re, -1.0, None, ALU.add, None, ["P"], ["sm0"])
                kb.tt(V, S(1), lre, lre, ALU.mult, ["LM"], ["sm1"])
                kb.tt(V, S(2), lim, lim, ALU.mult, ["LM"], ["sm2"])
                kb.tt(V, S(1), S(1), S(2), ALU.add, ["sm1", "sm2"], ["sm1"])
                kb.f.op(V, lambda e, o=S(1), i=S(1): e.reciprocal(out=o, in_=i), ["sm1"], ["sm1"])
                kb.tt(V, S(2), S(0), lre, ALU.mult, ["sm0", "LM"], ["sm2"])
                kb.tt(V, S(3), a1im, lim, ALU.mult, ["P", "LM"], ["sm3"])
                kb.tt(V, S(2), S(2), S(3), ALU.add, ["sm2", "sm3"], ["sm2"])
                kb.tt(V, S(4), S(2), S(1), ALU.mult, ["sm2", "sm1"], ["sm4"])
                kb.tt(V, S(2), a1im, lre, ALU.mult, ["P", "LM"], ["sm2"])
                kb.tt(V, S(3), S(0), lim, ALU.mult, ["sm0", "LM"], ["sm3"])
                kb.tt(V, S(2), S(2), S(3), ALU.subtract, ["sm2", "sm3"], ["sm2"])
                kb.tt(V, S(5), S(2), S(1), ALU.mult, ["sm2", "sm1"], ["sm5"])
                cre_b = fview(sm, 4 * GL, [(1, GL), (0, 16)]); cim_b = fview(sm, 5 * GL, [(1, GL), (0, 16)])
                kb.tt(V, Bb[:, 0], Bt[:, 0], cre_b, ALU.mult, ["Bt", "sm4"], ["Bb"])
                kb.tt(V, tb[:], Bt[:, 1], cim_b, ALU.mult, ["Bt", "sm5"], ["tb"])
                kb.tt(V, Bb[:, 0], Bb[:, 0], tb[:], ALU.subtract, ["Bb", "tb"], ["Bb"])
                kb.tt(V, Bb[:, 1], Bt[:, 1], cre_b, ALU.mult, ["Bt", "sm4"], ["Bb"])
                kb.tt(V, tb[:], Bt[:, 0], cim_b, ALU.mult, ["Bt", "sm5"], ["tb"])
                kb.tt(V, Bb[:, 1], Bb[:, 1], tb[:], ALU.add, ["Bb", "tb"], ["Bb"])
                kb.cp(V, S(6), a8re, ["P"], ["sm6"])
                kb.cp(V, S(7), a8im, ["P"], ["sm7"])
                for lvl in range(5):
                    base = (((blk * 2 + d) * 5 + lvl) * 2) * 2 * GL
                    a8t = lambda which, ri: fview(A8T, base + (which * 2 + ri) * GL, [(1, GL)])
                    kb.cp(V, a8t(0, 0), S(6), ["sm6"], ["A8T"])
                    kb.cp(V, a8t(0, 1), S(6), ["sm6"], ["A8T"])
                    kb.ts(V, a8t(1, 0), S(7), -1.0, None, ALU.mult, None, ["sm7"], ["A8T"])
                    kb.cp(V, a8t(1, 1), S(7), ["sm7"], ["A8T"])
                    if lvl < 4:
                        kb.tt(V, S(2), S(6), S(6), ALU.mult, ["sm6"], ["sm2"])
                        kb.tt(V, S(3), S(7), S(7), ALU.mult, ["sm7"], ["sm3"])
                        kb.stt(S(7), S(6), 2.0, S(7), ALU.mult, ALU.mult, ["sm6", "sm7"], ["sm7"])
                        kb.tt(V, S(6), S(2), S(3), ALU.subtract, ["sm2", "sm3"], ["sm6"])

                def cprod(dst, ty, src, key_src, key_dst):
                    pre = fview(P, ((ty * 2 + 0) * GL) * 8, [(8, GL), (1, 8), (0, 16)])
                    pim = fview(P, ((ty * 2 + 1) * GL) * 8, [(8, GL), (1, 8), (0, 16)])
                    sre = fview(src, 0, [(16, GL), (0, 8), (1, 16)])
                    sim = fview(src, GL * 16, [(16, GL), (0, 8), (1, 16)])
                    dre = fview(dst, 0, [(128, GL), (16, 8), (1, 16)])
                    dim_ = fview(dst, GL * 128, [(128, GL), (16, 8), (1, 16)])
                    twv = fview(tw, 0, [(128, GL), (16, 8), (1, 16)])
                    kb.tt(V, dre, pre, sre, ALU.mult, ["P", key_src], [key_dst])
                    kb.tt(V, twv, pim, sim, ALU.mult, ["P", key_src], ["tw"])
                    kb.tt(V, dre, dre, twv, ALU.subtract, [key_dst, "tw"], [key_dst])
                    kb.tt(V, dim_, pre, sim, ALU.mult, ["P", key_src], [key_dst])
                    kb.tt(V, twv, pim, sre, ALU.mult, ["P", key_src], ["tw"])
                    kb.tt(V, dim_, dim_, twv, ALU.add, [key_dst, "tw"], [key_dst])

                cprod(CM, 0, Ct, "Ct", "CM")
                cprod(WP, 1, Bb, "Bb", "WP")
                cprod(WE, 2, Bb, "Bb", "WE")
                kb.ts(V, CM[:, 1], CM[:, 1], -1.0, None, ALU.mult, None, ["CM"], ["CM"])
                kb.cp("pool", OUT[:, :, 2, :], CM[:, 0], ["CM"], ["OUT"])
                kb.cp("pool", OUT[:, :, 3, :], CM[:, 1], ["CM"], ["OUT"])
                for ri in range(2):
                    for q in range(2):
                        pt = pT[(ri * 2 + q) % 2]
                        ptk = "pT%d" % ((ri * 2 + q) % 2)
                        for i4 in range(4):
                            gl = q * 4 + i4
                            kb.tr(pt[:, i4 * 128:(i4 + 1) * 128], WE[:, ri, gl, :], idf[:], ["WE", "idf_pg"], [ptk])
                        kb.cp("act", OUT[:, q * 4:(q + 1) * 4, ri, :], pt[:, :].rearrange("p (g c) -> p g c", g=4), [ptk], ["OUT"])
                for gh in range(2):
                    for q in range(2):
                        pm = pM[(gh * 2 + q) % 2]
                        pmk = "pM%d" % ((gh * 2 + q) % 2)
                        for i4 in range(4):
                            gl = q * 4 + i4
                            lo, hi = gh * 64, (gh + 1) * 64
                            kb.mm(pm[:, i4 * 128:(i4 + 1) * 128], WP[lo:hi, 0, gl, :], CM[lo:hi, 0, gl, :], True, False, ["WP", "CM"], [pmk])
                            kb.mm(pm[:, i4 * 128:(i4 + 1) * 128], WP[lo:hi, 1, gl, :], CM[lo:hi, 1, gl, :], False, True, ["WP", "CM"], [pmk])
                        mk = fview(msk, d * 128, [(0, 4), (1, 128)])
                        kb.tt(V, OUT[:, q * 4:(q + 1) * 4, 4 + gh, :], pm[:, :].rearrange("p (g c) -> p g c", g=4), mk, ALU.mult, [pmk, "msk"], ["OUT"])
                kb.dma("pool", SW[j, d, blk], OUT[:], ["OUT"], ["SW%d_%d_%d" % (j, d, blk)])
                yield
        kb.dma("pool", A8D[j], A8T[:].rearrange("p a b c d e g -> p (a b c d e g)"), ["A8T"], ["A8D%d" % j])
        yield


def bcast_rows(ap2d, n):
    return ap2d.broadcast_to([n, ap2d.shape[1]])


def s5_layer(kb, i, j, io, scr, ctx_out):
    cfg, f = kb.cfg, kb.f
    GL = 8
    NC, NCL, NCC, L = cfg.NC, cfg.NCL, cfg.NCC, cfg.L
    XR, MOD, ZS, SW = scr["XR"], scr["MOD"], scr["ZS"], scr["SW"]
    with ExitStack() as es:
        sb = lambda n, s, dt=F32: kb.sb(es, n, s, dt)
        A8T = sb("A8T", [128, cfg.NB, 2, 5, 2, 2, GL])
        kb.dma("sp", A8T[:].rearrange("p a b c d e g -> p (a b c d e g)"), scr["A8D"][j], [], ["A8T"])
        idb = sb("idb", [128, 128], BF16)
        idf = sb("idf2", [128, 128])
        kb.dma("sp", idf[:], io["ident"], [], ["idf"])
        kb.cp("dve", idb[:], idf[:], ["idf"], ["idb"])
        ctiles = [(t * 1024, 128, t * 128, 0) for t in range(L // 1024)] + [(L, NCC, NCL, 1)]
        nct = len(ctiles)
        W8 = sb("W8", [128, 2, GL, 6, 128], BF16)
        mod4 = sb("mod4", [128, 4, 256]); dsk = sb("dsk", [128, 256])
        U32 = [sb("U32_%d" % c, [128, 8, 256]) for c in range(nct)]
        U8g = [sb("U8g%d" % c, [128, 16, 128], BF16) for c in range(2)]
        X8 = sb("X8", [128, 16, NC], BF16)
        Z = sb("Z", [128, 2, 2, GL, NC])
        NMAX_ = max(NCL, NCC) // 2
        tq1 = [sb("tq1_%d" % d_, [128, 2, GL, NMAX_]) for d_ in range(2)]
        tq2 = [sb("tq2_%d" % d_, [128, 2, GL, NMAX_]) for d_ in range(2)]
        Sbf = sb("Sbf", [128, 2, 2, GL, NC], BF16)
        Y8s = sb("Y8s", [128, 16, NC], BF16)
        ytm = [sb("ytm%d" % c, [128, 8, 256]) for c in range(2)]
        zt = [sb("zt%d" % c, [128, 8, 256], BF16) for c in range(2)]
        pX = [kb.ps(es, "pX%d" % c, [128, 512], BF16) for c in range(2)]
        pZ = [kb.ps(es, "pZ%d" % c, [128, 512]) for c in range(4)]
        pY = [kb.ps(es, "pY%d" % c, [128, 512]) for c in range(2)]
        DS, RS = 2 * GL * NC, GL * NC

        def cf(k):
            return NCL + k if k < NCC else k - NCC

        def cb(k):
            return NCL + (NCC - 1 - k) if k < NCC else NCL - 1 - (k - NCC)

        rot = 0
        for blk in range(cfg.NB):
            c0 = blk * 256
            for q, row in enumerate((1, 0, 7, 6)):
                kb.dma("sp", mod4[:, q, :], bcast_rows(MOD[i, row:row + 1, c0:c0 + 256], 128), [], ["mod4"])
            kb.dma("sp", dsk[:], bcast_rows(io["s5_d"][j, 0:1, c0:c0 + 256], 128), [], ["dsk"])
            for d in range(2):
                kb.dma("sp", W8[:, d], SW[j, d, blk], [], ["W8"])
            for ci, (row0, n, col0, isctx) in enumerate(ctiles):
                uk = "U32_%d" % ci
                kb.dma("sp", U32[ci][0:n], XR[row0:row0 + n * 8, c0:c0 + 256].rearrange("(c s) j -> c s j", s=8), [], [uk])
                scb = fview(mod4, (2 * isctx) * 256, [(0, 8), (1, 256)], pn=n)
                shb = fview(mod4, (2 * isctx + 1) * 256, [(0, 8), (1, 256)], pn=n)
                kb.tt("dve", U32[ci][0:n], U32[ci][0:n], scb, ALU.mult, [uk, "mod4"], [uk])
                kb.tt("dve", U32[ci][0:n], U32[ci][0:n], shb, ALU.add, [uk, "mod4"], [uk])
                ug = U8g[ci % 2]; ugk = "U8g%d" % (ci % 2)
                kb.cp("act", fview(ug, 0, [(16, 8), (128, 16), (1, 16)], pn=n),
                      fview(U32[ci], 0, [(256, 8), (16, 16), (1, 16)], pn=n), [uk], [ugk])
                for g4 in range(4):
                    px = pX[rot % 2]; pxk = "pX%d" % (rot % 2); rot += 1
                    for q in range(4):
                        g = g4 * 4 + q
                        kb.tr(px[:, q * 128:q * 128 + n], ug[0:n, g, :], idb[0:n, 0:n], [ugk, "idb"], [pxk])
                    kb.cp("dve" if g4 % 2 else "act", X8[:, g4 * 4:(g4 + 1) * 4, col0:col0 + n],
                          px[:, :].rearrange("p (g c) -> p g c", g=4)[:, :, 0:n], [pxk], ["X8"])
            zr = 0
            for d in range(2):
                for gl in range(GL):
                    pr = pZ[zr % 4]; prk = "pZ%d" % (zr % 4); zr += 1
                    pi = pZ[zr % 4]; pik = "pZ%d" % (zr % 4); zr += 1
                    for gh in range(2):
                        g = gh * 8 + gl
                        lo, hi = gh * 64, (gh + 1) * 64
                        kb.mm(pr[lo:hi, 0:NC], W8[:, d, gl, 0, lo:hi], X8[:, g, :], True, True, ["W8", "X8"], [prk])
                        kb.mm(pi[lo:hi, 0:NC], W8[:, d, gl, 1, lo:hi], X8[:, g, :], True, True, ["W8", "X8"], [pik])
                    kb.cp("act", Z[:, d, 0, gl, :], pr[:, 0:NC], [prk], ["Z%d" % d])
                    kb.cp("act" if d else "dve", Z[:, d, 1, gl, :], pi[:, 0:NC], [pik], ["Z%d" % d])
            NMAX = max(NCL, NCC) // 2
            ML = 4
            for d, RE in ((0, "dve"), (1, "pool")):
                zk, k1, k2 = "Z%d" % d, "t1_%d" % d, "t2_%d" % d
                segs = [(NCL, 1, NCC), (0, 1, NCL)] if d == 0 else [(NC - 1, -1, NCC), (NCL - 1, -1, NCL)]

                def cacc(dc0, dst_, sc0, sst, n, lvl, d=d, RE=RE, zk=zk, k1=k1, k2=k2):
                    if n <= 0:
                        return
                    ab = (((blk * 2 + d) * 5 + lvl) * 2) * 2 * GL
                    dst = fview(Z, d * DS + dc0, [(RS, 2), (NC, GL), (dst_, n)])
                    src = fview(Z, d * DS + sc0, [(RS, 2), (NC, GL), (sst, n)])
                    srw = fview(Z, d * DS + sc0 + RS, [(-RS, 2), (NC, GL), (sst, n)])
                    AR_ = fview(A8T, ab, [(GL, 2), (1, GL), (0, n)])
                    AI_ = fview(A8T, ab + 2 * GL, [(GL, 2), (1, GL), (0, n)])
                    T1 = fview(tq1[d], 0, [(GL * NMAX, 2), (NMAX, GL), (1, n)])
                    T2 = fview(tq2[d], 0, [(GL * NMAX, 2), (NMAX, GL), (1, n)])
                    kb.tt(RE, T1, src, AR_, ALU.mult, [zk, "A8T"], [k1])
                    kb.tt(RE, T2, srw, AI_, ALU.mult, [zk, "A8T"], [k2])
                    kb.tt(RE, T1, T1, T2, ALU.add, [k1, k2], [k1])
                    kb.tt(RE, dst, dst, T1, ALU.add, [zk, k1], [zk])

                def col(p, segs=segs):
                    (c0a, sga, la), (c0b, sgb, lb) = segs
                    return c0a + sga * p if p < la else c0b + sgb * (p - la)

                for lvl in range(ML):
                    h = 1 << lvl
                    for (sc0_, sg_, ln_) in segs:
                        cacc(sc0_ + sg_ * (2 * h - 1), sg_ * 2 * h, sc0_ + sg_ * (h - 1), sg_ * 2 * h, ln_ // (2 * h), lvl)
                BS = 1 << ML
                for q_ in range(1, NC // BS):
                    cacc(col(q_ * BS + BS - 1), 1, col(q_ * BS - 1), 1, 1, ML)
                for lvl in range(ML - 1, -1, -1):
                    h = 1 << lvl
                    (c0a, sga, la), (c0b, sgb, lb) = segs
                    cacc(c0a + sga * (2 * h + h - 1), sga * 2 * h, c0a + sga * (2 * h - 1), sga * 2 * h, la // (2 * h) - 1, lvl)
                    cacc(c0b + sgb * (h - 1), 1, c0a + sga * (la - 1), 1, 1, lvl)
                    cacc(c0b + sgb * (2 * h + h - 1), sgb * 2 * h, c0b + sgb * (2 * h - 1), sgb * 2 * h, lb // (2 * h) - 1, lvl)
            kb.cp("act", Sbf[:], Z[:], ["Z0", "Z1"], ["Sbf"])
            fw_p = [(1, NCL, 0), (0, 1, NC - 1), (NCL + 1, NC, NCL)]
            bw_p = [(NCL, NC - 1, NCL + 1), (NCL - 1, NCL, NCL), (0, NCL - 1, 1)]
            for gh in range(2):
                lo, hi = gh * 64, (gh + 1) * 64
                for gl in range(GL):
                    g = gh * 8 + gl
                    py = pY[g % 2]; pyk = "pY%d" % (g % 2)
                    kb.mm(py[:, 0:NC], W8[:, 0, gl, 4 + gh, :], X8[:, g, :], True, False, ["W8", "X8"], [pyk])
                    kb.mm(py[:, 0:NC], W8[:, 1, gl, 4 + gh, :], X8[:, g, :], False, False, ["W8", "X8"], [pyk])
                    pieces = [(0, p) for p in fw_p] + [(1, p) for p in bw_p]
                    pieces = [(d, p) for d, p in pieces if p[1] > p[0]]
                    for pi_, (d, (o0, o1, s0)) in enumerate(pieces):
                        for ri in range(2):
                            last = (pi_ == len(pieces) - 1) and ri == 1
                            kb.mm(py[:, o0:o1], W8[lo:hi, d, gl, 2 + ri, :], Sbf[lo:hi, d, ri, gl, s0:s0 + (o1 - o0)],
                                  False, last, ["W8", "Sbf"], [pyk])
                    kb.cp("act" if g % 2 else "dve", Y8s[:, g, :], py[:, 0:NC], [pyk], ["Y8s"])
            for ci, (row0, n, col0, isctx) in enumerate(ctiles):
                if isctx and not ctx_out:
                    continue
                uk = "U32_%d" % ci
                yt = ytm[ci % 2]; ytk = "ytm%d" % (ci % 2)
                for g4 in range(4):
                    px = pX[rot % 2]; pxk = "pX%d" % (rot % 2); rot += 1
                    for q in range(4):
                        g = g4 * 4 + q
                        kb.tr(px[0:n, q * 128:(q + 1) * 128], Y8s[:, g, col0:col0 + n], idb[:], ["Y8s", "idb"], [pxk])
                    kb.cp("act" if g4 % 2 else "dve", fview(yt, g4 * 64, [(16, 4), (256, 8), (1, 16)], pn=n),
                          fview(px, 0, [(128, 4), (16, 8), (1, 16)], pn=n), [pxk], [ytk])
                dkb = fview(dsk, 0, [(0, 8), (1, 256)], pn=n)
                kb.tt("dve", U32[ci][0:n], U32[ci][0:n], dkb, ALU.mult, [uk, "dsk"], [uk])
                kb.tt("dve", yt[0:n], yt[0:n], U32[ci][0:n], ALU.add, [ytk, uk], [ytk])
                z_ = zt[ci % 2]; zk = "zt%d" % (ci % 2)
                kb.act(z_[0:n], yt[0:n], AF.Gelu_apprx_tanh, [ytk], [zk])
                kb.dma("pool", ZS[row0:row0 + n * 8, c0:c0 + 256].rearrange("(c s) j -> c s j", s=8), z_[0:n], [zk], ["ZS"])
    f.barrier()


def adaln_gen(kb, es, io, scr):
    cfg, f = kb.cfg, kb.f
    D, NK, NT = cfg.D, cfg.NK, cfg.NT
    MOD = scr["MOD"]
    if True:
        sb = lambda n, s, dt=F32: kb.sb(es, n, s, dt)
        cond = sb("cond", [128, NK, 2]); cs = sb("cs", [128, NK, 2])
        kb.dma("sp", cond[:], io["cond"], [], ["cond"])
        kb.act(cs[:], cond[:], AF.Silu, ["cond"], ["cs"])
        aw = [sb("aw%d" % q, [128, NK, NT]) for q in range(2)]
        ab = [sb("ab%d" % q, [2, NT]) for q in range(2)]
        res = [sb("res%d" % q, [2, NT]) for q in range(2)]
        pa = [kb.ps(es, "pa%d" % q, [128, 512]) for q in range(2)]
        it = 0
        for i in range(cfg.DEPTH):
            for n in range(6 * D // NT):
                q = it % 2; it += 1
                kb.dma("sp", aw[q][:], io["ada_w"][i, :, n * NT:(n + 1) * NT].rearrange("(k p) c -> p k c", p=128), [], ["aw%d" % q])
                kb.dma("sp", ab[q][:], bcast_rows(io["ada_b"][i, 0:1, n * NT:(n + 1) * NT], 2), [], ["ab%d" % q])
                for k in range(NK):
                    kb.mm(pa[q][0:2, 0:NT], cs[:, k, :], aw[q][:, k, :], k == 0, k == NK - 1, ["cs", "aw%d" % q], ["pa%d" % q])
                v = (n * NT) // D
                off = n * NT - v * D
                kb.f.op("pool", lambda e, o=res[q][:], a_=ab[q][:]: e.tensor_copy(out=o, in_=a_), ["ab%d" % q], ["res%d" % q]) if False else None
                kb.stt(res[q][:], pa[q][0:2, 0:NT], 1.0 if v in (1, 4) else 0.0, ab[q][:], ALU.add, ALU.add,
                       ["pa%d" % q, "ab%d" % q], ["res%d" % q])
                kb.dma("pool", MOD[i].rearrange("(w v) d -> w v d", w=2)[:, v, off:off + NT], res[q][:], ["res%d" % q], ["MOD"])
                yield


def prologue(kb, io, scr):
    cfg, f = kb.cfg, kb.f
    with ExitStack() as es_a:
        A8T = kb.sb(es_a, "A8Tg", [128, cfg.NB, 2, 5, 2, 2, 8])
        ga = adaln_gen(kb, es_a, io, scr)
        a_done = False
        try:
            next(ga)
        except StopIteration:
            a_done = True
        n_a = cfg.DEPTH * 6 * cfg.D // cfg.NT
        n_p = cfg.NS5 * (2 * cfg.NB + 1)
        per = max(1, -(-n_a // max(1, n_p)))
        for j in range(cfg.NS5):
            with ExitStack() as es_j:
                for _ in s5_paramgen_gen(kb, es_j, j, io, scr["SW"], A8T, scr["A8D"]):
                    for _k in range(per):
                        if a_done:
                            break
                        try:
                            next(ga)
                        except StopIteration:
                            a_done = True
        if not a_done:
            for _ in ga:
                pass
    f.barrier()


def ln_tile(kb, eng2, t, tk, mv, st, sd, g_bc, b_bc, n=128):
    cfg = kb.cfg
    D = cfg.D
    nch = max(1, D // 512)
    w = D // nch
    for c in range(nch):
        kb.f.op("dve", lambda e, o=st[0:n, c, :], i_=t[0:n, c * w:(c + 1) * w]: e.bn_stats(out=o, in_=i_), [tk], [tk + "_st"])
    kb.f.op("dve", lambda e, o=mv[0:n, :], i_=st[0:n, :, :].rearrange("p c s -> p (c s)"): e.bn_aggr(out=o, in_=i_), [tk + "_st"], [tk + "_mv"])
    kb.ts("dve", sd[0:n, 0:1], mv[0:n, 1:2], LN_EPS, None, ALU.add, None, [tk + "_mv"], [tk + "_sd"])
    kb.act(sd[0:n, 0:1], sd[0:n, 0:1], AF.Sqrt, [tk + "_sd"], [tk + "_sd"])
    kb.f.op("dve", lambda e, o=sd[0:n, 1:2], i_=sd[0:n, 0:1]: e.reciprocal(out=o, in_=i_), [tk + "_sd"], [tk + "_sd"])
    kb.stt(sd[0:n, 2:3], mv[0:n, 0:1], -1.0, sd[0:n, 1:2], ALU.mult, ALU.mult, [tk + "_mv", tk + "_sd"], [tk + "_sd"])
    kb.act(t[0:n, :], t[0:n, :], AF.Identity, [tk, tk + "_sd"], [tk], scale=sd[0:n, 1:2], bias=sd[0:n, 2:3])
    kb.tt("dve", t[0:n, :], t[0:n, :], g_bc[0:n, :], ALU.mult, [tk, "lng"], [tk])
    kb.tt(eng2, t[0:n, :], t[0:n, :], b_bc[0:n, :], ALU.add, [tk, "lnb"], [tk])


def postnorm(kb, i, which, io, scr, with_ctx, mod_rows, dst_u, final_out=None):
    cfg, f = kb.cfg, kb.f
    D = cfg.D
    XR, MOD, TS = scr["XR"], scr["MOD"], scr["TS"]
    ntiles = cfg.NTT if with_ctx else cfg.NLT
    with ExitStack() as es:
        sb = lambda n, s, dt=F32: kb.sb(es, n, s, dt)
        lng = sb("lng", [128, D]); lnb = sb("lnb", [128, D])
        kb.dma("sp", lng[:], bcast_rows(io["ln_g"][2 * i + which:2 * i + which + 1, :], 128), [], ["lng"])
        kb.dma("sp", lnb[:], bcast_rows(io["ln_b"][2 * i + which:2 * i + which + 1, :], 128), [], ["lnb"])
        mods = None
        if mod_rows is not None:
            mods = [[sb("m%d_%d" % (a, b_), [128, D]) for b_ in range(2)] for a in range(2 if with_ctx else 1)]
            for a in range(len(mods)):
                for b_ in range(2):
                    kb.dma("sp", mods[a][b_][:], bcast_rows(MOD[i, 6 * a + mod_rows[b_]:6 * a + mod_rows[b_] + 1, :], 128), [], ["mods"])
        tb = [sb("tb%d" % q, [128, D]) for q in range(3)]
        ub = [sb("ub%d" % q, [128, D], BF16) for q in range(3)]
        st = [sb("st%d" % q, [128, max(1, D // 512), 6]) for q in range(3)]
        mv = [sb("mv%d" % q, [128, 2]) for q in range(3)]
        sd = [sb("sd%d" % q, [128, 4]) for q in range(3)]
        for tt in range(ntiles):
            q = tt % 3
            tk = "tb%d" % q
            rows = slice(tt * 128, (tt + 1) * 128)
            isctx = 1 if tt >= cfg.NLT else 0
            kb.dma("sp", tb[q][:], TS[rows, :], ["TS"], [tk])
            ln_tile(kb, "pool", tb[q], tk, mv[q], st[q], sd[q], lng, lnb)
            if final_out is not None:
                kb.out_stores.append(kb.dma("pool", final_out[rows, :], tb[q][:], [tk], ["OUTF"]))
            else:
                kb.dma("pool", XR[rows, :], tb[q][:], [tk], ["XR"])
            if mods is not None:
                m = mods[isctx]
                kb.tt("dve", tb[q][:], tb[q][:], m[0][:], ALU.mult, [tk, "mods"], [tk])
                kb.tt("dve", ub[q][:], tb[q][:], m[1][:], ALU.add, [tk, "mods"], ["ub%d" % q])
                kb.dma("pool", dst_u[rows, :], ub[q][:], ["ub%d" % q], ["U2"])
    f.barrier()


def glu_stage(kb, i, j, io, scr, with_ctx):
    cfg, f = kb.cfg, kb.f
    D, NK, NT = cfg.D, cfg.NK, cfg.NT
    XR, MOD, TS, ZS = scr["XR"], scr["MOD"], scr["TS"], scr["ZS"]
    ntiles = cfg.NTT if with_ctx else cfg.NLT
    with ExitStack() as es:
        sb = lambda n, s, dt=F32: kb.sb(es, n, s, dt)
        idb = sb("idb", [128, 128], BF16); idf = sb("idf", [128, 128])
        kb.dma("sp", idf[:], io["ident"], [], ["idf"])
        kb.cp("dve", idb[:], idf[:], ["idf"], ["idb"])
        zT = sb("zT", [128, NK, ntiles * 128], BF16)
        zin = [sb("zin%d" % q, [128, D], BF16) for q in range(2)]
        pT = [kb.ps(es, "pT%d" % q, [128, 512], BF16) for q in range(2)]
        r = 0
        for tt in range(ntiles):
            q = tt % 2
            kb.dma("sp", zin[q][:], ZS[tt * 128:(tt + 1) * 128, :], ["ZS"], ["zin%d" % q])
            for k4 in range(0, NK, 4):
                nk = min(4, NK - k4)
                p = pT[r % 2]; pk = "pT%d" % (r % 2); r += 1
                for a in range(nk):
                    kb.tr(p[:, a * 128:(a + 1) * 128], zin[q][:, (k4 + a) * 128:(k4 + a + 1) * 128], idb[:], ["zin%d" % q, "idb"], [pk])
                kb.cp("act" if (r % 2) else "dve", zT[:, k4:k4 + nk, tt * 128:(tt + 1) * 128],
                      p[:, 0:nk * 128].rearrange("p (a c) -> p a c", a=nk), [pk], ["zT"])
        KH = max(1, NK // 2)
        wst = [sb("wst%d" % q, [128, KH, NT]) for q in range(2)]
        wb = [sb("wb%d" % q, [128, NK, NT], BF16) for q in range(2)]
        bvg = sb("bvg", [128, 2, NT]); g1t = sb("g1t", [128, 2, NT])
        xs = [sb("xs%d" % q, [128, NT]) for q in range(2)]
        sg = [sb("sg%d" % q, [128, NT]) for q in range(2)]
        vv = [sb("vv%d" % q, [128, NT]) for q in range(2)]
        pv = [kb.ps(es, "pv%d" % q, [128, 512]) for q in range(2)]
        pg = [kb.ps(es, "pg%d" % q, [128, 512]) for q in range(2)]
        ws = 0
        for np_ in range(D // NT):
            cs_ = slice(np_ * NT, (np_ + 1) * NT)
            for vg in range(2):
                col0 = vg * D + np_ * NT
                for kh in range(0, NK, KH):
                    w_ = wst[ws % 2]; wk = "wst%d" % (ws % 2); ws += 1
                    kb.dma("sp", w_[:], io["s5_w_glu"][j, kh * 128:(kh + KH) * 128, col0:col0 + NT].rearrange("(k p) c -> p k c", p=128), [], [wk])
                    kb.cp("act", wb[vg][:, kh:kh + KH, :], w_[:], [wk], ["wb%d" % vg])
                kb.dma("sp", bvg[:, vg, :], bcast_rows(io["s5_b_glu"][j, 0:1, col0:col0 + NT], 128), [], ["bvg"])
            for a in range(2 if with_ctx else 1):
                kb.dma("sp", g1t[:, a, :], bcast_rows(MOD[i, 6 * a + 2:6 * a + 3, cs_], 128), [], ["g1t"])
            for tt in range(ntiles):
                q = tt % 2
                isctx = 1 if tt >= cfg.NLT else 0
                rows = slice(tt * 128, (tt + 1) * 128)
                kb.dma("sp", xs[q][:], XR[rows, cs_], [], ["xs%d" % q])
                for k in range(NK):
                    kb.mm(pv[q][:, 0:NT], zT[:, k, rows], wb[0][:, k, :], k == 0, k == NK - 1, ["zT", "wb0"], ["pv%d" % q])
                for k in range(NK):
                    kb.mm(pg[q][:, 0:NT], zT[:, k, rows], wb[1][:, k, :], k == 0, k == NK - 1, ["zT", "wb1"], ["pg%d" % q])
                kb.tt("dve", sg[q][:], pg[q][:, 0:NT], bvg[:, 1, :], ALU.add, ["pg%d" % q, "bvg"], ["sg%d" % q])
                kb.act(sg[q][:], sg[q][:], AF.Sigmoid, ["sg%d" % q], ["sg%d" % q])
                kb.tt("dve", vv[q][:], pv[q][:, 0:NT], bvg[:, 0, :], ALU.add, ["pv%d" % q, "bvg"], ["vv%d" % q])
                kb.tt("pool", vv[q][:], vv[q][:], sg[q][:], ALU.mult, ["vv%d" % q, "sg%d" % q], ["vv%d" % q])
                kb.tt("pool", vv[q][:], vv[q][:], g1t[:, isctx, :], ALU.mult, ["vv%d" % q, "g1t"], ["vv%d" % q])
                kb.stt(vv[q][:], xs[q][:], cfg.ALPHA, vv[q][:], ALU.mult, ALU.add, ["xs%d" % q, "vv%d" % q], ["vv%d" % q])
                kb.dma("pool", TS[rows, cs_], vv[q][:], ["vv%d" % q], ["TS"])
    f.barrier()


def conv_layer(kb, i, j, io, scr, with_ctx):
    cfg, f = kb.cfg, kb.f
    D, NK, NT, L, LC, GW = cfg.D, cfg.NK, cfg.NT, cfg.L, cfg.LC, cfg.GW
    XR, MOD, TS, CV = scr["XR"], scr["MOD"], scr["TS"], scr["CV"]
    ntiles = cfg.NTT if with_ctx else cfg.NLT
    NLT = cfg.NLT
    NTOK = ntiles * 128
    with ExitStack() as esl:
        sbl = lambda n, s, dt=F32: kb.sb(esl, n, s, dt)
        idf = sbl("idf", [128, 128]); idb = sbl("idb", [128, 128], BF16)
        kb.dma("sp", idf[:], io["ident"], [], ["idf"])
        kb.cp("dve", idb[:], idf[:], ["idf"], ["idb"])
        uT = sbl("uT", [128, NK, NTOK], BF16)
        with ExitStack() as es:
            sb = lambda n, s, dt=F32: kb.sb(es, n, s, dt)
            mods = [[sb("m%d_%d" % (a, b_), [128, D]) for b_ in range(2)] for a in range(2 if with_ctx else 1)]
            for a in range(len(mods)):
                for b_, row in enumerate((1, 0)):
                    kb.dma("sp", mods[a][b_][:], bcast_rows(MOD[i, 6 * a + row:6 * a + row + 1, :], 128), [], ["mods"])
            xt = [sb("xt%d" % q, [128, D]) for q in range(2)]
            ub = [sb("ub%d" % q, [128, D], BF16) for q in range(2)]
            pT = [kb.ps(es, "pT%d" % q, [128, 512], BF16) for q in range(2)]
            r = 0
            for tt in range(ntiles):
                q = tt % 2
                isctx = 1 if tt >= NLT else 0
                kb.dma("sp", xt[q][:], XR[tt * 128:(tt + 1) * 128, :], [], ["xt%d" % q])
                kb.tt("dve", xt[q][:], xt[q][:], mods[isctx][0][:], ALU.mult, ["xt%d" % q, "mods"], ["xt%d" % q])
                kb.tt("pool", ub[q][:], xt[q][:], mods[isctx][1][:], ALU.add, ["xt%d" % q, "mods"], ["ub%d" % q])
                for k4 in range(0, NK, 4):
                    nk = min(4, NK - k4)
                    p = pT[r % 2]; pk = "pT%d" % (r % 2); r += 1
                    for a in range(nk):
                        kb.tr(p[:, a * 128:(a + 1) * 128], ub[q][:, (k4 + a) * 128:(k4 + a + 1) * 128], idb[:], ["ub%d" % q, "idb"], [pk])
                    kb.cp("act" if (r % 2) else "dve", uT[:, k4:k4 + nk, tt * 128:(tt + 1) * 128],
                          p[:, 0:nk * 128].rearrange("p (a c) -> p a c", a=nk), [pk], ["uT"])
        f.barrier()
        with ExitStack() as es:
            sb = lambda n, s, dt=F32: kb.sb(es, n, s, dt)
            rows = L // GW
            WL = (rows + 30) * GW
            hl = sb("hl", [128, WL], BF16)
            hc = sb("hc", [128, LC + 30], BF16)
            kb.memset("pool", hl[:], 0.0, [], ["hl"])
            kb.memset("pool", hc[:], 0.0, [], ["hc"])
            bp = sb("bp", [128, 2 * NK]); wdw = sb("wdw", [128, NK, 31]); bdw = sb("bdw", [128, NK])
            kb.dma("sp", bp[:], io["cv_b_pw1"][j], [], ["bp"])
            kb.dma("sp", wdw[:], io["cv_w_dw"][j], [], ["wdw"])
            kb.dma("sp", bdw[:], io["cv_b_dw"][j], [], ["bdw"])
            DG = [sb("DG%d" % q, [128, 31, 128], BF16) for q in range(2)]
            wst = [sb("wst%d" % q, [128, NK, 128]) for q in range(2)]
            wab = [[sb("wab%d_%d" % (q, a), [128, NK, 128], BF16) for a in range(2)] for q in range(2)]
            sig = [sb("sig%d" % q, [128, NT]) for q in range(2)]
            cvs = [sb("cvs%d" % q, [128, NT]) for q in range(2)]
            cvt = [sb("cvt%d" % q, [128, NT // 128, 128]) for q in range(2)]
            pa = [kb.ps(es, "pa%d" % q, [128, 512]) for q in range(2)]
            pg = [kb.ps(es, "pg%d" % q, [128, 512]) for q in range(2)]
            pc = [kb.ps(es, "pc%d" % q, [128, 512]) for q in range(2)]
            pt = [kb.ps(es, "pt%d" % q, [128, 512]) for q in range(2)]
            blocks = [(tb * NT, NT, 0, 15 * GW + tb * NT) for tb in range(L // NT)]
            if with_ctx:
                blocks += [(L + tb * NT, min(NT, LC - tb * NT), 1, 15 + tb * NT) for tb in range((LC + NT - 1) // NT)]
            ws = 0; it = 0
            for cc in range(NK):
                cq = cc % 2
                for a in range(2):
                    w_ = wst[ws % 2]; wk = "wst%d" % (ws % 2); ws += 1
                    c0 = a * D + cc * 128
                    kb.dma("sp", w_[:], io["cv_w_pw1"][j, :, c0:c0 + 128].rearrange("(k p) c -> p k c", p=128), [], [wk])
                    kb.cp("act", wab[cq][a][:], w_[:], [wk], ["wab%d_%d" % (cq, a)])
                for k in range(31):
                    kb.ts("dve" if k % 2 else "pool", DG[cq][:, k, :], idf[:], wdw[:, cc, k:k + 1], None, ALU.mult, None, ["idf", "wdw"], ["DG%d" % cq])
                for (t0, n, isctx, hoff) in blocks:
                    q = it % 2; it += 1
                    for k in range(NK):
                        kb.mm(pa[q][:, 0:n], wab[cq][0][:, k, :], uT[:, k, t0:t0 + n], k == 0, k == NK - 1, ["wab%d_0" % cq, "uT"], ["pa%d" % q])
                    for k in range(NK):
                        kb.mm(pg[q][:, 0:n], wab[cq][1][:, k, :], uT[:, k, t0:t0 + n], k == 0, k == NK - 1, ["wab%d_1" % cq, "uT"], ["pg%d" % q])
                    kb.act(sig[q][:, 0:n], pg[q][:, 0:n], AF.Sigmoid, ["pg%d" % q, "bp"], ["sig%d" % q], bias=bp[:, NK + cc:NK + cc + 1])
                    hbuf, hk = (hc, "hc") if isctx else (hl, "hl")
                    kb.stt(hbuf[:, hoff:hoff + n], pa[q][:, 0:n], bp[:, cc:cc + 1], sig[q][:, 0:n], ALU.add, ALU.mult, ["pa%d" % q, "bp", "sig%d" % q], [hk])
                for (t0, n, isctx, hoff) in blocks:
                    q = it % 2; it += 1
                    hbuf, hk = (hc, "hc") if isctx else (hl, "hl")
                    step = 1 if isctx else GW
                    base = (hoff - 15) if isctx else (hoff - 15 * GW)
                    for k in range(31):
                        kb.mm(pc[q][:, 0:n], DG[cq][:, k, :], hbuf[:, base + k * step:base + k * step + n], k == 0, k == 30, ["DG%d" % cq, hk], ["pc%d" % q])
                    kb.act(cvs[q][:, 0:n], pc[q][:, 0:n], AF.Identity, ["pc%d" % q, "bdw"], ["cvs%d" % q], bias=bdw[:, cc:cc + 1])
                    na = n // 128
                    for a in range(na):
                        kb.tr(pt[q][:, a * 128:(a + 1) * 128], cvs[q][:, a * 128:(a + 1) * 128], idf[:], ["cvs%d" % q, "idf"], ["pt%d" % q])
                    kb.cp("dve", cvt[q][:, 0:na, :], pt[q][:, 0:n].rearrange("p (a c) -> p a c", a=na), ["pt%d" % q], ["cvt%d" % q])
                    kb.dma("pool", CV[t0:t0 + n, cc * 128:(cc + 1) * 128].rearrange("(a p) c -> p a c", p=128), cvt[q][:, 0:na, :], ["cvt%d" % q], ["CV"])
        f.barrier()
    with ExitStack() as es:
        sb = lambda n, s, dt=F32: kb.sb(es, n, s, dt)
        idf = sb("idf", [128, 128]); idb = sb("idb", [128, 128], BF16)
        kb.dma("sp", idf[:], io["ident"], [], ["idf"])
        kb.cp("dve", idb[:], idf[:], ["idf"], ["idb"])
        lng = sb("lng", [128, D]); lnb = sb("lnb", [128, D]); b2 = sb("b2", [128, D])
        kb.dma("sp", lng[:], bcast_rows(io["cv_ln_g"][j:j + 1, :], 128), [], ["lng"])
        kb.dma("sp", lnb[:], bcast_rows(io["cv_ln_b"][j:j + 1, :], 128), [], ["lnb"])
        kb.dma("sp", b2[:], bcast_rows(io["cv_b_pw2"][j, 0:1, :], 128), [], ["b2"])
        g1t = [sb("g1t%d" % a, [128, D]) for a in range(2 if with_ctx else 1)]
        for a in range(len(g1t)):
            kb.dma("sp", g1t[a][:], bcast_rows(MOD[i, 6 * a + 2:6 * a + 3, :], 128), [], ["g1t"])
        W2 = sb("W2", [128, NK, D], BF16)
        wst = [sb("wst%d" % q, [128, D]) for q in range(2)]
        for k in range(NK):
            kb.dma("sp", wst[k % 2][:], io["cv_w_pw2"][j, k * 128:(k + 1) * 128, :], [], ["wst%d" % (k % 2)])
            kb.cp("act", W2[:, k, :], wst[k % 2][:], ["wst%d" % (k % 2)], ["W2"])
        cvt = [sb("cvt%d" % q, [128, D]) for q in range(3)]
        sbf = [sb("sbf%d" % q, [128, D], BF16) for q in range(3)]
        sT = [sb("sT%d" % q, [128, NK, 128], BF16) for q in range(3)]
        xt = [sb("xt%d" % q, [128, D]) for q in range(2)]
        yt = [sb("yt%d" % q, [128, D]) for q in range(2)]
        st = [sb("st%d" % q, [128, max(1, D // 512), 6]) for q in range(3)]
        mv = [sb("mv%d" % q, [128, 2]) for q in range(3)]
        sd = [sb("sd%d" % q, [128, 4]) for q in range(3)]
        pT = [kb.ps(es, "pT%d" % q, [128, 512], BF16) for q in range(2)]
        po = [kb.ps(es, "po%d" % q, [128, 512]) for q in range(2)]
        r = 0; pr = 0
        for tt in range(ntiles):
            q = tt % 3
            q2 = tt % 2
            isctx = 1 if tt >= NLT else 0
            rows_ = slice(tt * 128, (tt + 1) * 128)
            kb.dma("sp", cvt[q][:], CV[rows_, :], ["CV"], ["cvt%d" % q])
            kb.dma("sp", xt[q2][:], XR[rows_, :], [], ["xt%d" % q2])
            ln_tile(kb, "pool", cvt[q], "cvt%d" % q, mv[q], st[q], sd[q], lng, lnb)
            kb.act(sbf[q][:], cvt[q][:], AF.Silu, ["cvt%d" % q], ["sbf%d" % q])
            for k4 in range(0, NK, 4):
                nk = min(4, NK - k4)
                p = pT[r % 2]; pk = "pT%d" % (r % 2); r += 1
                for a in range(nk):
                    kb.tr(p[:, a * 128:(a + 1) * 128], sbf[q][:, (k4 + a) * 128:(k4 + a + 1) * 128], idb[:], ["sbf%d" % q, "idb"], [pk])
                kb.cp("act" if (r % 2) else "dve", sT[q][:, k4:k4 + nk, :], p[:, 0:nk * 128].rearrange("p (a c) -> p a c", a=nk), [pk], ["sT%d" % q])
            for nt in range(D // NT):
                cs_ = slice(nt * NT, (nt + 1) * NT)
                pq = pr % 2; pr += 1
                for k in range(NK):
                    kb.mm(po[pq][:, 0:NT], sT[q][:, k, :], W2[:, k, cs_], k == 0, k == NK - 1, ["sT%d" % q, "W2"], ["po%d" % pq])
                kb.tt("dve", yt[q2][:, cs_], po[pq][:, 0:NT], b2[:, cs_], ALU.add, ["po%d" % pq, "b2"], ["yt%d" % q2])
            kb.tt("pool", yt[q2][:], yt[q2][:], g1t[isctx][:], ALU.mult, ["yt%d" % q2, "g1t"], ["yt%d" % q2])
            kb.stt(yt[q2][:], xt[q2][:], cfg.ALPHA, yt[q2][:], ALU.mult, ALU.add, ["xt%d" % q2, "yt%d" % q2], ["yt%d" % q2])
            kb.dma("pool", TS[rows_, :], yt[q2][:], ["yt%d" % q2], ["TS"])
    f.barrier()
    postnorm(kb, i, 0, io, scr, with_ctx, (4, 3), scr["U2"])


def declare_io(kb):
    cfg = kb.cfg
    D, L, LC, DEPTH, NK, NB = cfg.D, cfg.L, cfg.LC, cfg.DEPTH, cfg.NK, cfg.NB
    NS5, NCV, E, F = cfg.NS5, cfg.NCV, cfg.E, cfg.F
    io = {}
    io["x"] = kb.din("x", [L, D]); io["ctx"] = kb.din("ctx", [LC, D])
    io["cond"] = kb.din("cond", [128, NK, 2])
    io["ada_w"] = kb.din("ada_w", [DEPTH, D, 6 * D]); io["ada_b"] = kb.din("ada_b", [DEPTH, 1, 6 * D])
    io["ln_g"] = kb.din("ln_g", [DEPTH * 2, D]); io["ln_b"] = kb.din("ln_b", [DEPTH * 2, D])
    io["ident"] = kb.din("ident", [128, 128])
    io["iota_f"] = kb.din("iota_f", [128, cfg.TALL]); io["tokid"] = kb.din("tokid", [128, 18])
    io["sel"] = kb.din("sel", [16, 16, 128])
    io["shift"] = kb.din("shift", [cfg.CAPC, 128 // cfg.CAPC, 128])
    io["s5_kt"] = kb.din("s5_kt", [128, 2, 4, 2, 8]); io["s5_mask"] = kb.din("s5_mask", [128, 2, 128])
    io["s5_lam"] = kb.din("s5_lam", [NS5, 2, NB, 128, 3, 8])
    io["s5_B"] = kb.din("s5_B", [NS5, 2, NB, 128, 2, 8, 16]); io["s5_C"] = kb.din("s5_C", [NS5, 2, NB, 128, 2, 8, 16])
    io["s5_d"] = kb.din("s5_d", [NS5, 1, D])
    io["s5_w_glu"] = kb.din("s5_w_glu", [NS5, D, 2 * D]); io["s5_b_glu"] = kb.din("s5_b_glu", [NS5, 1, 2 * D])
    nc_ = max(NCV, 1)
    io["cv_w_pw1"] = kb.din("cv_w_pw1", [nc_, D, 2 * D]); io["cv_b_pw1"] = kb.din("cv_b_pw1", [nc_, 128, 2 * NK])
    io["cv_w_dw"] = kb.din("cv_w_dw", [nc_, 128, NK, 31]); io["cv_b_dw"] = kb.din("cv_b_dw", [nc_, 128, NK])
    io["cv_ln_g"] = kb.din("cv_ln_g", [nc_, D]); io["cv_ln_b"] = kb.din("cv_ln_b", [nc_, D])
    io["cv_w_pw2"] = kb.din("cv_w_pw2", [nc_, D, D]); io["cv_b_pw2"] = kb.din("cv_b_pw2", [nc_, 1, D])
    io["moe_w_router"] = kb.din("moe_w_router", [DEPTH, D, E])
    io["moe_w_in"] = kb.din("moe_w_in", [DEPTH, E, D, 2 * F]); io["moe_w_out"] = kb.din("moe_w_out", [DEPTH, E, F, D])
    io["out"] = kb.nc.dram_tensor("out", [L, D], F32, kind="ExternalOutput").ap()
    return io


def declare_scratch(kb):
    cfg = kb.cfg
    D, TALL = cfg.D, cfg.TALL
    scr = {}
    scr["XR"] = kb.dscr("XR", [TALL, D])
    scr["TS"] = kb.dscr("TS", [TALL, D])
    scr["MOD"] = kb.dscr("MOD", [cfg.DEPTH, 12, D])
    scr["ZS"] = kb.dscr("ZS", [TALL, D], BF16)
    scr["U2"] = kb.dscr("U2", [TALL, D], BF16)
    scr["CV"] = kb.dscr("CV", [TALL, D])
    scr["SW"] = kb.dscr("SW", [cfg.NS5, 2, cfg.NB, 128, 8, 6, 128], BF16)
    scr["A8D"] = kb.dscr("A8D", [cfg.NS5, 128, cfg.NB * 2 * 5 * 2 * 2 * 8])
    scr["YG"] = kb.dscr("YG", [cfg.E, cfg.CAPL + cfg.CAPC, D], BF16)
    return scr


def build(cfg, debug_outs=(), stop_after=None):
    kb = KB(cfg, debug_outs)
    io = declare_io(kb)
    scr = declare_scratch(kb)
    f = kb.f
    kb.dma("sp", scr["XR"][0:cfg.L, :], io["x"], [], ["XR"])
    kb.dma("sp", scr["XR"][cfg.L:cfg.TALL, :], io["ctx"], [], ["XR"])
    prologue(kb, io, scr)
    done = False
    for i in range(cfg.DEPTH):
        is_s5 = (i % 2) == 0
        j = i // 2
        ctx_out = any((k % 2) == 0 for k in range(i + 1, cfg.DEPTH))
        last = i == cfg.DEPTH - 1
        if is_s5:
            s5_layer(kb, i, j, io, scr, ctx_out)
            if stop_after == ("s5", i):
                break
            glu_stage(kb, i, j, io, scr, ctx_out)
            if stop_after == ("glu", i):
                break
            postnorm(kb, i, 0, io, scr, ctx_out, (4, 3), scr["U2"])
        else:
            conv_layer(kb, i, j, io, scr, ctx_out)
        if stop_after == ("mix", i):
            break
        moe_layer(kb, i, io, scr, ctx_out)
        if stop_after == ("moe", i):
            break
        postnorm(kb, i, 1, io, scr, ctx_out, None, None, final_out=io["out"] if last else None)
    if not kb.out_stores:
        with ExitStack() as es:
            t = kb.sb(es, "dbg", [128, cfg.D])
            kb.dma("sp", t[:], scr["XR"][0:128, :], ["XR"], ["dbg"])
            kb.out_stores.append(kb.dma("pool", io["out"][0:128, :], t[:], ["dbg"], ["OUTF"]))
    f.barrier()
    f.emit(final_wait_ops=kb.out_stores)
    return kb.nc


def host_consts(cfg):
    c = {}
    c["ident"] = np.eye(128, dtype=np.float32)
    c["iota_f"] = np.tile(np.arange(cfg.TALL, dtype=np.float32)[None, :], (128, 1))
    c["tokid"] = (np.arange(128, dtype=np.float32)[:, None] + 128.0 * np.arange(18, dtype=np.float32)[None, :]).astype(np.float32)
    sel = np.zeros((16, 16, 128), np.float32)
    for e_ in range(16):
        sel[e_, e_, :] = 1.0
    c["sel"] = sel
    ep = 128 // cfg.CAPC
    sh = np.zeros((cfg.CAPC, ep, 128), np.float32)
    for e4 in range(ep):
        for s_ in range(cfg.CAPC):
            sh[s_, e4, e4 * cfg.CAPC + s_] = 1.0
    c["shift"] = sh
    kt = np.zeros((2, 4, 8), np.float64)
    idx = np.arange(8)
    kt[0, 0] = idx + 1; kt[0, 1] = -(idx + 1); kt[0, 2] = 7 - idx
    kt[1, 0] = 8 - idx; kt[1, 1] = idx - 8; kt[1, 2] = idx
    kt[:, 3, 0] = 1; kt[:, 3, 1] = 8
    ktt = np.stack([kt, kt / (2.0 * math.pi)], axis=2)
    c["s5_kt"] = np.ascontiguousarray(np.broadcast_to(ktt[None], (128, 2, 4, 2, 8))).astype(np.float32)
    s_idx = np.arange(128) // 16
    mf = (s_idx[None, :] >= s_idx[:, None]).astype(np.float32)
    mb = (s_idx[None, :] <= s_idx[:, None]).astype(np.float32)
    c["s5_mask"] = np.ascontiguousarray(np.stack([mf, mb], axis=1))
    return c


def host_layout(cfg, inp, b):
    D, NK, NB, NS5, NCV = cfg.D, cfg.NK, cfg.NB, cfg.NS5, cfg.NCV
    f32 = np.float32
    m = {}
    m["x"] = np.ascontiguousarray(inp["x"][b], f32)
    m["ctx"] = np.ascontiguousarray(inp["ctx"][b], f32)
    cond = np.stack([np.asarray(inp["c"][b]).reshape(NK, 128).T, np.asarray(inp["c_ctx"]).reshape(NK, 128).T], axis=2)
    m["cond"] = np.ascontiguousarray(cond, f32)
    m["ada_w"] = np.asarray(inp["ada_w"], f32)
    m["ada_b"] = np.asarray(inp["ada_b"], f32)[:, None, :]
    m["ln_g"] = np.asarray(inp["ln_g"], f32).reshape(-1, D)
    m["ln_b"] = np.asarray(inp["ln_b"], f32).reshape(-1, D)

    def glay(a):
        a = np.asarray(a, f32).reshape(NS5, 2, NB, 2, 8, 64)
        return a.transpose(0, 1, 2, 3, 5, 4).reshape(NS5, 2, NB, 128, 8)
    ldt = np.broadcast_to(np.asarray(inp["s5_log_dt"], f32)[..., None], np.asarray(inp["s5_a_re"]).shape)
    m["s5_lam"] = np.ascontiguousarray(np.stack([glay(inp["s5_a_re"]), glay(inp["s5_a_im"]), glay(ldt)], axis=4))

    def blay(re, im):
        out = []
        for a in (re, im):
            a = np.asarray(a, f32).reshape(NS5, 2, NB, 2, 8, 64, 16)
            out.append(a.transpose(0, 1, 2, 3, 5, 4, 6).reshape(NS5, 2, NB, 128, 8, 16))
        return np.ascontiguousarray(np.stack(out, axis=4))
    m["s5_B"] = blay(inp["s5_b_re"], inp["s5_b_im"])
    cre = np.asarray(inp["s5_c_re"], f32).transpose(0, 1, 2, 4, 3)
    cim = np.asarray(inp["s5_c_im"], f32).transpose(0, 1, 2, 4, 3)
    m["s5_C"] = blay(cre, cim)
    m["s5_d"] = np.asarray(inp["s5_d"], f32)[:, None, :]
    m["s5_w_glu"] = np.asarray(inp["s5_w_glu"], f32)
    m["s5_b_glu"] = np.asarray(inp["s5_b_glu"], f32)[:, None, :]
    n_ = max(NCV, 1)

    def pad0(a, shape):
        a = np.asarray(a, f32)
        if a.shape[0] == 0:
            return np.zeros(shape, f32)
        return np.ascontiguousarray(a.reshape(shape))
    m["cv_w_pw1"] = pad0(inp["cv_w_pw1"], (n_, D, 2 * D))
    bp = np.asarray(inp["cv_b_pw1"], f32)
    m["cv_b_pw1"] = np.ascontiguousarray(bp.reshape(-1, 2 * NK, 128).transpose(0, 2, 1)) if bp.shape[0] else np.zeros((n_, 128, 2 * NK), f32)
    wd = np.asarray(inp["cv_w_dw"], f32)
    m["cv_w_dw"] = np.ascontiguousarray(wd.reshape(-1, 31, NK, 128).transpose(0, 3, 2, 1)) if wd.shape[0] else np.zeros((n_, 128, NK, 31), f32)
    bd = np.asarray(inp["cv_b_dw"], f32)
    m["cv_b_dw"] = np.ascontiguousarray(bd.reshape(-1, NK, 128).transpose(0, 2, 1)) if bd.shape[0] else np.zeros((n_, 128, NK), f32)
    m["cv_ln_g"] = pad0(inp["cv_ln_g"], (n_, D)); m["cv_ln_b"] = pad0(inp["cv_ln_b"], (n_, D))
    m["cv_w_pw2"] = pad0(inp["cv_w_pw2"], (n_, D, D)); m["cv_b_pw2"] = pad0(inp["cv_b_pw2"], (n_, 1, D))
    m["moe_w_router"] = np.asarray(inp["moe_w_router"], f32)
    m["moe_w_in"] = np.asarray(inp["moe_w_in"], f32)
    m["moe_w_out"] = np.asarray(inp["moe_w_out"], f32)
    m.update(host_consts(cfg))
    return m


def moe_layer(kb, i, io, scr, with_ctx):
    cfg, f = kb.cfg, kb.f
    D, NK, NT, E, F, NF, L, LC = cfg.D, cfg.NK, cfg.NT, cfg.E, cfg.F, cfg.NF, cfg.L, cfg.LC
    XR, MOD, TS, U2, YG = scr["XR"], scr["MOD"], scr["TS"], scr["U2"], scr["YG"]
    CAPL, CAPC = cfg.CAPL, (cfg.CAPC if with_ctx else 0)
    NSL = CAPL + CAPC
    ntiles = cfg.NTT if with_ctx else cfg.NLT
    NLT = cfg.NLT
    sets = [(0, L, CAPL, 0, 0)]
    if with_ctx:
        sets.append((L, LC, CAPC, CAPL, 1))
    stiles = [(s * 128, min(128, CAPL - s * 128), 0) for s in range((CAPL + 127) // 128)]
    if with_ctx:
        stiles.append((CAPL, CAPC, 1))
    NST = len(stiles)
    with ExitStack() as esl:
        sbl = lambda n, s, dt=F32: kb.sb(esl, n, s, dt)
        IDXF = sbl("IDXF", [16, NSL]); GAT = sbl("GAT", [16, NSL])
        IDXT = sbl("IDXT", [128, NST, 16]); GT = sbl("GT", [128, NST, 16])
        idf = sbl("idf", [128, 128]); idb = sbl("idb", [128, 128], BF16)
        tokid = sbl("tokid", [128, 16 + 2])
        kb.dma("sp", idf[:], io["ident"], [], ["idf"])
        kb.cp("dve", idb[:], idf[:], ["idf"], ["idb"])
        kb.dma("sp", tokid[:], io["tokid"], [], ["tokid"])
        esu = ExitStack()
        U2TM = kb.sb(esu, "U2TM", [128, ntiles, D], BF16)
        with ExitStack() as es:
            sb = lambda n, s, dt=F32: kb.sb(es, n, s, dt)
            wrf = sb("wrf", [128, NK, E]); wrb = sb("wrb", [128, NK, E], BF16)
            kb.dma("sp", wrf[:], io["moe_w_router"][i].rearrange("(k p) e -> p k e", p=128), [], ["wrf"])
            kb.cp("dve", wrb[:], wrf[:], ["wrf"], ["wrb"])
            AFFT = sb("AFFT", [16, L + LC]); WK = sb("WK", [16, L])
            IDXU = sb("IDXU", [16, NSL], U32)
            uT = [sb("uT%d" % q, [128, NK, 128], BF16) for q in range(2)]
            lg = [sb("lg%d" % q, [128, E]) for q in range(2)]
            sm = [sb("smx%d" % q, [128, 4]) for q in range(2)]
            pT = [kb.ps(es, "pT%d" % q, [128, 512], BF16) for q in range(2)]
            pL = [kb.ps(es, "pL%d" % q, [128, 512]) for q in range(2)]
            pA = [kb.ps(es, "pA%d" % q, [128, 512]) for q in range(2)]
            r = 0
            for tt in range(ntiles):
                q = tt % 2
                uk = "U2TM%d" % tt
                kb.dma("sp", U2TM[:, tt, :], U2[tt * 128:(tt + 1) * 128, :], ["U2"], [uk])
                for k4 in range(0, NK, 4):
                    nk = min(4, NK - k4)
                    p = pT[r % 2]; pk = "pT%d" % (r % 2); r += 1
                    for a in range(nk):
                        kb.tr(p[:, a * 128:(a + 1) * 128], U2TM[:, tt, (k4 + a) * 128:(k4 + a + 1) * 128], idb[:], [uk, "idb"], [pk])
                    kb.cp("act" if (r % 2) else "dve", uT[q][:, k4:k4 + nk, :], p[:, 0:nk * 128].rearrange("p (a c) -> p a c", a=nk), [pk], ["uT%d" % q])
                for k in range(NK):
                    kb.mm(pL[q][:, 0:E], uT[q][:, k, :], wrb[:, k, :], k == 0, k == NK - 1, ["uT%d" % q, "wrb"], ["pL%d" % q])
                lk, sk = "lg%d" % q, "smx%d" % q
                kb.f.op("dve", lambda e, o=sm[q][:, 0:1], i_=pL[q][:, 0:E]: e.tensor_reduce(out=o, in_=i_, axis=mybir.AxisListType.X, op=ALU.max), ["pL%d" % q], [sk])
                kb.ts("dve", sm[q][:, 1:2], sm[q][:, 0:1], -1.0, None, ALU.mult, None, [sk], [sk])
                kb.act(lg[q][:], pL[q][:, 0:E], AF.Exp, ["pL%d" % q, sk], [lk], bias=sm[q][:, 1:2])
                kb.f.op("dve", lambda e, o=sm[q][:, 2:3], i_=lg[q][:]: e.tensor_reduce(out=o, in_=i_, axis=mybir.AxisListType.X, op=ALU.add), [lk], [sk])
                kb.f.op("dve", lambda e, o=sm[q][:, 3:4], i_=sm[q][:, 2:3]: e.reciprocal(out=o, in_=i_), [sk], [sk])
                kb.ts("dve", lg[q][:], lg[q][:], sm[q][:, 3:4], None, ALU.mult, None, [lk, sk], [lk])
                kb.tr(pA[q][0:E, 0:128], lg[q][:], idf[:], [lk, "idf"], ["pA%d" % q])
                kb.cp("act", AFFT[:, tt * 128:(tt + 1) * 128], pA[q][0:E, 0:128], ["pA%d" % q], ["AFFT"])
            for (tok0, ntok, cap, slot0, isctx) in sets:
                src = AFFT[:, tok0:tok0 + ntok]
                srck = "AFFT"
                for rd in range(cap // 8):
                    sl = slice(slot0 + rd * 8, slot0 + rd * 8 + 8)
                    kb.f.op("dve", lambda e, o=GAT[:, sl], i_=src: e.max(out=o, in_=i_), [srck], ["GAT"])
                    kb.f.op("dve", lambda e, o=IDXU[:, sl], m_=GAT[:, sl], v_=src: e.max_index(out=o, in_max=m_, in_values=v_), [srck, "GAT"], ["IDXU"])
                    if rd < cap // 8 - 1:
                        dst = WK[:, 0:ntok]
                        kb.f.op("dve", lambda e, o=dst, m_=GAT[:, sl], v_=src: e.match_replace(out=o, in_to_replace=m_, in_values=v_, imm_value=-1.0), [srck, "GAT"], ["WK"])
                        src = dst
                        srck = "WK"
            kb.cp("dve", IDXF[:], IDXU[:], ["IDXU"], ["IDXF"])
            if with_ctx:
                kb.ts("dve", IDXF[:, CAPL:NSL], IDXF[:, CAPL:NSL], float(L), None, ALU.add, None, ["IDXF"], ["IDXF"])
            for si, (s0, n, _) in enumerate(stiles):
                q = si % 2
                kb.tr(pA[q][0:n, 0:16], IDXF[:, s0:s0 + n], idf[0:16, 0:16], ["IDXF", "idf"], ["pA%d" % q])
                kb.cp("dve", IDXT[0:n, si, :], pA[q][0:n, 0:16], ["pA%d" % q], ["IDXT"])
                kb.tr(pL[q][0:n, 0:16], GAT[:, s0:s0 + n], idf[0:16, 0:16], ["GAT", "idf"], ["pL%d" % q])
                kb.cp("dve", GT[0:n, si, :], pL[q][0:n, 0:16], ["pL%d" % q], ["GT"])
        f.barrier()
        with ExitStack() as es:
            sb = lambda n, s, dt=F32: kb.sb(es, n, s, dt)
            SEL = sb("SEL", [16, 16, 128])
            kb.dma("sp", SEL[:], io["sel"], [], ["SEL"])
            XselT = [sb("XselT0", [128, NK, NSL], BF16)] * 2
            HT = sb("HT", [128, NF, NSL], BF16)
            Sx = [sb("Sx0", [128, ntiles, CAPL], BF16)] * 2
            NWIN = 3
            WIN = [sb("WIN%d" % q, [128, 2, NK, 128], BF16) for q in range(NWIN)]
            WOUT = [sb("WOUT%d" % q, [128, NF, D], BF16) for q in range(2)]
            sg = [sb("sg%d" % q, [128, NSL]) for q in range(2)]
            ygs = [sb("ygs%d" % q, [128, D], BF16) for q in range(2)]
            pb = kb.ps(es, "pb", [128, 512])
            pgx = [kb.ps(es, "pgx%d" % q, [128, 512]) for q in range(2)]
            ph = [kb.ps(es, "ph%d" % q, [128, 512]) for q in range(4)]
            py = kb.ps(es, "py", [128, 512])
            wi_r = 0; wo_r = 0; yg_r = 0; gx_r = 0
            for e_ in range(E):
                q = e_ % 2
                sxk = "Sx0"
                kb.mm(pb[:, 0:NSL], SEL[:, e_, :], IDXF[:, :], True, True, ["SEL", "IDXF"], ["pb"])
                for tt in range(ntiles):
                    isctx = 1 if tt >= NLT else 0
                    s0, cap = (CAPL, CAPC) if isctx else (0, CAPL)
                    kb.ts("dve", Sx[q][:, tt, 0:cap], pb[:, s0:s0 + cap], tokid[:, tt:tt + 1], None, ALU.is_equal, None, ["pb", "tokid"], [sxk])
                xk = "XselT0"
                for k in range(NK):
                    pg_ = pgx[gx_r % 2]; pgk = "pgx%d" % (gx_r % 2); gx_r += 1
                    for tt in range(NLT):
                        kb.mm(pg_[:, 0:CAPL], U2TM[:, tt, k * 128:(k + 1) * 128], Sx[q][:, tt, 0:CAPL], tt == 0, tt == NLT - 1, ["U2TM%d" % tt, sxk], [pgk])
                    if with_ctx:
                        for tt in range(NLT, ntiles):
                            kb.mm(pg_[:, CAPL:NSL], U2TM[:, tt, k * 128:(k + 1) * 128], Sx[q][:, tt, 0:CAPC], tt == NLT, tt == ntiles - 1, ["U2TM%d" % tt, sxk], [pgk])
                    kb.cp("act" if k % 2 else "dve", XselT[q][:, k, :], pg_[:, 0:NSL], [pgk], [xk])
                wo = WOUT[e_ % 2]; wok = "WOUT%d" % (e_ % 2)
                for fc in range(NF):
                    wq = wi_r % NWIN; wi_r += 1
                    for gu in range(2):
                        c0 = gu * F + fc * 128
                        kb.dma("pool", WIN[wq][:, gu, :, :], io["moe_w_in"][i, e_, :, c0:c0 + 128].rearrange("(k p) c -> p k c", p=128), [], ["WIN%d_%d" % (wq, gu)])
                    kb.dma("pool", wo[:, fc, :], io["moe_w_out"][i, e_, fc * 128:(fc + 1) * 128, :], [], [wok + "_%d" % fc])
                    hq = fc % 2
                    phg = ph[hq * 2]; phu = ph[hq * 2 + 1]
                    pgk_, puk_ = "ph%d" % (hq * 2), "ph%d" % (hq * 2 + 1)
                    for k in range(NK):
                        kb.mm(phg[:, 0:NSL], WIN[wq][:, 0, k, :], XselT[q][:, k, :], k == 0, k == NK - 1, ["WIN%d_0" % wq, xk], [pgk_])
                    for k in range(NK):
                        kb.mm(phu[:, 0:NSL], WIN[wq][:, 1, k, :], XselT[q][:, k, :], k == 0, k == NK - 1, ["WIN%d_1" % wq, xk], [puk_])
                    kb.act(sg[hq][:], phg[:, 0:NSL], AF.Silu, [pgk_], ["sg%d" % hq])
                    kb.tt("dve", HT[:, fc, :], phu[:, 0:NSL], sg[hq][:], ALU.mult, [puk_, "sg%d" % hq], ["HT"])
                for si, (s0, n, _) in enumerate(stiles):
                    yq = yg_r % 2; yg_r += 1
                    for nt in range(D // NT):
                        for fk in range(NF):
                            kb.mm(py[0:n, 0:NT], HT[:, fk, s0:s0 + n], wo[:, fk, nt * NT:(nt + 1) * NT], fk == 0, fk == NF - 1, ["HT", wok + "_%d" % fk], ["py"])
                        if nt % 2:
                            kb.ts("dve", ygs[yq][0:n, nt * NT:(nt + 1) * NT], py[0:n, 0:NT], GT[0:n, si, e_:e_ + 1], None, ALU.mult, None, ["py", "GT"], ["ygs%d" % yq])
                        else:
                            kb.act(ygs[yq][0:n, nt * NT:(nt + 1) * NT], py[0:n, 0:NT], AF.Copy, ["py", "GT"], ["ygs%d" % yq], scale=GT[0:n, si, e_:e_ + 1])
                    kb.dma("sp", YG[e_, s0:s0 + n, :], ygs[yq][0:n, :], ["ygs%d" % yq], ["YG"])
        f.barrier()
        esu.close()
        with ExitStack() as es:
            sb = lambda n, s, dt=F32: kb.sb(es, n, s, dt)
            NSTL = (CAPL + 127) // 128
            PL = min(128, CAPL)
            iot = sb("iot", [128, L + LC])
            kb.dma("sp", iot[:], io["iota_f"], [], ["iot"])
            YGL = sb("YGL", [128, E, NSTL, D], BF16)
            for e_ in range(E):
                kb.dma("sp" if e_ % 2 else "pool", YGL[0:PL, e_, :, :], YG[e_, 0:CAPL, :].rearrange("(s p) c -> p s c", p=PL), ["YG"], ["YGL%d" % e_])
            g2t = sb("g2t", [128, 2, D])
            for a_ in range(2 if with_ctx else 1):
                kb.dma("sp", g2t[:, a_, :], bcast_rows(MOD[i, 6 * a_ + 5:6 * a_ + 6, :], 128), [], ["g2t"])
            NEQ = (E * CAPC + 127) // 128 if with_ctx else 0
            EP = 128 // CAPC if with_ctx else 1
            if with_ctx:
                YGC = sb("YGC", [128, NEQ, D], BF16)
                ygv = YG[:, CAPL:NSL, :].rearrange("(eq e4) s c -> e4 s eq c", e4=EP)
                for e4 in range(EP):
                    kb.dma("sp", YGC[e4 * CAPC:(e4 + 1) * CAPC, :, :], ygv[e4], ["YG"], ["YGC"])
                shf = sb("shf", [CAPC, EP, 128]); IDXC = sb("IDXC", [128, NEQ])
                kb.dma("sp", shf[:], io["shift"], [], ["shf"])
                pI = kb.ps(es, "pI", [128, 512])
                for e4 in range(EP):
                    rhs = fview(IDXT, NSTL * 16 + e4, [(EP, NEQ)], pn=CAPC)
                    kb.mm(pI[:, 0:NEQ], shf[:, e4, :], rhs, e4 == 0, e4 == EP - 1, ["shf", "IDXT"], ["pI"])
                kb.cp("dve", IDXC[:], pI[:, 0:NEQ], ["pI"], ["IDXC"])
                STC = [sb("STC%d" % q, [128, NEQ, 128], BF16) for q in range(2)]
            STL = [sb("STL%d" % q, [128, E * NSTL, 128], BF16) for q in range(2)]
            xs = [sb("xs%d" % q, [128, NT]) for q in range(2)]
            ft = [sb("ft%d" % q, [128, NT]) for q in range(2)]
            pf = [kb.ps(es, "pf%d" % q, [128, 512]) for q in range(2)]
            it = 0
            for tt in range(ntiles):
                tq = tt % 2
                isctx = 1 if tt >= NLT else 0
                rows = slice(tt * 128, (tt + 1) * 128)
                if not isctx:
                    terms = [(e_, s_) for e_ in range(E) for s_ in range(NSTL)]
                    for ti, (e_, s_) in enumerate(terms):
                        pl_ = ti % 3 == 2
                        kb.ts("pool" if pl_ else "dve", STL[tq][0:PL, ti, :], iot[0:PL, tt * 128:(tt + 1) * 128], IDXT[0:PL, s_, e_:e_ + 1], None,
                              ALU.is_equal, None, ["iot", "IDXT"], ["STL%d_%d_%d" % (tq, int(pl_), ti)])
                else:
                    for eq in range(NEQ):
                        kb.ts("dve", STC[tq][:, eq, :], iot[:, tt * 128:(tt + 1) * 128], IDXC[:, eq:eq + 1], None, ALU.is_equal, None, ["iot", "IDXC"], ["STC%d_%d" % (tq, eq)])
                for nt in range(D // NT):
                    cs_ = slice(nt * NT, (nt + 1) * NT)
                    q = it % 2; it += 1
                    kb.dma("sp", xs[q][:], XR[rows, cs_], [], ["xs%d" % q])
                    if not isctx:
                        for ti, (e_, s_) in enumerate(terms):
                            kb.mm(pf[q][:, 0:NT], STL[tq][0:PL, ti, :], YGL[0:PL, e_, s_, cs_], ti == 0, ti == len(terms) - 1,
                                  ["STL%d_%d_%d" % (tq, int(ti % 3 == 2), ti), "YGL%d" % e_], ["pf%d" % q])
                    else:
                        for eq in range(NEQ):
                            kb.mm(pf[q][:, 0:NT], STC[tq][:, eq, :], YGC[:, eq, cs_], eq == 0, eq == NEQ - 1, ["STC%d_%d" % (tq, eq), "YGC"], ["pf%d" % q])
                    kb.tt("dve", ft[q][:], pf[q][:, 0:NT], g2t[:, isctx, cs_], ALU.mult, ["pf%d" % q, "g2t"], ["ft%d" % q])
                    kb.stt(ft[q][:], xs[q][:], cfg.ALPHA, ft[q][:], ALU.mult, ALU.add, ["xs%d" % q, "ft%d" % q], ["ft%d" % q])
                    kb.dma("pool", TS[rows, cs_], ft[q][:], ["ft%d" % q], ["TS"])
    f.barrier()


def kernel(**inputs):
    cfg = Cfg()
    nb = int(np.asarray(inputs["x"]).shape[0])
    nc = build(cfg)
    maps = [host_layout(cfg, inputs, b) for b in range(nb)]
    res = run_bass_kernel_spmd(nc, maps, core_ids=list(range(nb)))
    out = np.stack([np.asarray(res.results[b]["out"]) for b in range(nb)])
    return out.astype(np.float32)
```

```python
import math
from contextlib import ExitStack

import numpy as np
import concourse.bass as bass
import concourse.mybir as mybir
from concourse.ap import AP
from concourse.bass_utils import run_bass_kernel_spmd

F32 = mybir.dt.float32
BF16 = mybir.dt.bfloat16
I32 = mybir.dt.int32
U32 = mybir.dt.uint32
AF = mybir.ActivationFunctionType
ALU = mybir.AluOpType

COMPUTE = ("pe", "act", "dve", "pool")
DMAQ = ("sp", "act", "pool")
NDMASEM = 8


class Op:
    __slots__ = ("eng", "fn", "deps", "is_dma", "needed", "semval")

    def __init__(self, eng, fn, is_dma):
        self.eng = eng
        self.fn = fn
        self.deps = []
        self.is_dma = is_dma
        self.needed = False
        self.semval = None


class FW:
    def __init__(self, nc):
        self.nc = nc
        self.streams = {e: [] for e in ("pe", "act", "dve", "pool", "sp")}
        self.last_w = {}
        self.readers = {}
        self.bar_idx = {e: 0 for e in self.streams}

    def op(self, eng, fn, reads=(), writes=(), dma=False):
        o = Op(eng, fn, dma)
        deps = {}
        for k in reads:
            lw = self.last_w.get(k)
            if lw is not None:
                deps[id(lw)] = lw
        for k in writes:
            lw = self.last_w.get(k)
            if lw is not None:
                deps[id(lw)] = lw
            for r in self.readers.get(k, ()):
                deps[id(r)] = r
        for d in deps.values():
            if (not d.is_dma) and (not dma) and d.eng == eng and eng == "pe":
                continue
            o.deps.append(d)
            d.needed = True
        for k in writes:
            self.last_w[k] = o
            self.readers[k] = []
        for k in reads:
            if k in writes:
                continue
            lst = self.readers.setdefault(k, [])
            if not dma:
                lst[:] = [r for r in lst if not (r.eng == eng and not r.is_dma)]
            lst.append(o)
        self.streams[eng].append(o)
        return o

    def barrier(self):
        lastops = []
        for e, st in self.streams.items():
            for o in reversed(st):
                if (not o.is_dma) and o.fn is not None:
                    lastops.append(o)
                    break
            lastops += [o for o in st[self.bar_idx[e]:] if o.is_dma]
        for e in self.streams:
            o = Op(e, None, False)
            o.deps = list(lastops)
            self.streams[e].append(o)
        for d in lastops:
            d.needed = True
        for e in self.streams:
            self.bar_idx[e] = len(self.streams[e])
        self.last_w = {}
        self.readers = {}

    def emit(self, final_wait_ops=()):
        nc = self.nc
        with ExitStack() as es:
            csem = {e: es.enter_context(nc.semaphore("c_" + e)) for e in COMPUTE}
            dsems = {e: [es.enter_context(nc.semaphore("d_%s%d" % (e, i))) for i in range(NDMASEM)]
                     for e in DMAQ}
            for e in COMPUTE:
                cnt = 0
                for o in self.streams[e]:
                    if o.is_dma or o.fn is None:
                        continue
                    if o.needed:
                        cnt += 1
                        o.semval = (csem[e], cnt)
            for e in DMAQ:
                dcount = [0] * NDMASEM
                rr = 0
                for o in self.streams[e]:
                    if not o.is_dma:
                        continue
                    dcount[rr] += 1
                    o.semval = (dsems[e][rr], 16 * dcount[rr])
                    rr = (rr + 1) % NDMASEM
            block = es.enter_context(nc.Block())
            handles = {"pe": block.tensor, "act": block.scalar, "dve": block.vector,
                       "pool": block.gpsimd, "sp": block.sync}
            for e in ("sp", "pool", "act", "dve", "pe"):
                ops = self.streams[e]
                finals = list(final_wait_ops) if e == "sp" else []

                def body(eng, ops=ops, finals=finals):
                    waited = {}

                    def wait(sem, val):
                        k = id(sem)
                        if waited.get(k, 0) >= val:
                            return
                        waited[k] = val
                        eng.wait_ge(sem, val)

                    for o in ops:
                        for d in o.deps:
                            wait(*d.semval)
                        if o.fn is None:
                            continue
                        if o.is_dma:
                            sem, val = o.semval
                            if val > 16:
                                wait(sem, val - 16)
                            o.fn(eng).then_inc(sem, 16)
                        else:
                            ins = o.fn(eng)
                            if o.needed:
                                ins.then_inc(o.semval[0], 1)
                    for o in finals:
                        wait(*o.semval)

                handles[e](body)


class Cfg:
    def __init__(self, D=2048, L=2048, LC=256, DEPTH=4):
        self.D, self.L, self.LC, self.DEPTH = D, L, LC, DEPTH
        self.E = 16
        self.GW = 64
        self.KW = 31
        self.NK = D // 128
        self.G = D // 16
        self.NB = self.G // 16
        self.F = D // 2
        self.NF = self.F // 128
        self.NT = min(512, D)
        self.TALL = L + LC
        self.NLT = L // 128
        self.NCT = LC // 128
        self.NTT = self.TALL // 128
        self.CAPL = 2 * L // 16
        self.CAPC = 2 * LC // 16
        self.NCL = L // 8
        self.NCC = LC // 8
        self.NC = self.NCL + self.NCC
        self.ALPHA = (2.0 * DEPTH) ** 0.25
        self.NS5 = (DEPTH + 1) // 2
        self.NCV = DEPTH // 2


LN_EPS = 1e-5
TWO_PI = 2.0 * math.pi


def fview(t, off, dims, p0=0, pn=None):
    base = t[:]
    pstep, pcount = base.ap[0]
    if pn is None:
        pn = pcount - p0
    return AP(base.tensor, base.offset + p0 * pstep + off, [[pstep, pn]] + [list(d) for d in dims])


class KB:
    def __init__(self, cfg, debug_outs=()):
        self.cfg = cfg
        self.nc = bass.Bass("TRN2", target_bir_lowering=False)
        self.f = FW(self.nc)
        self.debug_outs = set(debug_outs)
        self.uid = 0
        self.out_stores = []

    def din(self, name, shape, dt=F32):
        return self.nc.dram_tensor(name, list(shape), dt, kind="ExternalInput").ap()

    def dscr(self, name, shape, dt=F32):
        kind = "ExternalOutput" if name in self.debug_outs else "Internal"
        return self.nc.dram_tensor(name, list(shape), dt, kind=kind).ap()

    def sb(self, es, name, shape, dt=F32):
        self.uid += 1
        return es.enter_context(self.nc.sbuf_tensor("%s_%d" % (name, self.uid), list(shape), dt))

    def ps(self, es, name, shape, dt=F32):
        self.uid += 1
        return es.enter_context(self.nc.psum_tensor("%s_%d" % (name, self.uid), list(shape), dt))

    SCRATCH = ("XR", "TS", "U2", "ZS", "CV", "MOD", "YG", "OUTF")

    def dma(self, q, out, in_, r, w):
        w2 = []
        for k in w:
            if k in self.SCRATCH:
                self.uid += 1
                k = "%s#%d" % (k, self.uid)
            w2.append(k)
        return self.f.op(q, lambda e: e.dma_start(out=out, in_=in_), r, w2, dma=True)

    def tt(self, eng, out, in0, in1, op, r, w):
        return self.f.op(eng, lambda e: e.tensor_tensor(out=out, in0=in0, in1=in1, op=op), r, w)

    def ts(self, eng, out, in0, s1, s2, op0, op1, r, w):
        if s2 is None:
            return self.f.op(eng, lambda e: e.tensor_scalar(out=out, in0=in0, scalar1=s1, scalar2=None, op0=op0), r, w)
        return self.f.op(eng, lambda e: e.tensor_scalar(out=out, in0=in0, scalar1=s1, scalar2=s2, op0=op0, op1=op1), r, w)

    def stt(self, out, in0, scalar, in1, op0, op1, r, w):
        return self.f.op("dve", lambda e: e.scalar_tensor_tensor(out=out, in0=in0, scalar=scalar, in1=in1, op0=op0, op1=op1), r, w)

    def cp(self, eng, out, in_, r, w):
        if eng == "act":
            return self.f.op(eng, lambda e: e.copy(out=out, in_=in_), r, w)
        return self.f.op(eng, lambda e: e.tensor_copy(out=out, in_=in_), r, w)

    def act(self, out, in_, func, r, w, scale=1.0, bias=None):
        if bias is None:
            return self.f.op("act", lambda e: e.activation(out=out, in_=in_, func=func, scale=scale), r, w)
        return self.f.op("act", lambda e: e.activation(out=out, in_=in_, func=func, scale=scale, bias=bias), r, w)

    def mm(self, out, lhsT, rhs, start, stop, r, w):
        return self.f.op("pe", lambda e: e.matmul(out=out, lhsT=lhsT, rhs=rhs, start=start, stop=stop), r, w)

    def tr(self, out, in_, ident, r, w):
        return self.f.op("pe", lambda e: e.transpose(out=out, in_=in_, identity=ident), r, w)

    def memset(self, eng, ap, val, r, w):
        return self.f.op(eng, lambda e: e.memset(ap, val), r, w)


def s5_paramgen_gen(kb, es, j, io, SW, A8T, A8D):
    cfg, f = kb.cfg, kb.f
    GL = 8
    NB = cfg.NB
    GW = NB * GL
    sb = lambda n, s, dt=F32: kb.sb(es, n, s, dt)
    kt = sb("kt", [128, 2, 4, 2, 8])
    msk = sb("msk", [128, 2, 128])
    idf = sb("idf", [128, 128])
    kb.dma("sp", kt[:], io["s5_kt"], [], ["kt"])
    kb.dma("sp", msk[:], io["s5_mask"], [], ["msk"])
    kb.dma("sp", idf[:], io["ident"], [], ["idf_pg"])
    LM = sb("LM", [128, 3, GW]); Bt = sb("Bt", [128, 2, GW, 16]); Ct = sb("Ct", [128, 2, GW, 16])
    dtv = sb("dtv", [128, GW]); XRI = sb("XRI", [128, 2, GW])
    marg = sb("marg", [128, GW, 8]); mag = sb("mag", [128, GW, 8]); y = sb("y", [128, GW, 8])
    yi = sb("yi", [128, GW, 8], I32); yf = sb("yf", [128, GW, 8]); m1 = sb("m1", [128, GW, 8])
    sinv = sb("sinv", [128, GW, 8]); cosv = sb("cosv", [128, GW, 8])
    P = sb("P", [128, 4, 2, GW, 8])
    sm = sb("sm", [128, 8, GW])
    Bb = sb("Bb", [128, 2, GW, 16]); tb = sb("tb", [128, GW, 16])
    WP = sb("WP", [128, 2, GL, 128]); WE = sb("WE", [128, 2, GL, 128]); CM = sb("CM", [128, 2, GL, 128])
    tw = sb("tw", [128, GL, 128])
    OUT = sb("OUT", [128, GL, 6, 128], BF16)
    pT = [kb.ps(es, "pT%d" % i, [128, 512]) for i in range(2)]
    pM = [kb.ps(es, "pM%d" % i, [128, 512]) for i in range(2)]
    V = "dve"
    for d in range(2):
        for blk in range(NB):
            gs = slice(blk * GL, (blk + 1) * GL)
            kb.dma("sp", LM[:, :, gs], io["s5_lam"][j, d, blk], [], ["LM"])
            kb.dma("sp", Bt[:, :, gs, :], io["s5_B"][j, d, blk], [], ["Bt"])
            kb.dma("sp", Ct[:, :, gs, :], io["s5_C"][j, d, blk], [], ["Ct"])
        kb.act(dtv[:], LM[:, 2, :], AF.Exp, ["LM"], ["dtv"])
        kb.tt(V, XRI[:, 0, :], LM[:, 0, :], dtv[:], ALU.mult, ["LM", "dtv"], ["XRI"])
        kb.tt(V, XRI[:, 1, :], LM[:, 1, :], dtv[:], ALU.mult, ["LM", "dtv"], ["XRI"])
        for ty in range(4):
            ktab = fview(kt, ((d * 4 + ty) * 2 + 0) * 8, [(0, GW), (1, 8)])
            ktab2 = fview(kt, ((d * 4 + ty) * 2 + 1) * 8, [(0, GW), (1, 8)])
            xr_b = fview(XRI, 0, [(1, GW), (0, 8)])
            xi_b = fview(XRI, GW, [(1, GW), (0, 8)])
            kb.tt(V, marg[:], xr_b, ktab, ALU.mult, ["XRI", "kt"], ["marg"])
            kb.act(mag[:], marg[:], AF.Exp, ["marg"], ["mag"])
            kb.tt(V, y[:], xi_b, ktab2, ALU.mult, ["XRI", "kt"], ["y"])
            kb.cp(V, yi[:], y[:], ["y"], ["yi"])
            kb.cp(V, yf[:], yi[:], ["yi"], ["yf"])
            kb.tt(V, y[:], y[:], yf[:], ALU.subtract, ["y", "yf"], ["y"])
            kb.ts(V, m1[:], y[:], 0.5, None, ALU.is_gt, None, ["y"], ["m1"])
            kb.tt(V, y[:], y[:], m1[:], ALU.subtract, ["y", "m1"], ["y"])
            kb.ts(V, m1[:], y[:], -0.5, None, ALU.is_lt, None, ["y"], ["m1"])
            kb.tt(V, y[:], y[:], m1[:], ALU.add, ["y", "m1"], ["y"])
            kb.act(sinv[:], y[:], AF.Sin, ["y"], ["sinv"], scale=TWO_PI)
            kb.ts(V, yf[:], y[:], 0.25, None, ALU.add, None, ["y"], ["yf"])
            kb.ts(V, m1[:], yf[:], 0.5, None, ALU.is_gt, None, ["yf"], ["m1"])
            kb.tt(V, yf[:], yf[:], m1[:], ALU.subtract, ["yf", "m1"], ["yf"])
            kb.act(cosv[:], yf[:], AF.Sin, ["yf"], ["cosv"], scale=TWO_PI)
            kb.tt(V, P[:, ty, 0], mag[:], cosv[:], ALU.mult, ["mag", "cosv"], ["P"])
            kb.tt(V, P[:, ty, 1], mag[:], sinv[:], ALU.mult, ["mag", "sinv"], ["P"])
        yield
        a1re = P[:, 3, 0, :, 0]; a1im = P[:, 3, 1, :, 0]
        a8re = P[:, 3, 0, :, 1]; a8im = P[:, 3, 1, :, 1]
        lre = LM[:, 0, :]; lim = LM[:, 1, :]
        S = lambda i_: sm[:, i_, :]
        kb.ts(V, S(0), a1re, -1.0, None, ALU.add, None, ["P"], ["sm0"])
        kb.tt(V, S(1), lre, lre, ALU.mult, ["LM"], ["sm1"])
        kb.tt(V, S(2), lim, lim, ALU.mult, ["LM"], ["sm2"])
        kb.tt(V, S(1), S(1), S(2), ALU.add, ["sm1", "sm2"], ["sm1"])
        kb.f.op(V, lambda e, o=S(1), i_=S(1): e.reciprocal(out=o, in_=i_), ["sm1"], ["sm1"])
        kb.tt(V, S(2), S(0), lre, ALU.mult, ["sm0", "LM"], ["sm2"])
        kb.tt(V, S(3), a1im, lim, ALU.mult, ["P", "LM"], ["sm3"])
        kb.tt(V, S(2), S(2), S(3), ALU.add, ["sm2", "sm3"], ["sm2"])
        kb.tt(V, S(4), S(2), S(1), ALU.mult, ["sm2", "sm1"], ["sm4"])
        kb.tt(V, S(2), a1im, lre, ALU.mult, ["P", "LM"], ["sm2"])
        kb.tt(V, S(3), S(0), lim, ALU.mult, ["sm0", "LM"], ["sm3"])
        kb.tt(V, S(2), S(2), S(3), ALU.subtract, ["sm2", "sm3"], ["sm2"])
        kb.tt(V, S(5), S(2), S(1), ALU.mult, ["sm2", "sm1"], ["sm5"])
        cre_b = fview(sm, 4 * GW, [(1, GW), (0, 16)]); cim_b = fview(sm, 5 * GW, [(1, GW), (0, 16)])
        kb.tt(V, Bb[:, 0], Bt[:, 0], cre_b, ALU.mult, ["Bt", "sm4"], ["Bb"])
        kb.tt(V, tb[:], Bt[:, 1], cim_b, ALU.mult, ["Bt", "sm5"], ["tb"])
        kb.tt(V, Bb[:, 0], Bb[:, 0], tb[:], ALU.subtract, ["Bb", "tb"], ["Bb"])
        kb.tt(V, Bb[:, 1], Bt[:, 1], cre_b, ALU.mult, ["Bt", "sm4"], ["Bb"])
        kb.tt(V, tb[:], Bt[:, 0], cim_b, ALU.mult, ["Bt", "sm5"], ["tb"])
        kb.tt(V, Bb[:, 1], Bb[:, 1], tb[:], ALU.add, ["Bb", "tb"], ["Bb"])
        kb.cp(V, S(6), a8re, ["P"], ["sm6"])
        kb.cp(V, S(7), a8im, ["P"], ["sm7"])
        BST = 2 * 5 * 2 * 2 * GL
        for lvl in range(5):
            base = ((d * 5 + lvl) * 2) * 2 * GL
            a8t = lambda which, ri: fview(A8T, base + (which * 2 + ri) * GL, [(BST, NB), (1, GL)])
            s6 = fview(sm, 6 * GW, [(GL, NB), (1, GL)]); s7 = fview(sm, 7 * GW, [(GL, NB), (1, GL)])
            kb.cp(V, a8t(0, 0), s6, ["sm6"], ["A8T"])
            kb.cp(V, a8t(0, 1), s6, ["sm6"], ["A8T"])
            kb.ts(V, a8t(1, 0), s7, -1.0, None, ALU.mult, None, ["sm7"], ["A8T"])
            kb.cp(V, a8t(1, 1), s7, ["sm7"], ["A8T"])
            if lvl < 4:
                kb.tt(V, S(2), S(6), S(6), ALU.mult, ["sm6"], ["sm2"])
                kb.tt(V, S(3), S(7), S(7), ALU.mult, ["sm7"], ["sm3"])
                kb.stt(S(7), S(6), 2.0, S(7), ALU.mult, ALU.mult, ["sm6", "sm7"], ["sm7"])
                kb.tt(V, S(6), S(2), S(3), ALU.subtract, ["sm2", "sm3"], ["sm6"])
        yield
        for blk in range(NB):
            g0 = blk * GL

            def cprod(dst, ty, src, key_src, key_dst, g0=g0):
                pre = fview(P, ((ty * 2 + 0) * GW + g0) * 8, [(8, GL), (1, 8), (0, 16)])
                pim = fview(P, ((ty * 2 + 1) * GW + g0) * 8, [(8, GL), (1, 8), (0, 16)])
                sre = fview(src, g0 * 16, [(16, GL), (0, 8), (1, 16)])
                sim = fview(src, (GW + g0) * 16, [(16, GL), (0, 8), (1, 16)])
                dre = fview(dst, 0, [(128, GL), (16, 8), (1, 16)])
                dim_ = fview(dst, GL * 128, [(128, GL), (16, 8), (1, 16)])
                twv = fview(tw, 0, [(128, GL), (16, 8), (1, 16)])
                kb.tt(V, dre, pre, sre, ALU.mult, ["P", key_src], [key_dst])
                kb.tt(V, twv, pim, sim, ALU.mult, ["P", key_src], ["tw"])
                kb.tt(V, dre, dre, twv, ALU.subtract, [key_dst, "tw"], [key_dst])
                kb.tt(V, dim_, pre, sim, ALU.mult, ["P", key_src], [key_dst])
                kb.tt(V, twv, pim, sre, ALU.mult, ["P", key_src], ["tw"])
                kb.tt(V, dim_, dim_, twv, ALU.add, [key_dst, "tw"], [key_dst])

            cprod(CM, 0, Ct, "Ct", "CM")
            cprod(WP, 1, Bb, "Bb", "WP")
            cprod(WE, 2, Bb, "Bb", "WE")
            kb.ts(V, CM[:, 1], CM[:, 1], -1.0, None, ALU.mult, None, ["CM"], ["CM"])
            kb.cp("pool", OUT[:, :, 2, :], CM[:, 0], ["CM"], ["OUT"])
            kb.cp("pool", OUT[:, :, 3, :], CM[:, 1], ["CM"], ["OUT"])
            for ri in range(2):
                for q in range(2):
                    pt = pT[(ri * 2 + q) % 2]
                    ptk = "pT%d" % ((ri * 2 + q) % 2)
                    for i4 in range(4):
                        gl = q * 4 + i4
                        kb.tr(pt[:, i4 * 128:(i4 + 1) * 128], WE[:, ri, gl, :], idf[:], ["WE", "idf_pg"], [ptk])
                    kb.cp("act", OUT[:, q * 4:(q + 1) * 4, ri, :], pt[:, :].rearrange("p (g c) -> p g c", g=4), [ptk], ["OUT"])
            for gh in range(2):
                for q in range(2):
                    pm = pM[(gh * 2 + q) % 2]
                    pmk = "pM%d" % ((gh * 2 + q) % 2)
                    for i4 in range(4):
                        gl = q * 4 + i4
                        lo, hi = gh * 64, (gh + 1) * 64
                        kb.mm(pm[:, i4 * 128:(i4 + 1) * 128], WP[lo:hi, 0, gl, :], CM[lo:hi, 0, gl, :], True, False, ["WP", "CM"], [pmk])
                        kb.mm(pm[:, i4 * 128:(i4 + 1) * 128], WP[lo:hi, 1, gl, :], CM[lo:hi, 1, gl, :], False, True, ["WP", "CM"], [pmk])
                    mk = fview(msk, d * 128, [(0, 4), (1, 128)])
                    kb.tt(V, OUT[:, q * 4:(q + 1) * 4, 4 + gh, :], pm[:, :].rearrange("p (g c) -> p g c", g=4), mk, ALU.mult, [pmk, "msk"], ["OUT"])
            kb.dma("pool", SW[j, d, blk], OUT[:], ["OUT"], ["SW%d_%d_%d" % (j, d, blk)])
            yield
    kb.dma("pool", A8D[j], A8T[:].rearrange("p a b c d e g -> p (a b c d e g)"), ["A8T"], ["A8D%d" % j])
    yield


def bcast_rows(ap2d, n):
    return ap2d.broadcast_to([n, ap2d.shape[1]])


def s5_layer(kb, i, j, io, scr, ctx_out):
    cfg, f = kb.cfg, kb.f
    GL = 8
    NC, NCL, NCC, L = cfg.NC, cfg.NCL, cfg.NCC, cfg.L
    XR, MOD, ZS, SW = scr["XR"], scr["MOD"], scr["ZS"], scr["SW"]
    with ExitStack() as es:
        sb = lambda n, s, dt=F32: kb.sb(es, n, s, dt)
        A8T = sb("A8T", [128, cfg.NB, 2, 5, 2, 2, GL])
        kb.dma("sp", A8T[:].rearrange("p a b c d e g -> p (a b c d e g)"), scr["A8D"][j], [], ["A8T"])
        idb = sb("idb", [128, 128], BF16)
        idf = sb("idf2", [128, 128])
        kb.dma("sp", idf[:], io["ident"], [], ["idf"])
        kb.cp("dve", idb[:], idf[:], ["idf"], ["idb"])
        ctiles = [(t * 1024, 128, t * 128, 0) for t in range(L // 1024)] + [(L, NCC, NCL, 1)]
        nct = len(ctiles)
        W8 = sb("W8", [128, 2, GL, 6, 128], BF16)
        mod4 = sb("mod4", [128, 4, 256]); dsk = sb("dsk", [128, 256])
        U32 = [sb("U32_%d" % c, [128, 8, 256]) for c in range(nct)]
        U8g = [sb("U8g%d" % c, [128, 16, 128], BF16) for c in range(2)]
        X8 = sb("X8", [128, 16, NC], BF16)
        Z = sb("Z", [128, 2, 2, GL, NC])
        NMAX_ = max(NCL, NCC) // 2
        tq1 = [sb("tq1_%d" % d_, [128, 2, GL, NMAX_]) for d_ in range(2)]
        tq2 = [sb("tq2_%d" % d_, [128, 2, GL, NMAX_]) for d_ in range(2)]
        Sbf = sb("Sbf", [128, 2, 2, GL, NC], BF16)
        Y8s = sb("Y8s", [128, 16, NC], BF16)
        ytm = [sb("ytm%d" % c, [128, 8, 256]) for c in range(2)]
        zt = [sb("zt%d" % c, [128, 8, 256], BF16) for c in range(2)]
        pX = [kb.ps(es, "pX%d" % c, [128, 512], BF16) for c in range(2)]
        pZ = [kb.ps(es, "pZ%d" % c, [128, 512]) for c in range(4)]
        pY = [kb.ps(es, "pY%d" % c, [128, 512]) for c in range(2)]
        DS, RS = 2 * GL * NC, GL * NC

        def cf(k):
            return NCL + k if k < NCC else k - NCC

        def cb(k):
            return NCL + (NCC - 1 - k) if k < NCC else NCL - 1 - (k - NCC)

        rot = 0
        for blk in range(cfg.NB):
            c0 = blk * 256
            for q, row in enumerate((1, 0, 7, 6)):
                kb.dma("sp", mod4[:, q, :], bcast_rows(MOD[i, row:row + 1, c0:c0 + 256], 128), [], ["mod4"])
            kb.dma("sp", dsk[:], bcast_rows(io["s5_d"][j, 0:1, c0:c0 + 256], 128), [], ["dsk"])
            for d in range(2):
                kb.dma("sp", W8[:, d], SW[j, d, blk], [], ["W8"])
            for ci, (row0, n, col0, isctx) in enumerate(ctiles):
                uk = "U32_%d" % ci
                kb.dma("sp", U32[ci][0:n], XR[row0:row0 + n * 8, c0:c0 + 256].rearrange("(c s) j -> c s j", s=8), [], [uk])
                scb = fview(mod4, (2 * isctx) * 256, [(0, 8), (1, 256)], pn=n)
                shb = fview(mod4, (2 * isctx + 1) * 256, [(0, 8), (1, 256)], pn=n)
                kb.tt("dve", U32[ci][0:n], U32[ci][0:n], scb, ALU.mult, [uk, "mod4"], [uk])
                kb.tt("dve", U32[ci][0:n], U32[ci][0:n], shb, ALU.add, [uk, "mod4"], [uk])
                ug = U8g[ci % 2]; ugk = "U8g%d" % (ci % 2)
                kb.cp("act", fview(ug, 0, [(16, 8), (128, 16), (1, 16)], pn=n),
                      fview(U32[ci], 0, [(256, 8), (16, 16), (1, 16)], pn=n), [uk], [ugk])
                for g4 in range(4):
                    px = pX[rot % 2]; pxk = "pX%d" % (rot % 2); rot += 1
                    for q in range(4):
                        g = g4 * 4 + q
                        kb.tr(px[:, q * 128:q * 128 + n], ug[0:n, g, :], idb[0:n, 0:n], [ugk, "idb"], [pxk])
                    kb.cp("dve" if g4 % 2 else "act", X8[:, g4 * 4:(g4 + 1) * 4, col0:col0 + n],
                          px[:, :].rearrange("p (g c) -> p g c", g=4)[:, :, 0:n], [pxk], ["X8"])
            zr = 0
            for d in range(2):
                for gl in range(GL):
                    pr = pZ[zr % 4]; prk = "pZ%d" % (zr % 4); zr += 1
                    pi = pZ[zr % 4]; pik = "pZ%d" % (zr % 4); zr += 1
                    for gh in range(2):
                        g = gh * 8 + gl
                        lo, hi = gh * 64, (gh + 1) * 64
                        kb.mm(pr[lo:hi, 0:NC], W8[:, d, gl, 0, lo:hi], X8[:, g, :], True, True, ["W8", "X8"], [prk])
                        kb.mm(pi[lo:hi, 0:NC], W8[:, d, gl, 1, lo:hi], X8[:, g, :], True, True, ["W8", "X8"], [pik])
                    kb.cp("act", Z[:, d, 0, gl, :], pr[:, 0:NC], [prk], ["Z%d" % d])
                    kb.cp("act" if d else "dve", Z[:, d, 1, gl, :], pi[:, 0:NC], [pik], ["Z%d" % d])
            NMAX = max(NCL, NCC) // 2
            ML = 4
            for d, RE in ((0, "dve"), (1, "pool")):
                zk, k1, k2 = "Z%d" % d, "t1_%d" % d, "t2_%d" % d
                segs = [(NCL, 1, NCC), (0, 1, NCL)] if d == 0 else [(NC - 1, -1, NCC), (NCL - 1, -1, NCL)]

                def cacc(dc0, dst_, sc0, sst, n, lvl, d=d, RE=RE, zk=zk, k1=k1, k2=k2):
                    if n <= 0:
                        return
                    ab = (((blk * 2 + d) * 5 + lvl) * 2) * 2 * GL
                    dst = fview(Z, d * DS + dc0, [(RS, 2), (NC, GL), (dst_, n)])
                    src = fview(Z, d * DS + sc0, [(RS, 2), (NC, GL), (sst, n)])
                    srw = fview(Z, d * DS + sc0 + RS, [(-RS, 2), (NC, GL), (sst, n)])
                    AR_ = fview(A8T, ab, [(GL, 2), (1, GL), (0, n)])
                    AI_ = fview(A8T, ab + 2 * GL, [(GL, 2), (1, GL), (0, n)])
                    T1 = fview(tq1[d], 0, [(GL * NMAX, 2), (NMAX, GL), (1, n)])
                    T2 = fview(tq2[d], 0, [(GL * NMAX, 2), (NMAX, GL), (1, n)])
                    kb.tt(RE, T1, src, AR_, ALU.mult, [zk, "A8T"], [k1])
                    kb.tt(RE, T2, srw, AI_, ALU.mult, [zk, "A8T"], [k2])
                    kb.tt(RE, T1, T1, T2, ALU.add, [k1, k2], [k1])
                    kb.tt(RE, dst, dst, T1, ALU.add, [zk, k1], [zk])

                def col(p, segs=segs):
                    (c0a, sga, la), (c0b, sgb, lb) = segs
                    return c0a + sga * p if p < la else c0b + sgb * (p - la)

                for lvl in range(ML):
                    h = 1 << lvl
                    for (sc0_, sg_, ln_) in segs:
                        cacc(sc0_ + sg_ * (2 * h - 1), sg_ * 2 * h, sc0_ + sg_ * (h - 1), sg_ * 2 * h, ln_ // (2 * h), lvl)
                BS = 1 << ML
                for q_ in range(1, NC // BS):
                    cacc(col(q_ * BS + BS - 1), 1, col(q_ * BS - 1), 1, 1, ML)
                for lvl in range(ML - 1, -1, -1):
                    h = 1 << lvl
                    (c0a, sga, la), (c0b, sgb, lb) = segs
                    cacc(c0a + sga * (2 * h + h - 1), sga * 2 * h, c0a + sga * (2 * h - 1), sga * 2 * h, la // (2 * h) - 1, lvl)
                    cacc(c0b + sgb * (h - 1), 1, c0a + sga * (la - 1), 1, 1, lvl)
                    cacc(c0b + sgb * (2 * h + h - 1), sgb * 2 * h, c0b + sgb * (2 * h - 1), sgb * 2 * h, lb // (2 * h) - 1, lvl)
            kb.cp("act", Sbf[:], Z[:], ["Z0", "Z1"], ["Sbf"])
            fw_p = [(1, NCL, 0), (0, 1, NC - 1), (NCL + 1, NC, NCL)]
            bw_p = [(NCL, NC - 1, NCL + 1), (NCL - 1, NCL, NCL), (0, NCL - 1, 1)]
            for gh in range(2):
                lo, hi = gh * 64, (gh + 1) * 64
                for gl in range(GL):
                    g = gh * 8 + gl
                    py = pY[g % 2]; pyk = "pY%d" % (g % 2)
                    kb.mm(py[:, 0:NC], W8[:, 0, gl, 4 + gh, :], X8[:, g, :], True, False, ["W8", "X8"], [pyk])
                    kb.mm(py[:, 0:NC], W8[:, 1, gl, 4 + gh, :], X8[:, g, :], False, False, ["W8", "X8"], [pyk])
                    pieces = [(0, p) for p in fw_p] + [(1, p) for p in bw_p]
                    pieces = [(d, p) for d, p in pieces if p[1] > p[0]]
                    for pi_, (d, (o0, o1, s0)) in enumerate(pieces):
                        for ri in range(2):
                            last = (pi_ == len(pieces) - 1) and ri == 1
                            kb.mm(py[:, o0:o1], W8[lo:hi, d, gl, 2 + ri, :], Sbf[lo:hi, d, ri, gl, s0:s0 + (o1 - o0)],
                                  False, last, ["W8", "Sbf"], [pyk])
                    kb.cp("act" if g % 2 else "dve", Y8s[:, g, :], py[:, 0:NC], [pyk], ["Y8s"])
            for ci, (row0, n, col0, isctx) in enumerate(ctiles):
                if isctx and not ctx_out:
                    continue
                uk = "U32_%d" % ci
                yt = ytm[ci % 2]; ytk = "ytm%d" % (ci % 2)
                for g4 in range(4):
                    px = pX[rot % 2]; pxk = "pX%d" % (rot % 2); rot += 1
                    for q in range(4):
                        g = g4 * 4 + q
                        kb.tr(px[0:n, q * 128:(q + 1) * 128], Y8s[:, g, col0:col0 + n], idb[:], ["Y8s", "idb"], [pxk])
                    kb.cp("act" if g4 % 2 else "dve", fview(yt, g4 * 64, [(16, 4), (256, 8), (1, 16)], pn=n),
                          fview(px, 0, [(128, 4), (16, 8), (1, 16)], pn=n), [pxk], [ytk])
                dkb = fview(dsk, 0, [(0, 8), (1, 256)], pn=n)
                kb.tt("dve", U32[ci][0:n], U32[ci][0:n], dkb, ALU.mult, [uk, "dsk"], [uk])
                kb.tt("dve", yt[0:n], yt[0:n], U32[ci][0:n], ALU.add, [ytk, uk], [ytk])
                z_ = zt[ci % 2]; zk = "zt%d" % (ci % 2)
                kb.act(z_[0:n], yt[0:n], AF.Gelu_apprx_tanh, [ytk], [zk])
                kb.dma("pool", ZS[row0:row0 + n * 8, c0:c0 + 256].rearrange("(c s) j -> c s j", s=8), z_[0:n], [zk], ["ZS"])
    f.barrier()


def adaln_gen(kb, es, io, scr):
    cfg, f = kb.cfg, kb.f
    D, NK, NT = cfg.D, cfg.NK, cfg.NT
    MOD = scr["MOD"]
    if True:
        sb = lambda n, s, dt=F32: kb.sb(es, n, s, dt)
        cond = sb("cond", [128, NK, 2]); cs = sb("cs", [128, NK, 2])
        kb.dma("sp", cond[:], io["cond"], [], ["cond"])
        kb.act(cs[:], cond[:], AF.Silu, ["cond"], ["cs"])
        aw = [sb("aw%d" % q, [128, NK, NT]) for q in range(2)]
        ab = [sb("ab%d" % q, [2, NT]) for q in range(2)]
        res = [sb("res%d" % q, [2, NT]) for q in range(2)]
        pa = [kb.ps(es, "pa%d" % q, [128, 512]) for q in range(2)]
        it = 0
        for i in range(cfg.DEPTH):
            for n in range(6 * D // NT):
                q = it % 2; it += 1
                kb.dma("sp", aw[q][:], io["ada_w"][i, :, n * NT:(n + 1) * NT].rearrange("(k p) c -> p k c", p=128), [], ["aw%d" % q])
                kb.dma("sp", ab[q][:], bcast_rows(io["ada_b"][i, 0:1, n * NT:(n + 1) * NT], 2), [], ["ab%d" % q])
                for k in range(NK):
                    kb.mm(pa[q][0:2, 0:NT], cs[:, k, :], aw[q][:, k, :], k == 0, k == NK - 1, ["cs", "aw%d" % q], ["pa%d" % q])
                v = (n * NT) // D
                off = n * NT - v * D
                kb.f.op("pool", lambda e, o=res[q][:], a_=ab[q][:]: e.tensor_copy(out=o, in_=a_), ["ab%d" % q], ["res%d" % q]) if False else None
                kb.stt(res[q][:], pa[q][0:2, 0:NT], 1.0 if v in (1, 4) else 0.0, ab[q][:], ALU.add, ALU.add,
                       ["pa%d" % q, "ab%d" % q], ["res%d" % q])
                kb.dma("pool", MOD[i].rearrange("(w v) d -> w v d", w=2)[:, v, off:off + NT], res[q][:], ["res%d" % q], ["MOD"])
                yield


def prologue(kb, io, scr):
    cfg, f = kb.cfg, kb.f
    with ExitStack() as es_a:
        A8T = kb.sb(es_a, "A8Tg", [128, cfg.NB, 2, 5, 2, 2, 8])
        ga = adaln_gen(kb, es_a, io, scr)
        a_done = False
        try:
            next(ga)
        except StopIteration:
            a_done = True
        n_a = cfg.DEPTH * 6 * cfg.D // cfg.NT
        n_p = cfg.NS5 * (2 * (cfg.NB + 2) + 1)
        per = max(1, -(-n_a // max(1, n_p)))
        for j in range(cfg.NS5):
            with ExitStack() as es_j:
                for _ in s5_paramgen_gen(kb, es_j, j, io, scr["SW"], A8T, scr["A8D"]):
                    for _k in range(per):
                        if a_done:
                            break
                        try:
                            next(ga)
                        except StopIteration:
                            a_done = True
        if not a_done:
            for _ in ga:
                pass
    f.barrier()


def ln_tile(kb, eng2, t, tk, mv, st, sd, g_bc, b_bc, n=128):
    cfg = kb.cfg
    D = cfg.D
    nch = max(1, D // 512)
    w = D // nch
    for c in range(nch):
        kb.f.op("dve", lambda e, o=st[0:n, c, :], i_=t[0:n, c * w:(c + 1) * w]: e.bn_stats(out=o, in_=i_), [tk], [tk + "_st"])
    kb.f.op("dve", lambda e, o=mv[0:n, :], i_=st[0:n, :, :].rearrange("p c s -> p (c s)"): e.bn_aggr(out=o, in_=i_), [tk + "_st"], [tk + "_mv"])
    kb.ts("dve", sd[0:n, 0:1], mv[0:n, 1:2], LN_EPS, None, ALU.add, None, [tk + "_mv"], [tk + "_sd"])
    kb.act(sd[0:n, 0:1], sd[0:n, 0:1], AF.Sqrt, [tk + "_sd"], [tk + "_sd"])
    kb.f.op("dve", lambda e, o=sd[0:n, 1:2], i_=sd[0:n, 0:1]: e.reciprocal(out=o, in_=i_), [tk + "_sd"], [tk + "_sd"])
    kb.stt(sd[0:n, 2:3], mv[0:n, 0:1], -1.0, sd[0:n, 1:2], ALU.mult, ALU.mult, [tk + "_mv", tk + "_sd"], [tk + "_sd"])
    kb.act(t[0:n, :], t[0:n, :], AF.Identity, [tk, tk + "_sd"], [tk], scale=sd[0:n, 1:2], bias=sd[0:n, 2:3])
    kb.tt("dve", t[0:n, :], t[0:n, :], g_bc[0:n, :], ALU.mult, [tk, "lng"], [tk])
    kb.tt(eng2, t[0:n, :], t[0:n, :], b_bc[0:n, :], ALU.add, [tk, "lnb"], [tk])


def postnorm(kb, i, which, io, scr, with_ctx, mod_rows, dst_u, final_out=None):
    cfg, f = kb.cfg, kb.f
    D = cfg.D
    XR, MOD, TS = scr["XR"], scr["MOD"], scr["TS"]
    ntiles = cfg.NTT if with_ctx else cfg.NLT
    with ExitStack() as es:
        sb = lambda n, s, dt=F32: kb.sb(es, n, s, dt)
        lng = sb("lng", [128, D]); lnb = sb("lnb", [128, D])
        kb.dma("sp", lng[:], bcast_rows(io["ln_g"][2 * i + which:2 * i + which + 1, :], 128), [], ["lng"])
        kb.dma("sp", lnb[:], bcast_rows(io["ln_b"][2 * i + which:2 * i + which + 1, :], 128), [], ["lnb"])
        mods = None
        if mod_rows is not None:
            mods = [[sb("m%d_%d" % (a, b_), [128, D]) for b_ in range(2)] for a in range(2 if with_ctx else 1)]
            for a in range(len(mods)):
                for b_ in range(2):
                    kb.dma("sp", mods[a][b_][:], bcast_rows(MOD[i, 6 * a + mod_rows[b_]:6 * a + mod_rows[b_] + 1, :], 128), [], ["mods"])
        tb = [sb("tb%d" % q, [128, D]) for q in range(3)]
        ub = [sb("ub%d" % q, [128, D], BF16) for q in range(3)]
        st = [sb("st%d" % q, [128, max(1, D // 512), 6]) for q in range(3)]
        mv = [sb("mv%d" % q, [128, 2]) for q in range(3)]
        sd = [sb("sd%d" % q, [128, 4]) for q in range(3)]
        for tt in range(ntiles):
            q = tt % 3
            tk = "tb%d" % q
            rows = slice(tt * 128, (tt + 1) * 128)
            isctx = 1 if tt >= cfg.NLT else 0
            kb.dma("sp", tb[q][:], TS[rows, :], ["TS"], [tk])
            ln_tile(kb, "pool", tb[q], tk, mv[q], st[q], sd[q], lng, lnb)
            if final_out is not None:
                kb.out_stores.append(kb.dma("pool", final_out[rows, :], tb[q][:], [tk], ["OUTF"]))
            else:
                kb.dma("pool", XR[rows, :], tb[q][:], [tk], ["XR"])
            if mods is not None:
                m = mods[isctx]
                kb.tt("dve", tb[q][:], tb[q][:], m[0][:], ALU.mult, [tk, "mods"], [tk])
                kb.tt("dve", ub[q][:], tb[q][:], m[1][:], ALU.add, [tk, "mods"], ["ub%d" % q])
                kb.dma("pool", dst_u[rows, :], ub[q][:], ["ub%d" % q], ["U2"])
    f.barrier()


def glu_stage(kb, i, j, io, scr, with_ctx):
    cfg, f = kb.cfg, kb.f
    D, NK, NT = cfg.D, cfg.NK, cfg.NT
    XR, MOD, TS, ZS = scr["XR"], scr["MOD"], scr["TS"], scr["ZS"]
    ntiles = cfg.NTT if with_ctx else cfg.NLT
    with ExitStack() as es:
        sb = lambda n, s, dt=F32: kb.sb(es, n, s, dt)
        idb = sb("idb", [128, 128], BF16); idf = sb("idf", [128, 128])
        kb.dma("sp", idf[:], io["ident"], [], ["idf"])
        kb.cp("dve", idb[:], idf[:], ["idf"], ["idb"])
        zT = sb("zT", [128, NK, ntiles * 128], BF16)
        zin = [sb("zin%d" % q, [128, D], BF16) for q in range(2)]
        pT = [kb.ps(es, "pT%d" % q, [128, 512], BF16) for q in range(2)]
        r = 0
        for tt in range(ntiles):
            q = tt % 2
            kb.dma("sp", zin[q][:], ZS[tt * 128:(tt + 1) * 128, :], ["ZS"], ["zin%d" % q])
            for k4 in range(0, NK, 4):
                nk = min(4, NK - k4)
                p = pT[r % 2]; pk = "pT%d" % (r % 2); r += 1
                for a in range(nk):
                    kb.tr(p[:, a * 128:(a + 1) * 128], zin[q][:, (k4 + a) * 128:(k4 + a + 1) * 128], idb[:], ["zin%d" % q, "idb"], [pk])
                kb.cp("act" if (r % 2) else "dve", zT[:, k4:k4 + nk, tt * 128:(tt + 1) * 128],
                      p[:, 0:nk * 128].rearrange("p (a c) -> p a c", a=nk), [pk], ["zT"])
        KH = max(1, NK // 2)
        wst = [sb("wst%d" % q, [128, KH, NT]) for q in range(2)]
        wb = [sb("wb%d" % q, [128, NK, NT], BF16) for q in range(2)]
        bvg = sb("bvg", [128, 2, NT]); g1t = sb("g1t", [128, 2, NT])
        xs = [sb("xs%d" % q, [128, NT]) for q in range(2)]
        sg = [sb("sg%d" % q, [128, NT]) for q in range(2)]
        vv = [sb("vv%d" % q, [128, NT]) for q in range(2)]
        pv = [kb.ps(es, "pv%d" % q, [128, 512]) for q in range(2)]
        pg = [kb.ps(es, "pg%d" % q, [128, 512]) for q in range(2)]
        ws = 0
        for np_ in range(D // NT):
            cs_ = slice(np_ * NT, (np_ + 1) * NT)
            for vg in range(2):
                col0 = vg * D + np_ * NT
                for kh in range(0, NK, KH):
                    w_ = wst[ws % 2]; wk = "wst%d" % (ws % 2); ws += 1
                    kb.dma("sp", w_[:], io["s5_w_glu"][j, kh * 128:(kh + KH) * 128, col0:col0 + NT].rearrange("(k p) c -> p k c", p=128), [], [wk])
                    kb.cp("act", wb[vg][:, kh:kh + KH, :], w_[:], [wk], ["wb%d" % vg])
                kb.dma("sp", bvg[:, vg, :], bcast_rows(io["s5_b_glu"][j, 0:1, col0:col0 + NT], 128), [], ["bvg"])
            for a in range(2 if with_ctx else 1):
                kb.dma("sp", g1t[:, a, :], bcast_rows(MOD[i, 6 * a + 2:6 * a + 3, cs_], 128), [], ["g1t"])
            for tt in range(ntiles):
                q = tt % 2
                isctx = 1 if tt >= cfg.NLT else 0
                rows = slice(tt * 128, (tt + 1) * 128)
                kb.dma("sp", xs[q][:], XR[rows, cs_], [], ["xs%d" % q])
                for k in range(NK):
                    kb.mm(pv[q][:, 0:NT], zT[:, k, rows], wb[0][:, k, :], k == 0, k == NK - 1, ["zT", "wb0"], ["pv%d" % q])
                for k in range(NK):
                    kb.mm(pg[q][:, 0:NT], zT[:, k, rows], wb[1][:, k, :], k == 0, k == NK - 1, ["zT", "wb1"], ["pg%d" % q])
                kb.tt("dve", sg[q][:], pg[q][:, 0:NT], bvg[:, 1, :], ALU.add, ["pg%d" % q, "bvg"], ["sg%d" % q])
                kb.act(sg[q][:], sg[q][:], AF.Sigmoid, ["sg%d" % q], ["sg%d" % q])
                kb.tt("dve", vv[q][:], pv[q][:, 0:NT], bvg[:, 0, :], ALU.add, ["pv%d" % q, "bvg"], ["vv%d" % q])
                kb.tt("pool", vv[q][:], vv[q][:], sg[q][:], ALU.mult, ["vv%d" % q, "sg%d" % q], ["vv%d" % q])
                kb.tt("pool", vv[q][:], vv[q][:], g1t[:, isctx, :], ALU.mult, ["vv%d" % q, "g1t"], ["vv%d" % q])
                kb.stt(vv[q][:], xs[q][:], cfg.ALPHA, vv[q][:], ALU.mult, ALU.add, ["xs%d" % q, "vv%d" % q], ["vv%d" % q])
                kb.dma("pool", TS[rows, cs_], vv[q][:], ["vv%d" % q], ["TS"])
    f.barrier()


def conv_layer(kb, i, j, io, scr, with_ctx):
    cfg, f = kb.cfg, kb.f
    D, NK, NT, L, LC, GW = cfg.D, cfg.NK, cfg.NT, cfg.L, cfg.LC, cfg.GW
    XR, MOD, TS, CV = scr["XR"], scr["MOD"], scr["TS"], scr["CV"]
    ntiles = cfg.NTT if with_ctx else cfg.NLT
    NLT = cfg.NLT
    NTOK = ntiles * 128
    with ExitStack() as esl:
        sbl = lambda n, s, dt=F32: kb.sb(esl, n, s, dt)
        idf = sbl("idf", [128, 128]); idb = sbl("idb", [128, 128], BF16)
        kb.dma("sp", idf[:], io["ident"], [], ["idf"])
        kb.cp("dve", idb[:], idf[:], ["idf"], ["idb"])
        uT = sbl("uT", [128, NK, NTOK], BF16)
        with ExitStack() as es:
            sb = lambda n, s, dt=F32: kb.sb(es, n, s, dt)
            mods = [[sb("m%d_%d" % (a, b_), [128, D]) for b_ in range(2)] for a in range(2 if with_ctx else 1)]
            for a in range(len(mods)):
                for b_, row in enumerate((1, 0)):
                    kb.dma("sp", mods[a][b_][:], bcast_rows(MOD[i, 6 * a + row:6 * a + row + 1, :], 128), [], ["mods"])
            xt = [sb("xt%d" % q, [128, D]) for q in range(2)]
            ub = [sb("ub%d" % q, [128, D], BF16) for q in range(2)]
            pT = [kb.ps(es, "pT%d" % q, [128, 512], BF16) for q in range(2)]
            r = 0
            for tt in range(ntiles):
                q = tt % 2
                isctx = 1 if tt >= NLT else 0
                kb.dma("sp", xt[q][:], XR[tt * 128:(tt + 1) * 128, :], [], ["xt%d" % q])
                kb.tt("dve", xt[q][:], xt[q][:], mods[isctx][0][:], ALU.mult, ["xt%d" % q, "mods"], ["xt%d" % q])
                kb.tt("pool", ub[q][:], xt[q][:], mods[isctx][1][:], ALU.add, ["xt%d" % q, "mods"], ["ub%d" % q])
                for k4 in range(0, NK, 4):
                    nk = min(4, NK - k4)
                    p = pT[r % 2]; pk = "pT%d" % (r % 2); r += 1
                    for a in range(nk):
                        kb.tr(p[:, a * 128:(a + 1) * 128], ub[q][:, (k4 + a) * 128:(k4 + a + 1) * 128], idb[:], ["ub%d" % q, "idb"], [pk])
                    kb.cp("act" if (r % 2) else "dve", uT[:, k4:k4 + nk, tt * 128:(tt + 1) * 128],
                          p[:, 0:nk * 128].rearrange("p (a c) -> p a c", a=nk), [pk], ["uT"])
        f.barrier()
        with ExitStack() as es:
            sb = lambda n, s, dt=F32: kb.sb(es, n, s, dt)
            rows = L // GW
            WL = (rows + 30) * GW
            hl = sb("hl", [128, WL], BF16)
            hc = sb("hc", [128, LC + 30], BF16)
            kb.memset("pool", hl[:], 0.0, [], ["hl"])
            kb.memset("pool", hc[:], 0.0, [], ["hc"])
            bp = sb("bp", [128, 2 * NK]); wdw = sb("wdw", [128, NK, 31]); bdw = sb("bdw", [128, NK])
            kb.dma("sp", bp[:], io["cv_b_pw1"][j], [], ["bp"])
            kb.dma("sp", wdw[:], io["cv_w_dw"][j], [], ["wdw"])
            kb.dma("sp", bdw[:], io["cv_b_dw"][j], [], ["bdw"])
            DG = [sb("DG%d" % q, [128, 31, 128], BF16) for q in range(2)]
            wst = [sb("wst%d" % q, [128, NK, 128]) for q in range(2)]
            wab = [[sb("wab%d_%d" % (q, a), [128, NK, 128], BF16) for a in range(2)] for q in range(2)]
            sig = [sb("sig%d" % q, [128, NT]) for q in range(2)]
            cvs = [sb("cvs%d" % q, [128, NT]) for q in range(2)]
            cvt = [sb("cvt%d" % q, [128, NT // 128, 128]) for q in range(2)]
            pa = [kb.ps(es, "pa%d" % q, [128, 512]) for q in range(2)]
            pg = [kb.ps(es, "pg%d" % q, [128, 512]) for q in range(2)]
            pc = [kb.ps(es, "pc%d" % q, [128, 512]) for q in range(2)]
            pt = [kb.ps(es, "pt%d" % q, [128, 512]) for q in range(2)]
            blocks = [(tb * NT, NT, 0, 15 * GW + tb * NT) for tb in range(L // NT)]
            if with_ctx:
                blocks += [(L + tb * NT, min(NT, LC - tb * NT), 1, 15 + tb * NT) for tb in range((LC + NT - 1) // NT)]
            ws = 0; it = 0
            for cc in range(NK):
                cq = cc % 2
                for a in range(2):
                    w_ = wst[ws % 2]; wk = "wst%d" % (ws % 2); ws += 1
                    c0 = a * D + cc * 128
                    kb.dma("sp", w_[:], io["cv_w_pw1"][j, :, c0:c0 + 128].rearrange("(k p) c -> p k c", p=128), [], [wk])
                    kb.cp("act", wab[cq][a][:], w_[:], [wk], ["wab%d_%d" % (cq, a)])
                for k in range(31):
                    kb.ts("dve" if k % 2 else "pool", DG[cq][:, k, :], idf[:], wdw[:, cc, k:k + 1], None, ALU.mult, None, ["idf", "wdw"], ["DG%d" % cq])
                for (t0, n, isctx, hoff) in blocks:
                    q = it % 2; it += 1
                    for k in range(NK):
                        kb.mm(pa[q][:, 0:n], wab[cq][0][:, k, :], uT[:, k, t0:t0 + n], k == 0, k == NK - 1, ["wab%d_0" % cq, "uT"], ["pa%d" % q])
                    for k in range(NK):
                        kb.mm(pg[q][:, 0:n], wab[cq][1][:, k, :], uT[:, k, t0:t0 + n], k == 0, k == NK - 1, ["wab%d_1" % cq, "uT"], ["pg%d" % q])
                    kb.act(sig[q][:, 0:n], pg[q][:, 0:n], AF.Sigmoid, ["pg%d" % q, "bp"], ["sig%d" % q], bias=bp[:, NK + cc:NK + cc + 1])
                    hbuf, hk = (hc, "hc") if isctx else (hl, "hl")
                    kb.stt(hbuf[:, hoff:hoff + n], pa[q][:, 0:n], bp[:, cc:cc + 1], sig[q][:, 0:n], ALU.add, ALU.mult, ["pa%d" % q, "bp", "sig%d" % q], [hk])
                for (t0, n, isctx, hoff) in blocks:
                    q = it % 2; it += 1
                    hbuf, hk = (hc, "hc") if isctx else (hl, "hl")
                    step = 1 if isctx else GW
                    base = (hoff - 15) if isctx else (hoff - 15 * GW)
                    for k in range(31):
                        kb.mm(pc[q][:, 0:n], DG[cq][:, k, :], hbuf[:, base + k * step:base + k * step + n], k == 0, k == 30, ["DG%d" % cq, hk], ["pc%d" % q])
                    kb.act(cvs[q][:, 0:n], pc[q][:, 0:n], AF.Identity, ["pc%d" % q, "bdw"], ["cvs%d" % q], bias=bdw[:, cc:cc + 1])
                    na = n // 128
                    for a in range(na):
                        kb.tr(pt[q][:, a * 128:(a + 1) * 128], cvs[q][:, a * 128:(a + 1) * 128], idf[:], ["cvs%d" % q, "idf"], ["pt%d" % q])
                    kb.cp("dve", cvt[q][:, 0:na, :], pt[q][:, 0:n].rearrange("p (a c) -> p a c", a=na), ["pt%d" % q], ["cvt%d" % q])
                    kb.dma("pool", CV[t0:t0 + n, cc * 128:(cc + 1) * 128].rearrange("(a p) c -> p a c", p=128), cvt[q][:, 0:na, :], ["cvt%d" % q], ["CV"])
        f.barrier()
    with ExitStack() as es:
        sb = lambda n, s, dt=F32: kb.sb(es, n, s, dt)
        idf = sb("idf", [128, 128]); idb = sb("idb", [128, 128], BF16)
        kb.dma("sp", idf[:], io["ident"], [], ["idf"])
        kb.cp("dve", idb[:], idf[:], ["idf"], ["idb"])
        lng = sb("lng", [128, D]); lnb = sb("lnb", [128, D]); b2 = sb("b2", [128, D])
        kb.dma("sp", lng[:], bcast_rows(io["cv_ln_g"][j:j + 1, :], 128), [], ["lng"])
        kb.dma("sp", lnb[:], bcast_rows(io["cv_ln_b"][j:j + 1, :], 128), [], ["lnb"])
        kb.dma("sp", b2[:], bcast_rows(io["cv_b_pw2"][j, 0:1, :], 128), [], ["b2"])
        g1t = [sb("g1t%d" % a, [128, D]) for a in range(2 if with_ctx else 1)]
        for a in range(len(g1t)):
            kb.dma("sp", g1t[a][:], bcast_rows(MOD[i, 6 * a + 2:6 * a + 3, :], 128), [], ["g1t"])
        W2 = sb("W2", [128, NK, D], BF16)
        wst = [sb("wst%d" % q, [128, D]) for q in range(2)]
        for k in range(NK):
            kb.dma("sp", wst[k % 2][:], io["cv_w_pw2"][j, k * 128:(k + 1) * 128, :], [], ["wst%d" % (k % 2)])
            kb.cp("act", W2[:, k, :], wst[k % 2][:], ["wst%d" % (k % 2)], ["W2"])
        cvt = [sb("cvt%d" % q, [128, D]) for q in range(3)]
        sbf = [sb("sbf%d" % q, [128, D], BF16) for q in range(3)]
        sT = [sb("sT%d" % q, [128, NK, 128], BF16) for q in range(3)]
        xt = [sb("xt%d" % q, [128, D]) for q in range(2)]
        yt = [sb("yt%d" % q, [128, D]) for q in range(2)]
        st = [sb("st%d" % q, [128, max(1, D // 512), 6]) for q in range(3)]
        mv = [sb("mv%d" % q, [128, 2]) for q in range(3)]
        sd = [sb("sd%d" % q, [128, 4]) for q in range(3)]
        pT = [kb.ps(es, "pT%d" % q, [128, 512], BF16) for q in range(2)]
        po = [kb.ps(es, "po%d" % q, [128, 512]) for q in range(2)]
        r = 0; pr = 0
        for tt in range(ntiles):
            q = tt % 3
            q2 = tt % 2
            isctx = 1 if tt >= NLT else 0
            rows_ = slice(tt * 128, (tt + 1) * 128)
            kb.dma("sp", cvt[q][:], CV[rows_, :], ["CV"], ["cvt%d" % q])
            kb.dma("sp", xt[q2][:], XR[rows_, :], [], ["xt%d" % q2])
            ln_tile(kb, "pool", cvt[q], "cvt%d" % q, mv[q], st[q], sd[q], lng, lnb)
            kb.act(sbf[q][:], cvt[q][:], AF.Silu, ["cvt%d" % q], ["sbf%d" % q])
            for k4 in range(0, NK, 4):
                nk = min(4, NK - k4)
                p = pT[r % 2]; pk = "pT%d" % (r % 2); r += 1
                for a in range(nk):
                    kb.tr(p[:, a * 128:(a + 1) * 128], sbf[q][:, (k4 + a) * 128:(k4 + a + 1) * 128], idb[:], ["sbf%d" % q, "idb"], [pk])
                kb.cp("act" if (r % 2) else "dve", sT[q][:, k4:k4 + nk, :], p[:, 0:nk * 128].rearrange("p (a c) -> p a c", a=nk), [pk], ["sT%d" % q])
            for nt in range(D // NT):
                cs_ = slice(nt * NT, (nt + 1) * NT)
                pq = pr % 2; pr += 1
                for k in range(NK):
                    kb.mm(po[pq][:, 0:NT], sT[q][:, k, :], W2[:, k, cs_], k == 0, k == NK - 1, ["sT%d" % q, "W2"], ["po%d" % pq])
                kb.tt("dve", yt[q2][:, cs_], po[pq][:, 0:NT], b2[:, cs_], ALU.add, ["po%d" % pq, "b2"], ["yt%d" % q2])
            kb.tt("pool", yt[q2][:], yt[q2][:], g1t[isctx][:], ALU.mult, ["yt%d" % q2, "g1t"], ["yt%d" % q2])
            kb.stt(yt[q2][:], xt[q2][:], cfg.ALPHA, yt[q2][:], ALU.mult, ALU.add, ["xt%d" % q2, "yt%d" % q2], ["yt%d" % q2])
            kb.dma("pool", TS[rows_, :], yt[q2][:], ["yt%d" % q2], ["TS"])
    f.barrier()
    postnorm(kb, i, 0, io, scr, with_ctx, (4, 3), scr["U2"])


def declare_io(kb):
    cfg = kb.cfg
    D, L, LC, DEPTH, NK, NB = cfg.D, cfg.L, cfg.LC, cfg.DEPTH, cfg.NK, cfg.NB
    NS5, NCV, E, F = cfg.NS5, cfg.NCV, cfg.E, cfg.F
    io = {}
    io["x"] = kb.din("x", [L, D]); io["ctx"] = kb.din("ctx", [LC, D])
    io["cond"] = kb.din("cond", [128, NK, 2])
    io["ada_w"] = kb.din("ada_w", [DEPTH, D, 6 * D]); io["ada_b"] = kb.din("ada_b", [DEPTH, 1, 6 * D])
    io["ln_g"] = kb.din("ln_g", [DEPTH * 2, D]); io["ln_b"] = kb.din("ln_b", [DEPTH * 2, D])
    io["ident"] = kb.din("ident", [128, 128])
    io["iota_f"] = kb.din("iota_f", [128, cfg.TALL]); io["tokid"] = kb.din("tokid", [128, 18])
    io["sel"] = kb.din("sel", [16, 16, 128])
    io["shift"] = kb.din("shift", [cfg.CAPC, 128 // cfg.CAPC, 128])
    io["s5_kt"] = kb.din("s5_kt", [128, 2, 4, 2, 8]); io["s5_mask"] = kb.din("s5_mask", [128, 2, 128])
    io["s5_lam"] = kb.din("s5_lam", [NS5, 2, NB, 128, 3, 8])
    io["s5_B"] = kb.din("s5_B", [NS5, 2, NB, 128, 2, 8, 16]); io["s5_C"] = kb.din("s5_C", [NS5, 2, NB, 128, 2, 8, 16])
    io["s5_d"] = kb.din("s5_d", [NS5, 1, D])
    io["s5_w_glu"] = kb.din("s5_w_glu", [NS5, D, 2 * D]); io["s5_b_glu"] = kb.din("s5_b_glu", [NS5, 1, 2 * D])
    nc_ = max(NCV, 1)
    io["cv_w_pw1"] = kb.din("cv_w_pw1", [nc_, D, 2 * D]); io["cv_b_pw1"] = kb.din("cv_b_pw1", [nc_, 128, 2 * NK])
    io["cv_w_dw"] = kb.din("cv_w_dw", [nc_, 128, NK, 31]); io["cv_b_dw"] = kb.din("cv_b_dw", [nc_, 128, NK])
    io["cv_ln_g"] = kb.din("cv_ln_g", [nc_, D]); io["cv_ln_b"] = kb.din("cv_ln_b", [nc_, D])
    io["cv_w_pw2"] = kb.din("cv_w_pw2", [nc_, D, D]); io["cv_b_pw2"] = kb.din("cv_b_pw2", [nc_, 1, D])
    io["moe_w_router"] = kb.din("moe_w_router", [DEPTH, D, E])
    io["moe_w_in"] = kb.din("moe_w_in", [DEPTH, E, D, 2 * F]); io["moe_w_out"] = kb.din("moe_w_out", [DEPTH, E, F, D])
    io["out"] = kb.nc.dram_tensor("out", [L, D], F32, kind="ExternalOutput").ap()
    return io


def declare_scratch(kb):
    cfg = kb.cfg
    D, TALL = cfg.D, cfg.TALL
    scr = {}
    scr["XR"] = kb.dscr("XR", [TALL, D])
    scr["TS"] = kb.dscr("TS", [TALL, D])
    scr["MOD"] = kb.dscr("MOD", [cfg.DEPTH, 12, D])
    scr["ZS"] = kb.dscr("ZS", [TALL, D], BF16)
    scr["U2"] = kb.dscr("U2", [TALL, D], BF16)
    scr["CV"] = kb.dscr("CV", [TALL, D])
    scr["SW"] = kb.dscr("SW", [cfg.NS5, 2, cfg.NB, 128, 8, 6, 128], BF16)
    scr["A8D"] = kb.dscr("A8D", [cfg.NS5, 128, cfg.NB * 2 * 5 * 2 * 2 * 8])
    scr["YG"] = kb.dscr("YG", [cfg.E, cfg.CAPL + cfg.CAPC, D], BF16)
    return scr


def build(cfg, debug_outs=(), stop_after=None):
    kb = KB(cfg, debug_outs)
    io = declare_io(kb)
    scr = declare_scratch(kb)
    f = kb.f
    kb.dma("sp", scr["XR"][0:cfg.L, :], io["x"], [], ["XR"])
    kb.dma("sp", scr["XR"][cfg.L:cfg.TALL, :], io["ctx"], [], ["XR"])
    prologue(kb, io, scr)
    done = False
    for i in range(cfg.DEPTH):
        is_s5 = (i % 2) == 0
        j = i // 2
        ctx_out = any((k % 2) == 0 for k in range(i + 1, cfg.DEPTH))
        last = i == cfg.DEPTH - 1
        if is_s5:
            s5_layer(kb, i, j, io, scr, ctx_out)
            if stop_after == ("s5", i):
                break
            glu_stage(kb, i, j, io, scr, ctx_out)
            if stop_after == ("glu", i):
                break
            postnorm(kb, i, 0, io, scr, ctx_out, (4, 3), scr["U2"])
        else:
            conv_layer(kb, i, j, io, scr, ctx_out)
        if stop_after == ("mix", i):
            break
        moe_layer(kb, i, io, scr, ctx_out)
        if stop_after == ("moe", i):
            break
        postnorm(kb, i, 1, io, scr, ctx_out, None, None, final_out=io["out"] if last else None)
    if not kb.out_stores:
        with ExitStack() as es:
            t = kb.sb(es, "dbg", [128, cfg.D])
            kb.dma("sp", t[:], scr["XR"][0:128, :], ["XR"], ["dbg"])
            kb.out_stores.append(kb.dma("pool", io["out"][0:128, :], t[:], ["dbg"], ["OUTF"]))
    f.barrier()
    f.emit(final_wait_ops=kb.out_stores)
    return kb.nc


def host_consts(cfg):
    c = {}
    c["ident"] = np.eye(128, dtype=np.float32)
    c["iota_f"] = np.tile(np.arange(cfg.TALL, dtype=np.float32)[None, :], (128, 1))
    c["tokid"] = (np.arange(128, dtype=np.float32)[:, None] + 128.0 * np.arange(18, dtype=np.float32)[None, :]).astype(np.float32)
    sel = np.zeros((16, 16, 128), np.float32)
    for e_ in range(16):
        sel[e_, e_, :] = 1.0
    c["sel"] = sel
    ep = 128 // cfg.CAPC
    sh = np.zeros((cfg.CAPC, ep, 128), np.float32)
    for e4 in range(ep):
        for s_ in range(cfg.CAPC):
            sh[s_, e4, e4 * cfg.CAPC + s_] = 1.0
    c["shift"] = sh
    kt = np.zeros((2, 4, 8), np.float64)
    idx = np.arange(8)
    kt[0, 0] = idx + 1; kt[0, 1] = -(idx + 1); kt[0, 2] = 7 - idx
    kt[1, 0] = 8 - idx; kt[1, 1] = idx - 8; kt[1, 2] = idx
    kt[:, 3, 0] = 1; kt[:, 3, 1] = 8
    ktt = np.stack([kt, kt / (2.0 * math.pi)], axis=2)
    c["s5_kt"] = np.ascontiguousarray(np.broadcast_to(ktt[None], (128, 2, 4, 2, 8))).astype(np.float32)
    s_idx = np.arange(128) // 16
    mf = (s_idx[None, :] >= s_idx[:, None]).astype(np.float32)
    mb = (s_idx[None, :] <= s_idx[:, None]).astype(np.float32)
    c["s5_mask"] = np.ascontiguousarray(np.stack([mf, mb], axis=1))
    return c


def host_layout(cfg, inp, b):
    D, NK, NB, NS5, NCV = cfg.D, cfg.NK, cfg.NB, cfg.NS5, cfg.NCV
    f32 = np.float32
    m = {}
    m["x"] = np.ascontiguousarray(inp["x"][b], f32)
    m["ctx"] = np.ascontiguousarray(inp["ctx"][b], f32)
    cond = np.stack([np.asarray(inp["c"][b]).reshape(NK, 128).T, np.asarray(inp["c_ctx"]).reshape(NK, 128).T], axis=2)
    m["cond"] = np.ascontiguousarray(cond, f32)
    m["ada_w"] = np.asarray(inp["ada_w"], f32)
    m["ada_b"] = np.asarray(inp["ada_b"], f32)[:, None, :]
    m["ln_g"] = np.asarray(inp["ln_g"], f32).reshape(-1, D)
    m["ln_b"] = np.asarray(inp["ln_b"], f32).reshape(-1, D)

    def glay(a):
        a = np.asarray(a, f32).reshape(NS5, 2, NB, 2, 8, 64)
        return a.transpose(0, 1, 2, 3, 5, 4).reshape(NS5, 2, NB, 128, 8)
    ldt = np.broadcast_to(np.asarray(inp["s5_log_dt"], f32)[..., None], np.asarray(inp["s5_a_re"]).shape)
    m["s5_lam"] = np.ascontiguousarray(np.stack([glay(inp["s5_a_re"]), glay(inp["s5_a_im"]), glay(ldt)], axis=4))

    def blay(re, im):
        out = []
        for a in (re, im):
            a = np.asarray(a, f32).reshape(NS5, 2, NB, 2, 8, 64, 16)
            out.append(a.transpose(0, 1, 2, 3, 5, 4, 6).reshape(NS5, 2, NB, 128, 8, 16))
        return np.ascontiguousarray(np.stack(out, axis=4))
    m["s5_B"] = blay(inp["s5_b_re"], inp["s5_b_im"])
    cre = np.asarray(inp["s5_c_re"], f32).transpose(0, 1, 2, 4, 3)
    cim = np.asarray(inp["s5_c_im"], f32).transpose(0, 1, 2, 4, 3)
    m["s5_C"] = blay(cre, cim)
    m["s5_d"] = np.asarray(inp["s5_d"], f32)[:, None, :]
    m["s5_w_glu"] = np.asarray(inp["s5_w_glu"], f32)
    m["s5_b_glu"] = np.asarray(inp["s5_b_glu"], f32)[:, None, :]
    n_ = max(NCV, 1)

    def pad0(a, shape):
        a = np.asarray(a, f32)
        if a.shape[0] == 0:
            return np.zeros(shape, f32)
        return np.ascontiguousarray(a.reshape(shape))
    m["cv_w_pw1"] = pad0(inp["cv_w_pw1"], (n_, D, 2 * D))
    bp = np.asarray(inp["cv_b_pw1"], f32)
    m["cv_b_pw1"] = np.ascontiguousarray(bp.reshape(-1, 2 * NK, 128).transpose(0, 2, 1)) if bp.shape[0] else np.zeros((n_, 128, 2 * NK), f32)
    wd = np.asarray(inp["cv_w_dw"], f32)
    m["cv_w_dw"] = np.ascontiguousarray(wd.reshape(-1, 31, NK, 128).transpose(0, 3, 2, 1)) if wd.shape[0] else np.zeros((n_, 128, NK, 31), f32)
    bd = np.asarray(inp["cv_b_dw"], f32)
    m["cv_b_dw"] = np.ascontiguousarray(bd.reshape(-1, NK, 128).transpose(0, 2, 1)) if bd.shape[0] else np.zeros((n_, 128, NK), f32)
    m["cv_ln_g"] = pad0(inp["cv_ln_g"], (n_, D)); m["cv_ln_b"] = pad0(inp["cv_ln_b"], (n_, D))
    m["cv_w_pw2"] = pad0(inp["cv_w_pw2"], (n_, D, D)); m["cv_b_pw2"] = pad0(inp["cv_b_pw2"], (n_, 1, D))
    m["moe_w_router"] = np.asarray(inp["moe_w_router"], f32)
    m["moe_w_in"] = np.asarray(inp["moe_w_in"], f32)
    m["moe_w_out"] = np.asarray(inp["moe_w_out"], f32)
    m.update(host_consts(cfg))
    return m


def moe_layer(kb, i, io, scr, with_ctx):
    cfg, f = kb.cfg, kb.f
    D, NK, NT, E, F, NF, L, LC = cfg.D, cfg.NK, cfg.NT, cfg.E, cfg.F, cfg.NF, cfg.L, cfg.LC
    XR, MOD, TS, U2, YG = scr["XR"], scr["MOD"], scr["TS"], scr["U2"], scr["YG"]
    CAPL, CAPC = cfg.CAPL, (cfg.CAPC if with_ctx else 0)
    NSL = CAPL + CAPC
    ntiles = cfg.NTT if with_ctx else cfg.NLT
    NLT = cfg.NLT
    sets = [(0, L, CAPL, 0, 0)]
    if with_ctx:
        sets.append((L, LC, CAPC, CAPL, 1))
    stiles = [(s * 128, min(128, CAPL - s * 128), 0) for s in range((CAPL + 127) // 128)]
    if with_ctx:
        stiles.append((CAPL, CAPC, 1))
    NST = len(stiles)
    with ExitStack() as esl:
        sbl = lambda n, s, dt=F32: kb.sb(esl, n, s, dt)
        IDXF = sbl("IDXF", [16, NSL]); GAT = sbl("GAT", [16, NSL])
        IDXT = sbl("IDXT", [128, NST, 16]); GT = sbl("GT", [128, NST, 16])
        idf = sbl("idf", [128, 128]); idb = sbl("idb", [128, 128], BF16)
        tokid = sbl("tokid", [128, 16 + 2])
        kb.dma("sp", idf[:], io["ident"], [], ["idf"])
        kb.cp("dve", idb[:], idf[:], ["idf"], ["idb"])
        kb.dma("sp", tokid[:], io["tokid"], [], ["tokid"])
        esu = ExitStack()
        U2TM = kb.sb(esu, "U2TM", [128, ntiles, D], BF16)
        with ExitStack() as es:
            sb = lambda n, s, dt=F32: kb.sb(es, n, s, dt)
            wrf = sb("wrf", [128, NK, E]); wrb = sb("wrb", [128, NK, E], BF16)
            kb.dma("sp", wrf[:], io["moe_w_router"][i].rearrange("(k p) e -> p k e", p=128), [], ["wrf"])
            kb.cp("dve", wrb[:], wrf[:], ["wrf"], ["wrb"])
            AFFT = sb("AFFT", [16, L + LC]); WK = sb("WK", [16, L])
            IDXU = sb("IDXU", [16, NSL], U32)
            uT = [sb("uT%d" % q, [128, NK, 128], BF16) for q in range(2)]
            lg = [sb("lg%d" % q, [128, E]) for q in range(2)]
            sm = [sb("smx%d" % q, [128, 4]) for q in range(2)]
            pT = [kb.ps(es, "pT%d" % q, [128, 512], BF16) for q in range(2)]
            pL = [kb.ps(es, "pL%d" % q, [128, 512]) for q in range(2)]
            pA = [kb.ps(es, "pA%d" % q, [128, 512]) for q in range(2)]
            r = 0
            for tt in range(ntiles):
                q = tt % 2
                uk = "U2TM%d" % tt
                kb.dma("sp", U2TM[:, tt, :], U2[tt * 128:(tt + 1) * 128, :], ["U2"], [uk])
                for k4 in range(0, NK, 4):
                    nk = min(4, NK - k4)
                    p = pT[r % 2]; pk = "pT%d" % (r % 2); r += 1
                    for a in range(nk):
                        kb.tr(p[:, a * 128:(a + 1) * 128], U2TM[:, tt, (k4 + a) * 128:(k4 + a + 1) * 128], idb[:], [uk, "idb"], [pk])
                    kb.cp("act" if (r % 2) else "dve", uT[q][:, k4:k4 + nk, :], p[:, 0:nk * 128].rearrange("p (a c) -> p a c", a=nk), [pk], ["uT%d" % q])
                for k in range(NK):
                    kb.mm(pL[q][:, 0:E], uT[q][:, k, :], wrb[:, k, :], k == 0, k == NK - 1, ["uT%d" % q, "wrb"], ["pL%d" % q])
                lk, sk = "lg%d" % q, "smx%d" % q
                kb.f.op("dve", lambda e, o=sm[q][:, 0:1], i_=pL[q][:, 0:E]: e.tensor_reduce(out=o, in_=i_, axis=mybir.AxisListType.X, op=ALU.max), ["pL%d" % q], [sk])
                kb.ts("dve", sm[q][:, 1:2], sm[q][:, 0:1], -1.0, None, ALU.mult, None, [sk], [sk])
                kb.act(lg[q][:], pL[q][:, 0:E], AF.Exp, ["pL%d" % q, sk], [lk], bias=sm[q][:, 1:2])
                kb.f.op("dve", lambda e, o=sm[q][:, 2:3], i_=lg[q][:]: e.tensor_reduce(out=o, in_=i_, axis=mybir.AxisListType.X, op=ALU.add), [lk], [sk])
                kb.f.op("dve", lambda e, o=sm[q][:, 3:4], i_=sm[q][:, 2:3]: e.reciprocal(out=o, in_=i_), [sk], [sk])
                kb.ts("dve", lg[q][:], lg[q][:], sm[q][:, 3:4], None, ALU.mult, None, [lk, sk], [lk])
                kb.tr(pA[q][0:E, 0:128], lg[q][:], idf[:], [lk, "idf"], ["pA%d" % q])
                kb.cp("act", AFFT[:, tt * 128:(tt + 1) * 128], pA[q][0:E, 0:128], ["pA%d" % q], ["AFFT"])
            for (tok0, ntok, cap, slot0, isctx) in sets:
                src = AFFT[:, tok0:tok0 + ntok]
                srck = "AFFT"
                for rd in range(cap // 8):
                    sl = slice(slot0 + rd * 8, slot0 + rd * 8 + 8)
                    kb.f.op("dve", lambda e, o=GAT[:, sl], i_=src: e.max(out=o, in_=i_), [srck], ["GAT"])
                    kb.f.op("dve", lambda e, o=IDXU[:, sl], m_=GAT[:, sl], v_=src: e.max_index(out=o, in_max=m_, in_values=v_), [srck, "GAT"], ["IDXU"])
                    if rd < cap // 8 - 1:
                        dst = WK[:, 0:ntok]
                        kb.f.op("dve", lambda e, o=dst, m_=GAT[:, sl], v_=src: e.match_replace(out=o, in_to_replace=m_, in_values=v_, imm_value=-1.0), [srck, "GAT"], ["WK"])
                        src = dst
                        srck = "WK"
            kb.cp("dve", IDXF[:], IDXU[:], ["IDXU"], ["IDXF"])
            if with_ctx:
                kb.ts("dve", IDXF[:, CAPL:NSL], IDXF[:, CAPL:NSL], float(L), None, ALU.add, None, ["IDXF"], ["IDXF"])
            for si, (s0, n, _) in enumerate(stiles):
                q = si % 2
                kb.tr(pA[q][0:n, 0:16], IDXF[:, s0:s0 + n], idf[0:16, 0:16], ["IDXF", "idf"], ["pA%d" % q])
                kb.cp("dve", IDXT[0:n, si, :], pA[q][0:n, 0:16], ["pA%d" % q], ["IDXT"])
                kb.tr(pL[q][0:n, 0:16], GAT[:, s0:s0 + n], idf[0:16, 0:16], ["GAT", "idf"], ["pL%d" % q])
                kb.cp("dve", GT[0:n, si, :], pL[q][0:n, 0:16], ["pL%d" % q], ["GT"])
        f.barrier()
        with ExitStack() as es:
            sb = lambda n, s, dt=F32: kb.sb(es, n, s, dt)
            SEL = sb("SEL", [16, 16, 128])
            kb.dma("sp", SEL[:], io["sel"], [], ["SEL"])
            XselT = [sb("XselT0", [128, NK, NSL], BF16)] * 2
            HT = sb("HT", [128, NF, NSL], BF16)
            Sx = [sb("Sx0", [128, ntiles, CAPL], BF16)] * 2
            NWIN = 3
            WIN = [sb("WIN%d" % q, [128, 2, NK, 128], BF16) for q in range(NWIN)]
            WOUT = [sb("WOUT%d" % q, [128, NF, D], BF16) for q in range(2)]
            sg = [sb("sg%d" % q, [128, NSL]) for q in range(2)]
            ygs = [sb("ygs%d" % q, [128, D], BF16) for q in range(2)]
            pb = kb.ps(es, "pb", [128, 512])
            pgx = [kb.ps(es, "pgx%d" % q, [128, 512]) for q in range(2)]
            ph = [kb.ps(es, "ph%d" % q, [128, 512]) for q in range(4)]
            py = kb.ps(es, "py", [128, 512])
            wi_r = 0; wo_r = 0; yg_r = 0; gx_r = 0
            for e_ in range(E):
                q = e_ % 2
                sxk = "Sx0"
                kb.mm(pb[:, 0:NSL], SEL[:, e_, :], IDXF[:, :], True, True, ["SEL", "IDXF"], ["pb"])
                for tt in range(ntiles):
                    isctx = 1 if tt >= NLT else 0
                    s0, cap = (CAPL, CAPC) if isctx else (0, CAPL)
                    kb.ts("dve", Sx[q][:, tt, 0:cap], pb[:, s0:s0 + cap], tokid[:, tt:tt + 1], None, ALU.is_equal, None, ["pb", "tokid"], [sxk])
                xk = "XselT0"
                for k in range(NK):
                    pg_ = pgx[gx_r % 2]; pgk = "pgx%d" % (gx_r % 2); gx_r += 1
                    for tt in range(NLT):
                        kb.mm(pg_[:, 0:CAPL], U2TM[:, tt, k * 128:(k + 1) * 128], Sx[q][:, tt, 0:CAPL], tt == 0, tt == NLT - 1, ["U2TM%d" % tt, sxk], [pgk])
                    if with_ctx:
                        for tt in range(NLT, ntiles):
                            kb.mm(pg_[:, CAPL:NSL], U2TM[:, tt, k * 128:(k + 1) * 128], Sx[q][:, tt, 0:CAPC], tt == NLT, tt == ntiles - 1, ["U2TM%d" % tt, sxk], [pgk])
                    kb.cp("act" if k % 2 else "dve", XselT[q][:, k, :], pg_[:, 0:NSL], [pgk], [xk])
                wo = WOUT[e_ % 2]; wok = "WOUT%d" % (e_ % 2)
                for fc in range(NF):
                    wq = wi_r % NWIN; wi_r += 1
                    for gu in range(2):
                        c0 = gu * F + fc * 128
                        kb.dma("pool", WIN[wq][:, gu, :, :], io["moe_w_in"][i, e_, :, c0:c0 + 128].rearrange("(k p) c -> p k c", p=128), [], ["WIN%d_%d" % (wq, gu)])
                    kb.dma("pool", wo[:, fc, :], io["moe_w_out"][i, e_, fc * 128:(fc + 1) * 128, :], [], [wok + "_%d" % fc])
                    hq = fc % 2
                    phg = ph[hq * 2]; phu = ph[hq * 2 + 1]
                    pgk_, puk_ = "ph%d" % (hq * 2), "ph%d" % (hq * 2 + 1)
                    for k in range(NK):
                        kb.mm(phg[:, 0:NSL], WIN[wq][:, 0, k, :], XselT[q][:, k, :], k == 0, k == NK - 1, ["WIN%d_0" % wq, xk], [pgk_])
                    for k in range(NK):
                        kb.mm(phu[:, 0:NSL], WIN[wq][:, 1, k, :], XselT[q][:, k, :], k == 0, k == NK - 1, ["WIN%d_1" % wq, xk], [puk_])
                    kb.act(sg[hq][:], phg[:, 0:NSL], AF.Silu, [pgk_], ["sg%d" % hq])
                    kb.tt("dve", HT[:, fc, :], phu[:, 0:NSL], sg[hq][:], ALU.mult, [puk_, "sg%d" % hq], ["HT"])
                for si, (s0, n, _) in enumerate(stiles):
                    yq = yg_r % 2; yg_r += 1
                    for nt in range(D // NT):
                        for fk in range(NF):
                            kb.mm(py[0:n, 0:NT], HT[:, fk, s0:s0 + n], wo[:, fk, nt * NT:(nt + 1) * NT], fk == 0, fk == NF - 1, ["HT", wok + "_%d" % fk], ["py"])
                        if nt % 2:
                            kb.ts("dve", ygs[yq][0:n, nt * NT:(nt + 1) * NT], py[0:n, 0:NT], GT[0:n, si, e_:e_ + 1], None, ALU.mult, None, ["py", "GT"], ["ygs%d" % yq])
                        else:
                            kb.act(ygs[yq][0:n, nt * NT:(nt + 1) * NT], py[0:n, 0:NT], AF.Copy, ["py", "GT"], ["ygs%d" % yq], scale=GT[0:n, si, e_:e_ + 1])
                    kb.dma("sp", YG[e_, s0:s0 + n, :], ygs[yq][0:n, :], ["ygs%d" % yq], ["YG"])
        f.barrier()
        esu.close()
        with ExitStack() as es:
            sb = lambda n, s, dt=F32: kb.sb(es, n, s, dt)
            NSTL = (CAPL + 127) // 128
            PL = min(128, CAPL)
            iot = sb("iot", [128, L + LC])
            kb.dma("sp", iot[:], io["iota_f"], [], ["iot"])
            YGL = sb("YGL", [128, E, NSTL, D], BF16)
            for e_ in range(E):
                kb.dma("sp" if e_ % 2 else "pool", YGL[0:PL, e_, :, :], YG[e_, 0:CAPL, :].rearrange("(s p) c -> p s c", p=PL), ["YG"], ["YGL%d" % e_])
            g2t = sb("g2t", [128, 2, D])
            for a_ in range(2 if with_ctx else 1):
                kb.dma("sp", g2t[:, a_, :], bcast_rows(MOD[i, 6 * a_ + 5:6 * a_ + 6, :], 128), [], ["g2t"])
            NEQ = (E * CAPC + 127) // 128 if with_ctx else 0
            EP = 128 // CAPC if with_ctx else 1
            if with_ctx:
                YGC = sb("YGC", [128, NEQ, D], BF16)
                ygv = YG[:, CAPL:NSL, :].rearrange("(eq e4) s c -> e4 s eq c", e4=EP)
                for e4 in range(EP):
                    kb.dma("sp", YGC[e4 * CAPC:(e4 + 1) * CAPC, :, :], ygv[e4], ["YG"], ["YGC"])
                shf = sb("shf", [CAPC, EP, 128]); IDXC = sb("IDXC", [128, NEQ])
                kb.dma("sp", shf[:], io["shift"], [], ["shf"])
                pI = kb.ps(es, "pI", [128, 512])
                for e4 in range(EP):
                    rhs = fview(IDXT, NSTL * 16 + e4, [(EP, NEQ)], pn=CAPC)
                    kb.mm(pI[:, 0:NEQ], shf[:, e4, :], rhs, e4 == 0, e4 == EP - 1, ["shf", "IDXT"], ["pI"])
                kb.cp("dve", IDXC[:], pI[:, 0:NEQ], ["pI"], ["IDXC"])
                STC = [sb("STC%d" % q, [128, NEQ, 128], BF16) for q in range(2)]
            STL = [sb("STL%d" % q, [128, E * NSTL, 128], BF16) for q in range(2)]
            xs = [sb("xs%d" % q, [128, NT]) for q in range(2)]
            ft = [sb("ft%d" % q, [128, NT]) for q in range(2)]
            pf = [kb.ps(es, "pf%d" % q, [128, 512]) for q in range(2)]
            it = 0
            for tt in range(ntiles):
                tq = tt % 2
                isctx = 1 if tt >= NLT else 0
                rows = slice(tt * 128, (tt + 1) * 128)
                if not isctx:
                    terms = [(e_, s_) for e_ in range(E) for s_ in range(NSTL)]
                    for ti, (e_, s_) in enumerate(terms):
                        pl_ = ti % 3 == 2
                        kb.ts("pool" if pl_ else "dve", STL[tq][0:PL, ti, :], iot[0:PL, tt * 128:(tt + 1) * 128], IDXT[0:PL, s_, e_:e_ + 1], None,
                              ALU.is_equal, None, ["iot", "IDXT"], ["STL%d_%d_%d" % (tq, int(pl_), ti)])
                else:
                    for eq in range(NEQ):
                        kb.ts("dve", STC[tq][:, eq, :], iot[:, tt * 128:(tt + 1) * 128], IDXC[:, eq:eq + 1], None, ALU.is_equal, None, ["iot", "IDXC"], ["STC%d_%d" % (tq, eq)])
                for nt in range(D // NT):
                    cs_ = slice(nt * NT, (nt + 1) * NT)
                    q = it % 2; it += 1
                    kb.dma("sp", xs[q][:], XR[rows, cs_], [], ["xs%d" % q])
                    if not isctx:
                        for ti, (e_, s_) in enumerate(terms):
                            kb.mm(pf[q][:, 0:NT], STL[tq][0:PL, ti, :], YGL[0:PL, e_, s_, cs_], ti == 0, ti == len(terms) - 1,
                                  ["STL%d_%d_%d" % (tq, int(ti % 3 == 2), ti), "YGL%d" % e_], ["pf%d" % q])
                    else:
                        for eq in range(NEQ):
                            kb.mm(pf[q][:, 0:NT], STC[tq][:, eq, :], YGC[:, eq, cs_], eq == 0, eq == NEQ - 1, ["STC%d_%d" % (tq, eq), "YGC"], ["pf%d" % q])
                    kb.tt("dve", ft[q][:], pf[q][:, 0:NT], g2t[:, isctx, cs_], ALU.mult, ["pf%d" % q, "g2t"], ["ft%d" % q])
                    kb.stt(ft[q][:], xs[q][:], cfg.ALPHA, ft[q][:], ALU.mult, ALU.add, ["xs%d" % q, "ft%d" % q], ["ft%d" % q])
                    kb.dma("pool", TS[rows, cs_], ft[q][:], ["ft%d" % q], ["TS"])
    f.barrier()


def kernel(**inputs):
    cfg = Cfg()
    nb = int(np.asarray(inputs["x"]).shape[0])
    nc = build(cfg)
    maps = [host_layout(cfg, inputs, b) for b in range(nb)]
    res = run_bass_kernel_spmd(nc, maps, core_ids=list(range(nb)))
    out = np.stack([np.asarray(res.results[b]["out"]) for b in range(nb)])
    return out.astype(np.float32)
```

```python
import math
from contextlib import ExitStack

import numpy as np
import concourse.bass as bass
import concourse.mybir as mybir
from concourse.ap import AP
from concourse.bass_utils import run_bass_kernel_spmd

F32 = mybir.dt.float32
BF16 = mybir.dt.bfloat16
I32 = mybir.dt.int32
U32 = mybir.dt.uint32
AF = mybir.ActivationFunctionType
ALU = mybir.AluOpType

COMPUTE = ("pe", "act", "dve", "pool")
DMAQ = ("sp", "act", "pool")
NDMASEM = 8


class Op:
    __slots__ = ("eng", "fn", "deps", "is_dma", "needed", "semval")

    def __init__(self, eng, fn, is_dma):
        self.eng = eng
        self.fn = fn
        self.deps = []
        self.is_dma = is_dma
        self.needed = False
        self.semval = None


class FW:
    def __init__(self, nc):
        self.nc = nc
        self.streams = {e: [] for e in ("pe", "act", "dve", "pool", "sp")}
        self.last_w = {}
        self.readers = {}
        self.bar_idx = {e: 0 for e in self.streams}

    def op(self, eng, fn, reads=(), writes=(), dma=False):
        o = Op(eng, fn, dma)
        deps = {}
        for k in reads:
            lw = self.last_w.get(k)
            if lw is not None:
                deps[id(lw)] = lw
        for k in writes:
            lw = self.last_w.get(k)
            if lw is not None:
                deps[id(lw)] = lw
            for r in self.readers.get(k, ()):
                deps[id(r)] = r
        for d in deps.values():
            if (not d.is_dma) and (not dma) and d.eng == eng and eng == "pe":
                continue
            o.deps.append(d)
            d.needed = True
        for k in writes:
            self.last_w[k] = o
            self.readers[k] = []
        for k in reads:
            if k in writes:
                continue
            lst = self.readers.setdefault(k, [])
            if not dma:
                lst[:] = [r for r in lst if not (r.eng == eng and not r.is_dma)]
            lst.append(o)
        self.streams[eng].append(o)
        return o

    def barrier(self):
        lastops = []
        for e, st in self.streams.items():
            for o in reversed(st):
                if (not o.is_dma) and o.fn is not None:
                    lastops.append(o)
                    break
            lastops += [o for o in st[self.bar_idx[e]:] if o.is_dma]
        for e in self.streams:
            o = Op(e, None, False)
            o.deps = list(lastops)
            self.streams[e].append(o)
        for d in lastops:
            d.needed = True
        for e in self.streams:
            self.bar_idx[e] = len(self.streams[e])
        self.last_w = {}
        self.readers = {}

    def emit(self, final_wait_ops=()):
        nc = self.nc
        with ExitStack() as es:
            csem = {e: es.enter_context(nc.semaphore("c_" + e)) for e in COMPUTE}
            dsems = {e: [es.enter_context(nc.semaphore("d_%s%d" % (e, i))) for i in range(NDMASEM)]
                     for e in DMAQ}
            for e in COMPUTE:
                cnt = 0
                for o in self.streams[e]:
                    if o.is_dma or o.fn is None:
                        continue
                    if o.needed:
                        cnt += 1
                        o.semval = (csem[e], cnt)
            for e in DMAQ:
                dcount = [0] * NDMASEM
                rr = 0
                for o in self.streams[e]:
                    if not o.is_dma:
                        continue
                    dcount[rr] += 1
                    o.semval = (dsems[e][rr], 16 * dcount[rr])
                    rr = (rr + 1) % NDMASEM
            block = es.enter_context(nc.Block())
            handles = {"pe": block.tensor, "act": block.scalar, "dve": block.vector,
                       "pool": block.gpsimd, "sp": block.sync}
            for e in ("sp", "pool", "act", "dve", "pe"):
                ops = self.streams[e]
                finals = list(final_wait_ops) if e == "sp" else []

                def body(eng, ops=ops, finals=finals):
                    waited = {}

                    def wait(sem, val):
                        k = id(sem)
                        if waited.get(k, 0) >= val:
                            return
                        waited[k] = val
                        eng.wait_ge(sem, val)

                    for o in ops:
                        for d in o.deps:
                            wait(*d.semval)
                        if o.fn is None:
                            continue
                        if o.is_dma:
                            sem, val = o.semval
                            if val > 16:
                                wait(sem, val - 16)
                            o.fn(eng).then_inc(sem, 16)
                        else:
                            ins = o.fn(eng)
                            if o.needed:
                                ins.then_inc(o.semval[0], 1)
                    for o in finals:
                        wait(*o.semval)

                handles[e](body)


class Cfg:
    def __init__(self, D=2048, L=2048, LC=256, DEPTH=4):
        self.D, self.L, self.LC, self.DEPTH = D, L, LC, DEPTH
        self.E = 16
        self.GW = 64
        self.KW = 31
        self.NK = D // 128
        self.G = D // 16
        self.NB = self.G // 16
        self.F = D // 2
        self.NF = self.F // 128
        self.NT = min(512, D)
        self.TALL = L + LC
        self.NLT = L // 128
        self.NCT = LC // 128
        self.NTT = self.TALL // 128
        self.CAPL = 2 * L // 16
        self.CAPC = 2 * LC // 16
        self.NCL = L // 8
        self.NCC = LC // 8
        self.NC = self.NCL + self.NCC
        self.ALPHA = (2.0 * DEPTH) ** 0.25
        self.NS5 = (DEPTH + 1) // 2
        self.NCV = DEPTH // 2


LN_EPS = 1e-5
TWO_PI = 2.0 * math.pi


def fview(t, off, dims, p0=0, pn=None):
    base = t[:]
    pstep, pcount = base.ap[0]
    if pn is None:
        pn = pcount - p0
    return AP(base.tensor, base.offset + p0 * pstep + off, [[pstep, pn]] + [list(d) for d in dims])


class KB:
    def __init__(self, cfg, debug_outs=()):
        self.cfg = cfg
        self.nc = bass.Bass("TRN2", target_bir_lowering=False)
        self.f = FW(self.nc)
        self.debug_outs = set(debug_outs)
        self.uid = 0
        self.out_stores = []

    def din(self, name, shape, dt=F32):
        return self.nc.dram_tensor(name, list(shape), dt, kind="ExternalInput").ap()

    def dscr(self, name, shape, dt=F32):
        kind = "ExternalOutput" if name in self.debug_outs else "Internal"
        return self.nc.dram_tensor(name, list(shape), dt, kind=kind).ap()

    def sb(self, es, name, shape, dt=F32):
        self.uid += 1
        return es.enter_context(self.nc.sbuf_tensor("%s_%d" % (name, self.uid), list(shape), dt))

    def ps(self, es, name, shape, dt=F32):
        self.uid += 1
        return es.enter_context(self.nc.psum_tensor("%s_%d" % (name, self.uid), list(shape), dt))

    SCRATCH = ("XR", "TS", "U2", "ZS", "CV", "MOD", "YG", "OUTF")

    def dma(self, q, out, in_, r, w):
        w2 = []
        for k in w:
            if k in self.SCRATCH:
                self.uid += 1
                k = "%s#%d" % (k, self.uid)
            w2.append(k)
        return self.f.op(q, lambda e: e.dma_start(out=out, in_=in_), r, w2, dma=True)

    def tt(self, eng, out, in0, in1, op, r, w):
        return self.f.op(eng, lambda e: e.tensor_tensor(out=out, in0=in0, in1=in1, op=op), r, w)

    def ts(self, eng, out, in0, s1, s2, op0, op1, r, w):
        if s2 is None:
            return self.f.op(eng, lambda e: e.tensor_scalar(out=out, in0=in0, scalar1=s1, scalar2=None, op0=op0), r, w)
        return self.f.op(eng, lambda e: e.tensor_scalar(out=out, in0=in0, scalar1=s1, scalar2=s2, op0=op0, op1=op1), r, w)

    def stt(self, out, in0, scalar, in1, op0, op1, r, w):
        return self.f.op("dve", lambda e: e.scalar_tensor_tensor(out=out, in0=in0, scalar=scalar, in1=in1, op0=op0, op1=op1), r, w)

    def cp(self, eng, out, in_, r, w):
        if eng == "act":
            return self.f.op(eng, lambda e: e.copy(out=out, in_=in_), r, w)
        return self.f.op(eng, lambda e: e.tensor_copy(out=out, in_=in_), r, w)

    def act(self, out, in_, func, r, w, scale=1.0, bias=None):
        if bias is None:
            return self.f.op("act", lambda e: e.activation(out=out, in_=in_, func=func, scale=scale), r, w)
        return self.f.op("act", lambda e: e.activation(out=out, in_=in_, func=func, scale=scale, bias=bias), r, w)

    def mm(self, out, lhsT, rhs, start, stop, r, w):
        return self.f.op("pe", lambda e: e.matmul(out=out, lhsT=lhsT, rhs=rhs, start=start, stop=stop), r, w)

    def tr(self, out, in_, ident, r, w):
        return self.f.op("pe", lambda e: e.transpose(out=out, in_=in_, identity=ident), r, w)

    def memset(self, eng, ap, val, r, w):
        return self.f.op(eng, lambda e: e.memset(ap, val), r, w)


def s5_paramgen_gen(kb, es, j, io, SW, A8T, A8D):
    cfg, f = kb.cfg, kb.f
    GL = 8
    NB = cfg.NB
    GW = NB * GL
    sb = lambda n, s, dt=F32: kb.sb(es, n, s, dt)
    kt = sb("kt", [128, 2, 4, 2, 8])
    msk = sb("msk", [128, 2, 128])
    idf = sb("idf", [128, 128])
    kb.dma("sp", kt[:], io["s5_kt"], [], ["kt"])
    kb.dma("sp", msk[:], io["s5_mask"], [], ["msk"])
    kb.dma("sp", idf[:], io["ident"], [], ["idf_pg"])
    LM = sb("LM", [128, 3, GW]); Bt = sb("Bt", [128, 2, GW, 16]); Ct = sb("Ct", [128, 2, GW, 16])
    dtv = sb("dtv", [128, GW]); XRI = sb("XRI", [128, 2, GW])
    marg = sb("marg", [128, GW, 8]); mag = sb("mag", [128, GW, 8]); y = sb("y", [128, GW, 8])
    yi = sb("yi", [128, GW, 8], I32); yf = sb("yf", [128, GW, 8]); m1 = sb("m1", [128, GW, 8])
    sinv = sb("sinv", [128, GW, 8]); cosv = sb("cosv", [128, GW, 8])
    P = sb("P", [128, 4, 2, GW, 8])
    sm = sb("sm", [128, 8, GW])
    Bb = sb("Bb", [128, 2, GW, 16]); tb = sb("tb", [128, GW, 16])
    WP = sb("WP", [128, 2, GL, 128]); WE = sb("WE", [128, 2, GL, 128]); CM = sb("CM", [128, 2, GL, 128])
    tw = sb("tw", [128, GL, 128])
    OUT = sb("OUT", [128, GL, 6, 128], BF16)
    pT = [kb.ps(es, "pT%d" % i, [128, 512]) for i in range(2)]
    pM = [kb.ps(es, "pM%d" % i, [128, 512]) for i in range(2)]
    V = "dve"
    for d in range(2):
        for blk in range(NB):
            gs = slice(blk * GL, (blk + 1) * GL)
            kb.dma("sp", LM[:, :, gs], io["s5_lam"][j, d, blk], [], ["LM"])
            kb.dma("sp", Bt[:, :, gs, :], io["s5_B"][j, d, blk], [], ["Bt"])
            kb.dma("sp", Ct[:, :, gs, :], io["s5_C"][j, d, blk], [], ["Ct"])
        kb.act(dtv[:], LM[:, 2, :], AF.Exp, ["LM"], ["dtv"])
        kb.tt(V, XRI[:, 0, :], LM[:, 0, :], dtv[:], ALU.mult, ["LM", "dtv"], ["XRI"])
        kb.tt(V, XRI[:, 1, :], LM[:, 1, :], dtv[:], ALU.mult, ["LM", "dtv"], ["XRI"])
        for ty in range(4):
            ktab = fview(kt, ((d * 4 + ty) * 2 + 0) * 8, [(0, GW), (1, 8)])
            ktab2 = fview(kt, ((d * 4 + ty) * 2 + 1) * 8, [(0, GW), (1, 8)])
            xr_b = fview(XRI, 0, [(1, GW), (0, 8)])
            xi_b = fview(XRI, GW, [(1, GW), (0, 8)])
            kb.tt(V, marg[:], xr_b, ktab, ALU.mult, ["XRI", "kt"], ["marg"])
            kb.act(mag[:], marg[:], AF.Exp, ["marg"], ["mag"])
            kb.tt(V, y[:], xi_b, ktab2, ALU.mult, ["XRI", "kt"], ["y"])
            kb.cp(V, yi[:], y[:], ["y"], ["yi"])
            kb.cp(V, yf[:], yi[:], ["yi"], ["yf"])
            kb.tt(V, y[:], y[:], yf[:], ALU.subtract, ["y", "yf"], ["y"])
            kb.ts(V, m1[:], y[:], 0.5, None, ALU.is_gt, None, ["y"], ["m1"])
            kb.tt(V, y[:], y[:], m1[:], ALU.subtract, ["y", "m1"], ["y"])
            kb.ts(V, m1[:], y[:], -0.5, None, ALU.is_lt, None, ["y"], ["m1"])
            kb.tt(V, y[:], y[:], m1[:], ALU.add, ["y", "m1"], ["y"])
            kb.act(sinv[:], y[:], AF.Sin, ["y"], ["sinv"], scale=TWO_PI)
            kb.ts(V, yf[:], y[:], 0.25, None, ALU.add, None, ["y"], ["yf"])
            kb.ts(V, m1[:], yf[:], 0.5, None, ALU.is_gt, None, ["yf"], ["m1"])
            kb.tt(V, yf[:], yf[:], m1[:], ALU.subtract, ["yf", "m1"], ["yf"])
            kb.act(cosv[:], yf[:], AF.Sin, ["yf"], ["cosv"], scale=TWO_PI)
            kb.tt(V, P[:, ty, 0], mag[:], cosv[:], ALU.mult, ["mag", "cosv"], ["P"])
            kb.tt(V, P[:, ty, 1], mag[:], sinv[:], ALU.mult, ["mag", "sinv"], ["P"])
        yield
        a1re = P[:, 3, 0, :, 0]; a1im = P[:, 3, 1, :, 0]
        a8re = P[:, 3, 0, :, 1]; a8im = P[:, 3, 1, :, 1]
        lre = LM[:, 0, :]; lim = LM[:, 1, :]
        S = lambda i_: sm[:, i_, :]
        kb.ts(V, S(0), a1re, -1.0, None, ALU.add, None, ["P"], ["sm0"])
        kb.tt(V, S(1), lre, lre, ALU.mult, ["LM"], ["sm1"])
        kb.tt(V, S(2), lim, lim, ALU.mult, ["LM"], ["sm2"])
        kb.tt(V, S(1), S(1), S(2), ALU.add, ["sm1", "sm2"], ["sm1"])
        kb.f.op(V, lambda e, o=S(1), i_=S(1): e.reciprocal(out=o, in_=i_), ["sm1"], ["sm1"])
        kb.tt(V, S(2), S(0), lre, ALU.mult, ["sm0", "LM"], ["sm2"])
        kb.tt(V, S(3), a1im, lim, ALU.mult, ["P", "LM"], ["sm3"])
        kb.tt(V, S(2), S(2), S(3), ALU.add, ["sm2", "sm3"], ["sm2"])
        kb.tt(V, S(4), S(2), S(1), ALU.mult, ["sm2", "sm1"], ["sm4"])
        kb.tt(V, S(2), a1im, lre, ALU.mult, ["P", "LM"], ["sm2"])
        kb.tt(V, S(3), S(0), lim, ALU.mult, ["sm0", "LM"], ["sm3"])
        kb.tt(V, S(2), S(2), S(3), ALU.subtract, ["sm2", "sm3"], ["sm2"])
        kb.tt(V, S(5), S(2), S(1), ALU.mult, ["sm2", "sm1"], ["sm5"])
        cre_b = fview(sm, 4 * GW, [(1, GW), (0, 16)]); cim_b = fview(sm, 5 * GW, [(1, GW), (0, 16)])
        kb.tt(V, Bb[:, 0], Bt[:, 0], cre_b, ALU.mult, ["Bt", "sm4"], ["Bb"])
        kb.tt(V, tb[:], Bt[:, 1], cim_b, ALU.mult, ["Bt", "sm5"], ["tb"])
        kb.tt(V, Bb[:, 0], Bb[:, 0], tb[:], ALU.subtract, ["Bb", "tb"], ["Bb"])
        kb.tt(V, Bb[:, 1], Bt[:, 1], cre_b, ALU.mult, ["Bt", "sm4"], ["Bb"])
        kb.tt(V, tb[:], Bt[:, 0], cim_b, ALU.mult, ["Bt", "sm5"], ["tb"])
        kb.tt(V, Bb[:, 1], Bb[:, 1], tb[:], ALU.add, ["Bb", "tb"], ["Bb"])
        kb.cp(V, S(6), a8re, ["P"], ["sm6"])
        kb.cp(V, S(7), a8im, ["P"], ["sm7"])
        BST = 2 * 5 * 2 * 2 * GL
        for lvl in range(5):
            base = ((d * 5 + lvl) * 2) * 2 * GL
            a8t = lambda which, ri: fview(A8T, base + (which * 2 + ri) * GL, [(BST, NB), (1, GL)])
            s6 = fview(sm, 6 * GW, [(GL, NB), (1, GL)]); s7 = fview(sm, 7 * GW, [(GL, NB), (1, GL)])
            kb.cp(V, a8t(0, 0), s6, ["sm6"], ["A8T"])
            kb.cp(V, a8t(0, 1), s6, ["sm6"], ["A8T"])
            kb.ts(V, a8t(1, 0), s7, -1.0, None, ALU.mult, None, ["sm7"], ["A8T"])
            kb.cp(V, a8t(1, 1), s7, ["sm7"], ["A8T"])
            if lvl < 4:
                kb.tt(V, S(2), S(6), S(6), ALU.mult, ["sm6"], ["sm2"])
                kb.tt(V, S(3), S(7), S(7), ALU.mult, ["sm7"], ["sm3"])
                kb.stt(S(7), S(6), 2.0, S(7), ALU.mult, ALU.mult, ["sm6", "sm7"], ["sm7"])
                kb.tt(V, S(6), S(2), S(3), ALU.subtract, ["sm2", "sm3"], ["sm6"])
        yield
        for blk in range(NB):
            g0 = blk * GL

            def cprod(dst, ty, src, key_src, key_dst, g0=g0):
                pre = fview(P, ((ty * 2 + 0) * GW + g0) * 8, [(8, GL), (1, 8), (0, 16)])
                pim = fview(P, ((ty * 2 + 1) * GW + g0) * 8, [(8, GL), (1, 8), (0, 16)])
                sre = fview(src, g0 * 16, [(16, GL), (0, 8), (1, 16)])
                sim = fview(src, (GW + g0) * 16, [(16, GL), (0, 8), (1, 16)])
                dre = fview(dst, 0, [(128, GL), (16, 8), (1, 16)])
                dim_ = fview(dst, GL * 128, [(128, GL), (16, 8), (1, 16)])
                twv = fview(tw, 0, [(128, GL), (16, 8), (1, 16)])
                kb.tt(V, dre, pre, sre, ALU.mult, ["P", key_src], [key_dst])
                kb.tt(V, twv, pim, sim, ALU.mult, ["P", key_src], ["tw"])
                kb.tt(V, dre, dre, twv, ALU.subtract, [key_dst, "tw"], [key_dst])
                kb.tt(V, dim_, pre, sim, ALU.mult, ["P", key_src], [key_dst])
                kb.tt(V, twv, pim, sre, ALU.mult, ["P", key_src], ["tw"])
                kb.tt(V, dim_, dim_, twv, ALU.add, [key_dst, "tw"], [key_dst])

            cprod(CM, 0, Ct, "Ct", "CM")
            cprod(WP, 1, Bb, "Bb", "WP")
            cprod(WE, 2, Bb, "Bb", "WE")
            kb.ts(V, CM[:, 1], CM[:, 1], -1.0, None, ALU.mult, None, ["CM"], ["CM"])
            kb.cp("pool", OUT[:, :, 2, :], CM[:, 0], ["CM"], ["OUT"])
            kb.cp("pool", OUT[:, :, 3, :], CM[:, 1], ["CM"], ["OUT"])
            for ri in range(2):
                for q in range(2):
                    pt = pT[(ri * 2 + q) % 2]
                    ptk = "pT%d" % ((ri * 2 + q) % 2)
                    for i4 in range(4):
                        gl = q * 4 + i4
                        kb.tr(pt[:, i4 * 128:(i4 + 1) * 128], WE[:, ri, gl, :], idf[:], ["WE", "idf_pg"], [ptk])
                    kb.cp("act", OUT[:, q * 4:(q + 1) * 4, ri, :], pt[:, :].rearrange("p (g c) -> p g c", g=4), [ptk], ["OUT"])
            for gh in range(2):
                for q in range(2):
                    pm = pM[(gh * 2 + q) % 2]
                    pmk = "pM%d" % ((gh * 2 + q) % 2)
                    for i4 in range(4):
                        gl = q * 4 + i4
                        lo, hi = gh * 64, (gh + 1) * 64
                        kb.mm(pm[:, i4 * 128:(i4 + 1) * 128], WP[lo:hi, 0, gl, :], CM[lo:hi, 0, gl, :], True, False, ["WP", "CM"], [pmk])
                        kb.mm(pm[:, i4 * 128:(i4 + 1) * 128], WP[lo:hi, 1, gl, :], CM[lo:hi, 1, gl, :], False, True, ["WP", "CM"], [pmk])
                    mk = fview(msk, d * 128, [(0, 4), (1, 128)])
                    kb.tt(V, OUT[:, q * 4:(q + 1) * 4, 4 + gh, :], pm[:, :].rearrange("p (g c) -> p g c", g=4), mk, ALU.mult, [pmk, "msk"], ["OUT"])
            kb.dma("pool", SW[j, d, blk], OUT[:], ["OUT"], ["SW%d_%d_%d" % (j, d, blk)])
            yield
    kb.dma("pool", A8D[j], A8T[:].rearrange("p a b c d e g -> p (a b c d e g)"), ["A8T"], ["A8D%d" % j])
    yield


def bcast_rows(ap2d, n):
    return ap2d.broadcast_to([n, ap2d.shape[1]])


def s5_layer(kb, i, j, io, scr, ctx_out):
    cfg, f = kb.cfg, kb.f
    GL = 8
    NC, NCL, NCC, L = cfg.NC, cfg.NCL, cfg.NCC, cfg.L
    XR, MOD, ZS, SW = scr["XR"], scr["MOD"], scr["ZS"], scr["SW"]
    with ExitStack() as es:
        sb = lambda n, s, dt=F32: kb.sb(es, n, s, dt)
        A8T = sb("A8T", [128, cfg.NB, 2, 5, 2, 2, GL])
        kb.dma("sp", A8T[:].rearrange("p a b c d e g -> p (a b c d e g)"), scr["A8D"][j], [], ["A8T"])
        idb = sb("idb", [128, 128], BF16)
        idf = sb("idf2", [128, 128])
        kb.dma("sp", idf[:], io["ident"], [], ["idf"])
        kb.cp("dve", idb[:], idf[:], ["idf"], ["idb"])
        ctiles = [(t * 1024, 128, t * 128, 0) for t in range(L // 1024)] + [(L, NCC, NCL, 1)]
        nct = len(ctiles)
        W8 = sb("W8", [128, 2, GL, 6, 128], BF16)
        mod4 = sb("mod4", [128, 4, 256]); dsk = sb("dsk", [128, 256])
        U32 = [sb("U32_%d" % c, [128, 8, 256]) for c in range(nct)]
        U8g = [sb("U8g%d" % c, [128, 16, 128], BF16) for c in range(2)]
        X8 = sb("X8", [128, 16, NC], BF16)
        Z = sb("Z", [128, 2, 2, GL, NC])
        NMAX_ = max(NCL, NCC) // 2
        tq1 = [sb("tq1_%d" % d_, [128, 2, GL, NMAX_]) for d_ in range(2)]
        tq2 = [sb("tq2_%d" % d_, [128, 2, GL, NMAX_]) for d_ in range(2)]
        Sbf = sb("Sbf", [128, 2, 2, GL, NC], BF16)
        Y8s = sb("Y8s", [128, 16, NC], BF16)
        ytm = [sb("ytm%d" % c, [128, 8, 256]) for c in range(2)]
        zt = [sb("zt%d" % c, [128, 8, 256], BF16) for c in range(2)]
        pX = [kb.ps(es, "pX%d" % c, [128, 512], BF16) for c in range(2)]
        pZ = [kb.ps(es, "pZ%d" % c, [128, 512]) for c in range(4)]
        pY = [kb.ps(es, "pY%d" % c, [128, 512]) for c in range(2)]
        DS, RS = 2 * GL * NC, GL * NC

        def cf(k):
            return NCL + k if k < NCC else k - NCC

        def cb(k):
            return NCL + (NCC - 1 - k) if k < NCC else NCL - 1 - (k - NCC)

        rot = 0
        for blk in range(cfg.NB):
            c0 = blk * 256
            for q, row in enumerate((1, 0, 7, 6)):
                kb.dma("sp", mod4[:, q, :], bcast_rows(MOD[i, row:row + 1, c0:c0 + 256], 128), [], ["mod4"])
            kb.dma("sp", dsk[:], bcast_rows(io["s5_d"][j, 0:1, c0:c0 + 256], 128), [], ["dsk"])
            for d in range(2):
                kb.dma("sp", W8[:, d], SW[j, d, blk], [], ["W8"])
            for ci, (row0, n, col0, isctx) in enumerate(ctiles):
                uk = "U32_%d" % ci
                kb.dma("sp", U32[ci][0:n], XR[row0:row0 + n * 8, c0:c0 + 256].rearrange("(c s) j -> c s j", s=8), [], [uk])
                scb = fview(mod4, (2 * isctx) * 256, [(0, 8), (1, 256)], pn=n)
                shb = fview(mod4, (2 * isctx + 1) * 256, [(0, 8), (1, 256)], pn=n)
                kb.tt("dve", U32[ci][0:n], U32[ci][0:n], scb, ALU.mult, [uk, "mod4"], [uk])
                kb.tt("dve", U32[ci][0:n], U32[ci][0:n], shb, ALU.add, [uk, "mod4"], [uk])
                ug = U8g[ci % 2]; ugk = "U8g%d" % (ci % 2)
                kb.cp("act", fview(ug, 0, [(16, 8), (128, 16), (1, 16)], pn=n),
                      fview(U32[ci], 0, [(256, 8), (16, 16), (1, 16)], pn=n), [uk], [ugk])
                for g4 in range(4):
                    px = pX[rot % 2]; pxk = "pX%d" % (rot % 2); rot += 1
                    for q in range(4):
                        g = g4 * 4 + q
                        kb.tr(px[:, q * 128:q * 128 + n], ug[0:n, g, :], idb[0:n, 0:n], [ugk, "idb"], [pxk])
                    kb.cp("dve" if g4 % 2 else "act", X8[:, g4 * 4:(g4 + 1) * 4, col0:col0 + n],
                          px[:, :].rearrange("p (g c) -> p g c", g=4)[:, :, 0:n], [pxk], ["X8"])
            zr = 0
            for d in range(2):
                for gl in range(GL):
                    pr = pZ[zr % 4]; prk = "pZ%d" % (zr % 4); zr += 1
                    pi = pZ[zr % 4]; pik = "pZ%d" % (zr % 4); zr += 1
                    for gh in range(2):
                        g = gh * 8 + gl
                        lo, hi = gh * 64, (gh + 1) * 64
                        kb.mm(pr[lo:hi, 0:NC], W8[:, d, gl, 0, lo:hi], X8[:, g, :], True, True, ["W8", "X8"], [prk])
                        kb.mm(pi[lo:hi, 0:NC], W8[:, d, gl, 1, lo:hi], X8[:, g, :], True, True, ["W8", "X8"], [pik])
                    kb.cp("act", Z[:, d, 0, gl, :], pr[:, 0:NC], [prk], ["Z%d" % d])
                    kb.cp("act" if d else "dve", Z[:, d, 1, gl, :], pi[:, 0:NC], [pik], ["Z%d" % d])
            NMAX = max(NCL, NCC) // 2
            ML = 4
            for d, RE in ((0, "dve"), (1, "pool")):
                zk, k1, k2 = "Z%d" % d, "t1_%d" % d, "t2_%d" % d
                segs = [(NCL, 1, NCC), (0, 1, NCL)] if d == 0 else [(NC - 1, -1, NCC), (NCL - 1, -1, NCL)]

                def cacc(dc0, dst_, sc0, sst, n, lvl, d=d, RE=RE, zk=zk, k1=k1, k2=k2):
                    if n <= 0:
                        return
                    ab = (((blk * 2 + d) * 5 + lvl) * 2) * 2 * GL
                    dst = fview(Z, d * DS + dc0, [(RS, 2), (NC, GL), (dst_, n)])
                    src = fview(Z, d * DS + sc0, [(RS, 2), (NC, GL), (sst, n)])
                    srw = fview(Z, d * DS + sc0 + RS, [(-RS, 2), (NC, GL), (sst, n)])
                    AR_ = fview(A8T, ab, [(GL, 2), (1, GL), (0, n)])
                    AI_ = fview(A8T, ab + 2 * GL, [(GL, 2), (1, GL), (0, n)])
                    T1 = fview(tq1[d], 0, [(GL * NMAX, 2), (NMAX, GL), (1, n)])
                    T2 = fview(tq2[d], 0, [(GL * NMAX, 2), (NMAX, GL), (1, n)])
                    kb.tt(RE, T1, src, AR_, ALU.mult, [zk, "A8T"], [k1])
                    kb.tt(RE, T2, srw, AI_, ALU.mult, [zk, "A8T"], [k2])
                    kb.tt(RE, T1, T1, T2, ALU.add, [k1, k2], [k1])
                    kb.tt(RE, dst, dst, T1, ALU.add, [zk, k1], [zk])

                def col(p, segs=segs):
                    (c0a, sga, la), (c0b, sgb, lb) = segs
                    return c0a + sga * p if p < la else c0b + sgb * (p - la)

                for lvl in range(ML):
                    h = 1 << lvl
                    for (sc0_, sg_, ln_) in segs:
                        cacc(sc0_ + sg_ * (2 * h - 1), sg_ * 2 * h, sc0_ + sg_ * (h - 1), sg_ * 2 * h, ln_ // (2 * h), lvl)
                BS = 1 << ML
                for q_ in range(1, NC // BS):
                    cacc(col(q_ * BS + BS - 1), 1, col(q_ * BS - 1), 1, 1, ML)
                for lvl in range(ML - 1, -1, -1):
                    h = 1 << lvl
                    (c0a, sga, la), (c0b, sgb, lb) = segs
                    cacc(c0a + sga * (2 * h + h - 1), sga * 2 * h, c0a + sga * (2 * h - 1), sga * 2 * h, la // (2 * h) - 1, lvl)
                    cacc(c0b + sgb * (h - 1), 1, c0a + sga * (la - 1), 1, 1, lvl)
                    cacc(c0b + sgb * (2 * h + h - 1), sgb * 2 * h, c0b + sgb * (2 * h - 1), sgb * 2 * h, lb // (2 * h) - 1, lvl)
            kb.cp("act", Sbf[:], Z[:], ["Z0", "Z1"], ["Sbf"])
            fw_p = [(1, NCL, 0), (0, 1, NC - 1), (NCL + 1, NC, NCL)]
            bw_p = [(NCL, NC - 1, NCL + 1), (NCL - 1, NCL, NCL), (0, NCL - 1, 1)]
            for gh in range(2):
                lo, hi = gh * 64, (gh + 1) * 64
                for gl in range(GL):
                    g = gh * 8 + gl
                    py = pY[g % 2]; pyk = "pY%d" % (g % 2)
                    kb.mm(py[:, 0:NC], W8[:, 0, gl, 4 + gh, :], X8[:, g, :], True, False, ["W8", "X8"], [pyk])
                    kb.mm(py[:, 0:NC], W8[:, 1, gl, 4 + gh, :], X8[:, g, :], False, False, ["W8", "X8"], [pyk])
                    pieces = [(0, p) for p in fw_p] + [(1, p) for p in bw_p]
                    pieces = [(d, p) for d, p in pieces if p[1] > p[0]]
                    for pi_, (d, (o0, o1, s0)) in enumerate(pieces):
                        for ri in range(2):
                            last = (pi_ == len(pieces) - 1) and ri == 1
                            kb.mm(py[:, o0:o1], W8[lo:hi, d, gl, 2 + ri, :], Sbf[lo:hi, d, ri, gl, s0:s0 + (o1 - o0)],
                                  False, last, ["W8", "Sbf"], [pyk])
                    kb.cp("act" if g % 2 else "dve", Y8s[:, g, :], py[:, 0:NC], [pyk], ["Y8s"])
            for ci, (row0, n, col0, isctx) in enumerate(ctiles):
                if isctx and not ctx_out:
                    continue
                uk = "U32_%d" % ci
                yt = ytm[ci % 2]; ytk = "ytm%d" % (ci % 2)
                for g4 in range(4):
                    px = pX[rot % 2]; pxk = "pX%d" % (rot % 2); rot += 1
                    for q in range(4):
                        g = g4 * 4 + q
                        kb.tr(px[0:n, q * 128:(q + 1) * 128], Y8s[:, g, col0:col0 + n], idb[:], ["Y8s", "idb"], [pxk])
                    kb.cp("act" if g4 % 2 else "dve", fview(yt, g4 * 64, [(16, 4), (256, 8), (1, 16)], pn=n),
                          fview(px, 0, [(128, 4), (16, 8), (1, 16)], pn=n), [pxk], [ytk])
                dkb = fview(dsk, 0, [(0, 8), (1, 256)], pn=n)
                kb.tt("dve", U32[ci][0:n], U32[ci][0:n], dkb, ALU.mult, [uk, "dsk"], [uk])
                kb.tt("dve", yt[0:n], yt[0:n], U32[ci][0:n], ALU.add, [ytk, uk], [ytk])
                z_ = zt[ci % 2]; zk = "zt%d" % (ci % 2)
                kb.act(z_[0:n], yt[0:n], AF.Gelu_apprx_tanh, [ytk], [zk])
                kb.dma("pool", ZS[row0:row0 + n * 8, c0:c0 + 256].rearrange("(c s) j -> c s j", s=8), z_[0:n], [zk], ["ZS"])
    f.barrier()


def adaln_gen(kb, es, io, scr):
    cfg, f = kb.cfg, kb.f
    D, NK, NT = cfg.D, cfg.NK, cfg.NT
    MOD = scr["MOD"]
    if True:
        sb = lambda n, s, dt=F32: kb.sb(es, n, s, dt)
        cond = sb("cond", [128, NK, 2]); cs = sb("cs", [128, NK, 2])
        kb.dma("sp", cond[:], io["cond"], [], ["cond"])
        kb.act(cs[:], cond[:], AF.Silu, ["cond"], ["cs"])
        aw = [sb("aw%d" % q, [128, NK, NT]) for q in range(2)]
        ab = [sb("ab%d" % q, [1, NT]) for q in range(2)]
        res = [sb("res%d" % q, [2, NT]) for q in range(2)]
        ones = sb("ones", [1, 2])
        kb.memset("pool", ones[:], 1.0, [], ["ones"])
        pa = [kb.ps(es, "pa%d" % q, [128, 512]) for q in range(2)]
        it = 0
        for i in range(cfg.DEPTH):
            for n in range(6 * D // NT):
                q = it % 2; it += 1
                kb.dma("sp", aw[q][:], io["ada_w"][i, :, n * NT:(n + 1) * NT].rearrange("(k p) c -> p k c", p=128), [], ["aw%d" % q])
                kb.dma("sp", ab[q][:], io["ada_b"][i, 0:1, n * NT:(n + 1) * NT], [], ["ab%d" % q])
                for k in range(NK):
                    kb.mm(pa[q][0:2, 0:NT], cs[:, k, :], aw[q][:, k, :], k == 0, False, ["cs", "aw%d" % q], ["pa%d" % q])
                kb.mm(pa[q][0:2, 0:NT], ones[0:1, 0:2], ab[q][0:1, :], False, True, ["ones", "ab%d" % q], ["pa%d" % q])
                v = (n * NT) // D
                off = n * NT - v * D
                kb.act(res[q][:], pa[q][0:2, 0:NT], AF.Identity, ["pa%d" % q], ["res%d" % q], bias=(1.0 if v in (1, 4) else 0.0))
                kb.dma("pool", MOD[i].rearrange("(w v) d -> w v d", w=2)[:, v, off:off + NT], res[q][:], ["res%d" % q], ["MOD"])
                yield


def prologue(kb, io, scr):
    cfg, f = kb.cfg, kb.f
    with ExitStack() as es_a:
        A8T = kb.sb(es_a, "A8Tg", [128, cfg.NB, 2, 5, 2, 2, 8])
        ga = adaln_gen(kb, es_a, io, scr)
        a_done = False
        try:
            next(ga)
        except StopIteration:
            a_done = True
        n_a = cfg.DEPTH * 6 * cfg.D // cfg.NT
        n_p = cfg.NS5 * (2 * (cfg.NB + 2) + 1)
        per = max(1, -(-n_a // max(1, n_p)))
        for j in range(cfg.NS5):
            with ExitStack() as es_j:
                for _ in s5_paramgen_gen(kb, es_j, j, io, scr["SW"], A8T, scr["A8D"]):
                    for _k in range(per):
                        if a_done:
                            break
                        try:
                            next(ga)
                        except StopIteration:
                            a_done = True
        if not a_done:
            for _ in ga:
                pass
    f.barrier()


def ln_tile(kb, eng2, t, tk, mv, st, sd, g_bc, b_bc, n=128):
    cfg = kb.cfg
    D = cfg.D
    nch = max(1, D // 512)
    w = D // nch
    for c in range(nch):
        kb.f.op("dve", lambda e, o=st[0:n, c, :], i_=t[0:n, c * w:(c + 1) * w]: e.bn_stats(out=o, in_=i_), [tk], [tk + "_st"])
    kb.f.op("dve", lambda e, o=mv[0:n, :], i_=st[0:n, :, :].rearrange("p c s -> p (c s)"): e.bn_aggr(out=o, in_=i_), [tk + "_st"], [tk + "_mv"])
    kb.ts("dve", sd[0:n, 0:1], mv[0:n, 1:2], LN_EPS, None, ALU.add, None, [tk + "_mv"], [tk + "_sd"])
    kb.act(sd[0:n, 0:1], sd[0:n, 0:1], AF.Sqrt, [tk + "_sd"], [tk + "_sd"])
    kb.f.op("dve", lambda e, o=sd[0:n, 1:2], i_=sd[0:n, 0:1]: e.reciprocal(out=o, in_=i_), [tk + "_sd"], [tk + "_sd"])
    kb.stt(sd[0:n, 2:3], mv[0:n, 0:1], -1.0, sd[0:n, 1:2], ALU.mult, ALU.mult, [tk + "_mv", tk + "_sd"], [tk + "_sd"])
    kb.act(t[0:n, :], t[0:n, :], AF.Identity, [tk, tk + "_sd"], [tk], scale=sd[0:n, 1:2], bias=sd[0:n, 2:3])
    kb.tt("dve", t[0:n, :], t[0:n, :], g_bc[0:n, :], ALU.mult, [tk, "lng"], [tk])
    kb.tt(eng2, t[0:n, :], t[0:n, :], b_bc[0:n, :], ALU.add, [tk, "lnb"], [tk])


def postnorm(kb, i, which, io, scr, with_ctx, mod_rows, dst_u, final_out=None):
    cfg, f = kb.cfg, kb.f
    D = cfg.D
    XR, MOD, TS = scr["XR"], scr["MOD"], scr["TS"]
    ntiles = cfg.NTT if with_ctx else cfg.NLT
    with ExitStack() as es:
        sb = lambda n, s, dt=F32: kb.sb(es, n, s, dt)
        lng = sb("lng", [128, D]); lnb = sb("lnb", [128, D])
        kb.dma("sp", lng[:], bcast_rows(io["ln_g"][2 * i + which:2 * i + which + 1, :], 128), [], ["lng"])
        kb.dma("sp", lnb[:], bcast_rows(io["ln_b"][2 * i + which:2 * i + which + 1, :], 128), [], ["lnb"])
        mods = None
        if mod_rows is not None:
            mods = [[sb("m%d_%d" % (a, b_), [128, D]) for b_ in range(2)] for a in range(2 if with_ctx else 1)]
            for a in range(len(mods)):
                for b_ in range(2):
                    kb.dma("sp", mods[a][b_][:], bcast_rows(MOD[i, 6 * a + mod_rows[b_]:6 * a + mod_rows[b_] + 1, :], 128), [], ["mods"])
        tb = [sb("tb%d" % q, [128, D]) for q in range(3)]
        ub = [sb("ub%d" % q, [128, D], BF16) for q in range(3)]
        st = [sb("st%d" % q, [128, max(1, D // 512), 6]) for q in range(3)]
        mv = [sb("mv%d" % q, [128, 2]) for q in range(3)]
        sd = [sb("sd%d" % q, [128, 4]) for q in range(3)]
        for tt in range(ntiles):
            q = tt % 3
            tk = "tb%d" % q
            rows = slice(tt * 128, (tt + 1) * 128)
            isctx = 1 if tt >= cfg.NLT else 0
            kb.dma("sp", tb[q][:], TS[rows, :], ["TS"], [tk])
            ln_tile(kb, "pool", tb[q], tk, mv[q], st[q], sd[q], lng, lnb)
            if final_out is not None:
                kb.out_stores.append(kb.dma("pool", final_out[rows, :], tb[q][:], [tk], ["OUTF"]))
            else:
                kb.dma("pool", XR[rows, :], tb[q][:], [tk], ["XR"])
            if mods is not None:
                m = mods[isctx]
                kb.tt("dve", tb[q][:], tb[q][:], m[0][:], ALU.mult, [tk, "mods"], [tk])
                kb.tt("dve", ub[q][:], tb[q][:], m[1][:], ALU.add, [tk, "mods"], ["ub%d" % q])
                kb.dma("pool", dst_u[rows, :], ub[q][:], ["ub%d" % q], ["U2"])
    f.barrier()


def glu_stage(kb, i, j, io, scr, with_ctx):
    cfg, f = kb.cfg, kb.f
    D, NK, NT = cfg.D, cfg.NK, cfg.NT
    XR, MOD, TS, ZS = scr["XR"], scr["MOD"], scr["TS"], scr["ZS"]
    ntiles = cfg.NTT if with_ctx else cfg.NLT
    with ExitStack() as es:
        sb = lambda n, s, dt=F32: kb.sb(es, n, s, dt)
        idb = sb("idb", [128, 128], BF16); idf = sb("idf", [128, 128])
        kb.dma("sp", idf[:], io["ident"], [], ["idf"])
        kb.cp("dve", idb[:], idf[:], ["idf"], ["idb"])
        zT = sb("zT", [128, NK, ntiles * 128], BF16)
        zin = [sb("zin%d" % q, [128, D], BF16) for q in range(2)]
        pT = [kb.ps(es, "pT%d" % q, [128, 512], BF16) for q in range(2)]
        r = 0
        for tt in range(ntiles):
            q = tt % 2
            kb.dma("sp", zin[q][:], ZS[tt * 128:(tt + 1) * 128, :], ["ZS"], ["zin%d" % q])
            for k4 in range(0, NK, 4):
                nk = min(4, NK - k4)
                p = pT[r % 2]; pk = "pT%d" % (r % 2); r += 1
                for a in range(nk):
                    kb.tr(p[:, a * 128:(a + 1) * 128], zin[q][:, (k4 + a) * 128:(k4 + a + 1) * 128], idb[:], ["zin%d" % q, "idb"], [pk])
                kb.cp("act" if (r % 2) else "dve", zT[:, k4:k4 + nk, tt * 128:(tt + 1) * 128],
                      p[:, 0:nk * 128].rearrange("p (a c) -> p a c", a=nk), [pk], ["zT"])
        KH = max(1, NK // 2)
        wst = [sb("wst%d" % q, [128, KH, NT]) for q in range(2)]
        wb = [sb("wb%d" % q, [128, NK, NT], BF16) for q in range(2)]
        bvg = sb("bvg", [128, 2, NT]); g1t = sb("g1t", [128, 2, NT])
        xs = [sb("xs%d" % q, [128, NT]) for q in range(2)]
        sg = [sb("sg%d" % q, [128, NT]) for q in range(2)]
        vv = [sb("vv%d" % q, [128, NT]) for q in range(2)]
        pv = [kb.ps(es, "pv%d" % q, [128, 512]) for q in range(2)]
        pg = [kb.ps(es, "pg%d" % q, [128, 512]) for q in range(2)]
        ws = 0
        for np_ in range(D // NT):
            cs_ = slice(np_ * NT, (np_ + 1) * NT)
            for vg in range(2):
                col0 = vg * D + np_ * NT
                for kh in range(0, NK, KH):
                    w_ = wst[ws % 2]; wk = "wst%d" % (ws % 2); ws += 1
                    kb.dma("sp", w_[:], io["s5_w_glu"][j, kh * 128:(kh + KH) * 128, col0:col0 + NT].rearrange("(k p) c -> p k c", p=128), [], [wk])
                    kb.cp("act", wb[vg][:, kh:kh + KH, :], w_[:], [wk], ["wb%d" % vg])
                kb.dma("sp", bvg[:, vg, :], bcast_rows(io["s5_b_glu"][j, 0:1, col0:col0 + NT], 128), [], ["bvg"])
            for a in range(2 if with_ctx else 1):
                kb.dma("sp", g1t[:, a, :], bcast_rows(MOD[i, 6 * a + 2:6 * a + 3, cs_], 128), [], ["g1t"])
            for tt in range(ntiles):
                q = tt % 2
                isctx = 1 if tt >= cfg.NLT else 0
                rows = slice(tt * 128, (tt + 1) * 128)
                kb.dma("sp", xs[q][:], XR[rows, cs_], [], ["xs%d" % q])
                for k in range(NK):
                    kb.mm(pv[q][:, 0:NT], zT[:, k, rows], wb[0][:, k, :], k == 0, k == NK - 1, ["zT", "wb0"], ["pv%d" % q])
                for k in range(NK):
                    kb.mm(pg[q][:, 0:NT], zT[:, k, rows], wb[1][:, k, :], k == 0, k == NK - 1, ["zT", "wb1"], ["pg%d" % q])
                kb.tt("dve", sg[q][:], pg[q][:, 0:NT], bvg[:, 1, :], ALU.add, ["pg%d" % q, "bvg"], ["sg%d" % q])
                kb.act(sg[q][:], sg[q][:], AF.Sigmoid, ["sg%d" % q], ["sg%d" % q])
                kb.tt("dve", vv[q][:], pv[q][:, 0:NT], bvg[:, 0, :], ALU.add, ["pv%d" % q, "bvg"], ["vv%d" % q])
                kb.tt("pool", vv[q][:], vv[q][:], sg[q][:], ALU.mult, ["vv%d" % q, "sg%d" % q], ["vv%d" % q])
                kb.tt("pool", vv[q][:], vv[q][:], g1t[:, isctx, :], ALU.mult, ["vv%d" % q, "g1t"], ["vv%d" % q])
                kb.stt(vv[q][:], xs[q][:], cfg.ALPHA, vv[q][:], ALU.mult, ALU.add, ["xs%d" % q, "vv%d" % q], ["vv%d" % q])
                kb.dma("pool", TS[rows, cs_], vv[q][:], ["vv%d" % q], ["TS"])
    f.barrier()


def conv_layer(kb, i, j, io, scr, with_ctx):
    cfg, f = kb.cfg, kb.f
    D, NK, NT, L, LC, GW = cfg.D, cfg.NK, cfg.NT, cfg.L, cfg.LC, cfg.GW
    XR, MOD, TS, CV = scr["XR"], scr["MOD"], scr["TS"], scr["CV"]
    ntiles = cfg.NTT if with_ctx else cfg.NLT
    NLT = cfg.NLT
    NTOK = ntiles * 128
    with ExitStack() as esl:
        sbl = lambda n, s, dt=F32: kb.sb(esl, n, s, dt)
        idf = sbl("idf", [128, 128]); idb = sbl("idb", [128, 128], BF16)
        kb.dma("sp", idf[:], io["ident"], [], ["idf"])
        kb.cp("dve", idb[:], idf[:], ["idf"], ["idb"])
        uT = sbl("uT", [128, NK, NTOK], BF16)
        with ExitStack() as es:
            sb = lambda n, s, dt=F32: kb.sb(es, n, s, dt)
            mods = [[sb("m%d_%d" % (a, b_), [128, D]) for b_ in range(2)] for a in range(2 if with_ctx else 1)]
            for a in range(len(mods)):
                for b_, row in enumerate((1, 0)):
                    kb.dma("sp", mods[a][b_][:], bcast_rows(MOD[i, 6 * a + row:6 * a + row + 1, :], 128), [], ["mods"])
            xt = [sb("xt%d" % q, [128, D]) for q in range(2)]
            ub = [sb("ub%d" % q, [128, D], BF16) for q in range(2)]
            pT = [kb.ps(es, "pT%d" % q, [128, 512], BF16) for q in range(2)]
            r = 0
            for tt in range(ntiles):
                q = tt % 2
                isctx = 1 if tt >= NLT else 0
                kb.dma("sp", xt[q][:], XR[tt * 128:(tt + 1) * 128, :], [], ["xt%d" % q])
                kb.tt("dve", xt[q][:], xt[q][:], mods[isctx][0][:], ALU.mult, ["xt%d" % q, "mods"], ["xt%d" % q])
                kb.tt("pool", ub[q][:], xt[q][:], mods[isctx][1][:], ALU.add, ["xt%d" % q, "mods"], ["ub%d" % q])
                for k4 in range(0, NK, 4):
                    nk = min(4, NK - k4)
                    p = pT[r % 2]; pk = "pT%d" % (r % 2); r += 1
                    for a in range(nk):
                        kb.tr(p[:, a * 128:(a + 1) * 128], ub[q][:, (k4 + a) * 128:(k4 + a + 1) * 128], idb[:], ["ub%d" % q, "idb"], [pk])
                    kb.cp("act" if (r % 2) else "dve", uT[:, k4:k4 + nk, tt * 128:(tt + 1) * 128],
                          p[:, 0:nk * 128].rearrange("p (a c) -> p a c", a=nk), [pk], ["uT"])
        f.barrier()
        with ExitStack() as es:
            sb = lambda n, s, dt=F32: kb.sb(es, n, s, dt)
            rows = L // GW
            WL = (rows + 30) * GW
            hl = sb("hl", [128, WL], BF16)
            hc = sb("hc", [128, LC + 30], BF16)
            kb.memset("pool", hl[:], 0.0, [], ["hl"])
            kb.memset("pool", hc[:], 0.0, [], ["hc"])
            bp = sb("bp", [128, 2 * NK]); wdw = sb("wdw", [128, NK, 31]); bdw = sb("bdw", [128, NK])
            kb.dma("sp", bp[:], io["cv_b_pw1"][j], [], ["bp"])
            kb.dma("sp", wdw[:], io["cv_w_dw"][j], [], ["wdw"])
            kb.dma("sp", bdw[:], io["cv_b_dw"][j], [], ["bdw"])
            DG = [sb("DG%d" % q, [128, 31, 128], BF16) for q in range(2)]
            wst = [sb("wst%d" % q, [128, NK, 128]) for q in range(2)]
            wab = [[sb("wab%d_%d" % (q, a), [128, NK, 128], BF16) for a in range(2)] for q in range(2)]
            sig = [sb("sig%d" % q, [128, NT]) for q in range(2)]
            cvs = [sb("cvs%d" % q, [128, NT]) for q in range(2)]
            cvt = [sb("cvt%d" % q, [128, NT // 128, 128]) for q in range(2)]
            pa = [kb.ps(es, "pa%d" % q, [128, 512]) for q in range(2)]
            pg = [kb.ps(es, "pg%d" % q, [128, 512]) for q in range(2)]
            pc = [kb.ps(es, "pc%d" % q, [128, 512]) for q in range(2)]
            pt = [kb.ps(es, "pt%d" % q, [128, 512]) for q in range(2)]
            blocks = [(tb * NT, NT, 0, 15 * GW + tb * NT) for tb in range(L // NT)]
            if with_ctx:
                blocks += [(L + tb * NT, min(NT, LC - tb * NT), 1, 15 + tb * NT) for tb in range((LC + NT - 1) // NT)]
            ws = 0; it = 0
            for cc in range(NK):
                cq = cc % 2
                for a in range(2):
                    w_ = wst[ws % 2]; wk = "wst%d" % (ws % 2); ws += 1
                    c0 = a * D + cc * 128
                    kb.dma("sp", w_[:], io["cv_w_pw1"][j, :, c0:c0 + 128].rearrange("(k p) c -> p k c", p=128), [], [wk])
                    kb.cp("act", wab[cq][a][:], w_[:], [wk], ["wab%d_%d" % (cq, a)])
                for k in range(31):
                    kb.ts("dve" if k % 2 else "pool", DG[cq][:, k, :], idf[:], wdw[:, cc, k:k + 1], None, ALU.mult, None, ["idf", "wdw"], ["DG%d" % cq])
                for (t0, n, isctx, hoff) in blocks:
                    q = it % 2; it += 1
                    for k in range(NK):
                        kb.mm(pa[q][:, 0:n], wab[cq][0][:, k, :], uT[:, k, t0:t0 + n], k == 0, k == NK - 1, ["wab%d_0" % cq, "uT"], ["pa%d" % q])
                    for k in range(NK):
                        kb.mm(pg[q][:, 0:n], wab[cq][1][:, k, :], uT[:, k, t0:t0 + n], k == 0, k == NK - 1, ["wab%d_1" % cq, "uT"], ["pg%d" % q])
                    kb.act(sig[q][:, 0:n], pg[q][:, 0:n], AF.Sigmoid, ["pg%d" % q, "bp"], ["sig%d" % q], bias=bp[:, NK + cc:NK + cc + 1])
                    hbuf, hk = (hc, "hc") if isctx else (hl, "hl")
                    kb.stt(hbuf[:, hoff:hoff + n], pa[q][:, 0:n], bp[:, cc:cc + 1], sig[q][:, 0:n], ALU.add, ALU.mult, ["pa%d" % q, "bp", "sig%d" % q], [hk])
                for (t0, n, isctx, hoff) in blocks:
                    q = it % 2; it += 1
                    hbuf, hk = (hc, "hc") if isctx else (hl, "hl")
                    step = 1 if isctx else GW
                    base = (hoff - 15) if isctx else (hoff - 15 * GW)
                    for k in range(31):
                        kb.mm(pc[q][:, 0:n], DG[cq][:, k, :], hbuf[:, base + k * step:base + k * step + n], k == 0, k == 30, ["DG%d" % cq, hk], ["pc%d" % q])
                    kb.act(cvs[q][:, 0:n], pc[q][:, 0:n], AF.Identity, ["pc%d" % q, "bdw"], ["cvs%d" % q], bias=bdw[:, cc:cc + 1])
                    na = n // 128
                    for a in range(na):
                        kb.tr(pt[q][:, a * 128:(a + 1) * 128], cvs[q][:, a * 128:(a + 1) * 128], idf[:], ["cvs%d" % q, "idf"], ["pt%d" % q])
                    kb.cp("dve", cvt[q][:, 0:na, :], pt[q][:, 0:n].rearrange("p (a c) -> p a c", a=na), ["pt%d" % q], ["cvt%d" % q])
                    kb.dma("pool", CV[t0:t0 + n, cc * 128:(cc + 1) * 128].rearrange("(a p) c -> p a c", p=128), cvt[q][:, 0:na, :], ["cvt%d" % q], ["CV"])
        f.barrier()
    with ExitStack() as es:
        sb = lambda n, s, dt=F32: kb.sb(es, n, s, dt)
        idf = sb("idf", [128, 128]); idb = sb("idb", [128, 128], BF16)
        kb.dma("sp", idf[:], io["ident"], [], ["idf"])
        kb.cp("dve", idb[:], idf[:], ["idf"], ["idb"])
        lng = sb("lng", [128, D]); lnb = sb("lnb", [128, D]); b2 = sb("b2", [128, D])
        kb.dma("sp", lng[:], bcast_rows(io["cv_ln_g"][j:j + 1, :], 128), [], ["lng"])
        kb.dma("sp", lnb[:], bcast_rows(io["cv_ln_b"][j:j + 1, :], 128), [], ["lnb"])
        kb.dma("sp", b2[:], bcast_rows(io["cv_b_pw2"][j, 0:1, :], 128), [], ["b2"])
        g1t = [sb("g1t%d" % a, [128, D]) for a in range(2 if with_ctx else 1)]
        for a in range(len(g1t)):
            kb.dma("sp", g1t[a][:], bcast_rows(MOD[i, 6 * a + 2:6 * a + 3, :], 128), [], ["g1t"])
        W2 = sb("W2", [128, NK, D], BF16)
        wst = [sb("wst%d" % q, [128, D]) for q in range(2)]
        for k in range(NK):
            kb.dma("sp", wst[k % 2][:], io["cv_w_pw2"][j, k * 128:(k + 1) * 128, :], [], ["wst%d" % (k % 2)])
            kb.cp("act", W2[:, k, :], wst[k % 2][:], ["wst%d" % (k % 2)], ["W2"])
        cvt = [sb("cvt%d" % q, [128, D]) for q in range(3)]
        sbf = [sb("sbf%d" % q, [128, D], BF16) for q in range(3)]
        sT = [sb("sT%d" % q, [128, NK, 128], BF16) for q in range(3)]
        xt = [sb("xt%d" % q, [128, D]) for q in range(2)]
        yt = [sb("yt%d" % q, [128, D]) for q in range(2)]
        st = [sb("st%d" % q, [128, max(1, D // 512), 6]) for q in range(3)]
        mv = [sb("mv%d" % q, [128, 2]) for q in range(3)]
        sd = [sb("sd%d" % q, [128, 4]) for q in range(3)]
        pT = [kb.ps(es, "pT%d" % q, [128, 512], BF16) for q in range(2)]
        po = [kb.ps(es, "po%d" % q, [128, 512]) for q in range(2)]
        r = 0; pr = 0
        for tt in range(ntiles):
            q = tt % 3
            q2 = tt % 2
            isctx = 1 if tt >= NLT else 0
            rows_ = slice(tt * 128, (tt + 1) * 128)
            kb.dma("sp", cvt[q][:], CV[rows_, :], ["CV"], ["cvt%d" % q])
            kb.dma("sp", xt[q2][:], XR[rows_, :], [], ["xt%d" % q2])
            ln_tile(kb, "pool", cvt[q], "cvt%d" % q, mv[q], st[q], sd[q], lng, lnb)
            kb.act(sbf[q][:], cvt[q][:], AF.Silu, ["cvt%d" % q], ["sbf%d" % q])
            for k4 in range(0, NK, 4):
                nk = min(4, NK - k4)
                p = pT[r % 2]; pk = "pT%d" % (r % 2); r += 1
                for a in range(nk):
                    kb.tr(p[:, a * 128:(a + 1) * 128], sbf[q][:, (k4 + a) * 128:(k4 + a + 1) * 128], idb[:], ["sbf%d" % q, "idb"], [pk])
                kb.cp("act" if (r % 2) else "dve", sT[q][:, k4:k4 + nk, :], p[:, 0:nk * 128].rearrange("p (a c) -> p a c", a=nk), [pk], ["sT%d" % q])
            for nt in range(D // NT):
                cs_ = slice(nt * NT, (nt + 1) * NT)
                pq = pr % 2; pr += 1
                for k in range(NK):
                    kb.mm(po[pq][:, 0:NT], sT[q][:, k, :], W2[:, k, cs_], k == 0, k == NK - 1, ["sT%d" % q, "W2"], ["po%d" % pq])
                kb.tt("dve", yt[q2][:, cs_], po[pq][:, 0:NT], b2[:, cs_], ALU.add, ["po%d" % pq, "b2"], ["yt%d" % q2])
            kb.tt("pool", yt[q2][:], yt[q2][:], g1t[isctx][:], ALU.mult, ["yt%d" % q2, "g1t"], ["yt%d" % q2])
            kb.stt(yt[q2][:], xt[q2][:], cfg.ALPHA, yt[q2][:], ALU.mult, ALU.add, ["xt%d" % q2, "yt%d" % q2], ["yt%d" % q2])
            kb.dma("pool", TS[rows_, :], yt[q2][:], ["yt%d" % q2], ["TS"])
    f.barrier()
    postnorm(kb, i, 0, io, scr, with_ctx, (4, 3), scr["U2"])


def declare_io(kb):
    cfg = kb.cfg
    D, L, LC, DEPTH, NK, NB = cfg.D, cfg.L, cfg.LC, cfg.DEPTH, cfg.NK, cfg.NB
    NS5, NCV, E, F = cfg.NS5, cfg.NCV, cfg.E, cfg.F
    io = {}
    io["x"] = kb.din("x", [L, D]); io["ctx"] = kb.din("ctx", [LC, D])
    io["cond"] = kb.din("cond", [128, NK, 2])
    io["ada_w"] = kb.din("ada_w", [DEPTH, D, 6 * D]); io["ada_b"] = kb.din("ada_b", [DEPTH, 1, 6 * D])
    io["ln_g"] = kb.din("ln_g", [DEPTH * 2, D]); io["ln_b"] = kb.din("ln_b", [DEPTH * 2, D])
    io["ident"] = kb.din("ident", [128, 128])
    io["iota_f"] = kb.din("iota_f", [128, cfg.TALL]); io["tokid"] = kb.din("tokid", [128, 18])
    io["sel"] = kb.din("sel", [16, 16, 128])
    io["shift"] = kb.din("shift", [cfg.CAPC, 128 // cfg.CAPC, 128])
    io["s5_kt"] = kb.din("s5_kt", [128, 2, 4, 2, 8]); io["s5_mask"] = kb.din("s5_mask", [128, 2, 128])
    io["s5_lam"] = kb.din("s5_lam", [NS5, 2, NB, 128, 3, 8])
    io["s5_B"] = kb.din("s5_B", [NS5, 2, NB, 128, 2, 8, 16]); io["s5_C"] = kb.din("s5_C", [NS5, 2, NB, 128, 2, 8, 16])
    io["s5_d"] = kb.din("s5_d", [NS5, 1, D])
    io["s5_w_glu"] = kb.din("s5_w_glu", [NS5, D, 2 * D]); io["s5_b_glu"] = kb.din("s5_b_glu", [NS5, 1, 2 * D])
    nc_ = max(NCV, 1)
    io["cv_w_pw1"] = kb.din("cv_w_pw1", [nc_, D, 2 * D]); io["cv_b_pw1"] = kb.din("cv_b_pw1", [nc_, 128, 2 * NK])
    io["cv_w_dw"] = kb.din("cv_w_dw", [nc_, 128, NK, 31]); io["cv_b_dw"] = kb.din("cv_b_dw", [nc_, 128, NK])
    io["cv_ln_g"] = kb.din("cv_ln_g", [nc_, D]); io["cv_ln_b"] = kb.din("cv_ln_b", [nc_, D])
    io["cv_w_pw2"] = kb.din("cv_w_pw2", [nc_, D, D]); io["cv_b_pw2"] = kb.din("cv_b_pw2", [nc_, 1, D])
    io["moe_w_router"] = kb.din("moe_w_router", [DEPTH, D, E])
    io["moe_w_in"] = kb.din("moe_w_in", [DEPTH, E, D, 2 * F]); io["moe_w_out"] = kb.din("moe_w_out", [DEPTH, E, F, D])
    io["out"] = kb.nc.dram_tensor("out", [L, D], F32, kind="ExternalOutput").ap()
    return io


def declare_scratch(kb):
    cfg = kb.cfg
    D, TALL = cfg.D, cfg.TALL
    scr = {}
    scr["XR"] = kb.dscr("XR", [TALL, D])
    scr["TS"] = kb.dscr("TS", [TALL, D])
    scr["MOD"] = kb.dscr("MOD", [cfg.DEPTH, 12, D])
    scr["ZS"] = kb.dscr("ZS", [TALL, D], BF16)
    scr["U2"] = kb.dscr("U2", [TALL, D], BF16)
    scr["CV"] = kb.dscr("CV", [TALL, D])
    scr["SW"] = kb.dscr("SW", [cfg.NS5, 2, cfg.NB, 128, 8, 6, 128], BF16)
    scr["A8D"] = kb.dscr("A8D", [cfg.NS5, 128, cfg.NB * 2 * 5 * 2 * 2 * 8])
    scr["YG"] = kb.dscr("YG", [cfg.E, cfg.CAPL + cfg.CAPC, D], BF16)
    return scr


def build(cfg, debug_outs=(), stop_after=None):
    kb = KB(cfg, debug_outs)
    io = declare_io(kb)
    scr = declare_scratch(kb)
    f = kb.f
    kb.dma("sp", scr["XR"][0:cfg.L, :], io["x"], [], ["XR"])
    kb.dma("sp", scr["XR"][cfg.L:cfg.TALL, :], io["ctx"], [], ["XR"])
    prologue(kb, io, scr)
    done = False
    for i in range(cfg.DEPTH):
        is_s5 = (i % 2) == 0
        j = i // 2
        ctx_out = any((k % 2) == 0 for k in range(i + 1, cfg.DEPTH))
        last = i == cfg.DEPTH - 1
        if is_s5:
            s5_layer(kb, i, j, io, scr, ctx_out)
            if stop_after == ("s5", i):
                break
            glu_stage(kb, i, j, io, scr, ctx_out)
            if stop_after == ("glu", i):
                break
            postnorm(kb, i, 0, io, scr, ctx_out, (4, 3), scr["U2"])
        else:
            conv_layer(kb, i, j, io, scr, ctx_out)
        if stop_after == ("mix", i):
            break
        moe_layer(kb, i, io, scr, ctx_out)
        if stop_after == ("moe", i):
            break
        postnorm(kb, i, 1, io, scr, ctx_out, None, None, final_out=io["out"] if last else None)
    if not kb.out_stores:
        with ExitStack() as es:
            t = kb.sb(es, "dbg", [128, cfg.D])
            kb.dma("sp", t[:], scr["XR"][0:128, :], ["XR"], ["dbg"])
            kb.out_stores.append(kb.dma("pool", io["out"][0:128, :], t[:], ["dbg"], ["OUTF"]))
    f.barrier()
    f.emit(final_wait_ops=kb.out_stores)
    return kb.nc


def host_consts(cfg):
    c = {}
    c["ident"] = np.eye(128, dtype=np.float32)
    c["iota_f"] = np.tile(np.arange(cfg.TALL, dtype=np.float32)[None, :], (128, 1))
    c["tokid"] = (np.arange(128, dtype=np.float32)[:, None] + 128.0 * np.arange(18, dtype=np.float32)[None, :]).astype(np.float32)
    sel = np.zeros((16, 16, 128), np.float32)
    for e_ in range(16):
        sel[e_, e_, :] = 1.0
    c["sel"] = sel
    ep = 128 // cfg.CAPC
    sh = np.zeros((cfg.CAPC, ep, 128), np.float32)
    for e4 in range(ep):
        for s_ in range(cfg.CAPC):
            sh[s_, e4, e4 * cfg.CAPC + s_] = 1.0
    c["shift"] = sh
    kt = np.zeros((2, 4, 8), np.float64)
    idx = np.arange(8)
    kt[0, 0] = idx + 1; kt[0, 1] = -(idx + 1); kt[0, 2] = 7 - idx
    kt[1, 0] = 8 - idx; kt[1, 1] = idx - 8; kt[1, 2] = idx
    kt[:, 3, 0] = 1; kt[:, 3, 1] = 8
    ktt = np.stack([kt, kt / (2.0 * math.pi)], axis=2)
    c["s5_kt"] = np.ascontiguousarray(np.broadcast_to(ktt[None], (128, 2, 4, 2, 8))).astype(np.float32)
    s_idx = np.arange(128) // 16
    mf = (s_idx[None, :] >= s_idx[:, None]).astype(np.float32)
    mb = (s_idx[None, :] <= s_idx[:, None]).astype(np.float32)
    c["s5_mask"] = np.ascontiguousarray(np.stack([mf, mb], axis=1))
    return c


def host_layout(cfg, inp, b):
    D, NK, NB, NS5, NCV = cfg.D, cfg.NK, cfg.NB, cfg.NS5, cfg.NCV
    f32 = np.float32
    m = {}
    m["x"] = np.ascontiguousarray(inp["x"][b], f32)
    m["ctx"] = np.ascontiguousarray(inp["ctx"][b], f32)
    cond = np.stack([np.asarray(inp["c"][b]).reshape(NK, 128).T, np.asarray(inp["c_ctx"]).reshape(NK, 128).T], axis=2)
    m["cond"] = np.ascontiguousarray(cond, f32)
    m["ada_w"] = np.asarray(inp["ada_w"], f32)
    m["ada_b"] = np.asarray(inp["ada_b"], f32)[:, None, :]
    m["ln_g"] = np.asarray(inp["ln_g"], f32).reshape(-1, D)
    m["ln_b"] = np.asarray(inp["ln_b"], f32).reshape(-1, D)

    def glay(a):
        a = np.asarray(a, f32).reshape(NS5, 2, NB, 2, 8, 64)
        return a.transpose(0, 1, 2, 3, 5, 4).reshape(NS5, 2, NB, 128, 8)
    ldt = np.broadcast_to(np.asarray(inp["s5_log_dt"], f32)[..., None], np.asarray(inp["s5_a_re"]).shape)
    m["s5_lam"] = np.ascontiguousarray(np.stack([glay(inp["s5_a_re"]), glay(inp["s5_a_im"]), glay(ldt)], axis=4))

    def blay(re, im):
        out = []
        for a in (re, im):
            a = np.asarray(a, f32).reshape(NS5, 2, NB, 2, 8, 64, 16)
            out.append(a.transpose(0, 1, 2, 3, 5, 4, 6).reshape(NS5, 2, NB, 128, 8, 16))
        return np.ascontiguousarray(np.stack(out, axis=4))
    m["s5_B"] = blay(inp["s5_b_re"], inp["s5_b_im"])
    cre = np.asarray(inp["s5_c_re"], f32).transpose(0, 1, 2, 4, 3)
    cim = np.asarray(inp["s5_c_im"], f32).transpose(0, 1, 2, 4, 3)
    m["s5_C"] = blay(cre, cim)
    m["s5_d"] = np.asarray(inp["s5_d"], f32)[:, None, :]
    m["s5_w_glu"] = np.asarray(inp["s5_w_glu"], f32)
    m["s5_b_glu"] = np.asarray(inp["s5_b_glu"], f32)[:, None, :]
    n_ = max(NCV, 1)

    def pad0(a, shape):
        a = np.asarray(a, f32)
        if a.shape[0] == 0:
            return np.zeros(shape, f32)
        return np.ascontiguousarray(a.reshape(shape))
    m["cv_w_pw1"] = pad0(inp["cv_w_pw1"], (n_, D, 2 * D))
    bp = np.asarray(inp["cv_b_pw1"], f32)
    m["cv_b_pw1"] = np.ascontiguousarray(bp.reshape(-1, 2 * NK, 128).transpose(0, 2, 1)) if bp.shape[0] else np.zeros((n_, 128, 2 * NK), f32)
    wd = np.asarray(inp["cv_w_dw"], f32)
    m["cv_w_dw"] = np.ascontiguousarray(wd.reshape(-1, 31, NK, 128).transpose(0, 3, 2, 1)) if wd.shape[0] else np.zeros((n_, 128, NK, 31), f32)
    bd = np.asarray(inp["cv_b_dw"], f32)
    m["cv_b_dw"] = np.ascontiguousarray(bd.reshape(-1, NK, 128).transpose(0, 2, 1)) if bd.shape[0] else np.zeros((n_, 128, NK), f32)
    m["cv_ln_g"] = pad0(inp["cv_ln_g"], (n_, D)); m["cv_ln_b"] = pad0(inp["cv_ln_b"], (n_, D))
    m["cv_w_pw2"] = pad0(inp["cv_w_pw2"], (n_, D, D)); m["cv_b_pw2"] = pad0(inp["cv_b_pw2"], (n_, 1, D))
    m["moe_w_router"] = np.asarray(inp["moe_w_router"], f32)
    m["moe_w_in"] = np.asarray(inp["moe_w_in"], f32)
    m["moe_w_out"] = np.asarray(inp["moe_w_out"], f32)
    m.update(host_consts(cfg))
    return m


def moe_layer(kb, i, io, scr, with_ctx):
    cfg, f = kb.cfg, kb.f
    D, NK, NT, E, F, NF, L, LC = cfg.D, cfg.NK, cfg.NT, cfg.E, cfg.F, cfg.NF, cfg.L, cfg.LC
    XR, MOD, TS, U2, YG = scr["XR"], scr["MOD"], scr["TS"], scr["U2"], scr["YG"]
    CAPL, CAPC = cfg.CAPL, (cfg.CAPC if with_ctx else 0)
    NSL = CAPL + CAPC
    ntiles = cfg.NTT if with_ctx else cfg.NLT
    NLT = cfg.NLT
    sets = [(0, L, CAPL, 0, 0)]
    if with_ctx:
        sets.append((L, LC, CAPC, CAPL, 1))
    stiles = [(s * 128, min(128, CAPL - s * 128), 0) for s in range((CAPL + 127) // 128)]
    if with_ctx:
        stiles.append((CAPL, CAPC, 1))
    NST = len(stiles)
    with ExitStack() as esl:
        sbl = lambda n, s, dt=F32: kb.sb(esl, n, s, dt)
        IDXF = sbl("IDXF", [16, NSL]); GAT = sbl("GAT", [16, NSL])
        IDXT = sbl("IDXT", [128, NST, 16]); GT = sbl("GT", [128, NST, 16])
        idf = sbl("idf", [128, 128]); idb = sbl("idb", [128, 128], BF16)
        tokid = sbl("tokid", [128, 16 + 2])
        kb.dma("sp", idf[:], io["ident"], [], ["idf"])
        kb.cp("dve", idb[:], idf[:], ["idf"], ["idb"])
        kb.dma("sp", tokid[:], io["tokid"], [], ["tokid"])
        esu = ExitStack()
        U2TM = kb.sb(esu, "U2TM", [128, ntiles, D], BF16)
        with ExitStack() as es:
            sb = lambda n, s, dt=F32: kb.sb(es, n, s, dt)
            wrf = sb("wrf", [128, NK, E]); wrb = sb("wrb", [128, NK, E], BF16)
            kb.dma("sp", wrf[:], io["moe_w_router"][i].rearrange("(k p) e -> p k e", p=128), [], ["wrf"])
            kb.cp("dve", wrb[:], wrf[:], ["wrf"], ["wrb"])
            AFFT = sb("AFFT", [16, L + LC]); WK = sb("WK", [16, L])
            IDXU = sb("IDXU", [16, NSL], U32)
            uT = [sb("uT%d" % q, [128, NK, 128], BF16) for q in range(2)]
            lg = [sb("lg%d" % q, [128, E]) for q in range(2)]
            sm = [sb("smx%d" % q, [128, 4]) for q in range(2)]
            pT = [kb.ps(es, "pT%d" % q, [128, 512], BF16) for q in range(2)]
            pL = [kb.ps(es, "pL%d" % q, [128, 512]) for q in range(2)]
            pA = [kb.ps(es, "pA%d" % q, [128, 512]) for q in range(2)]
            r = 0
            for tt in range(ntiles):
                q = tt % 2
                uk = "U2TM%d" % tt
                kb.dma("sp", U2TM[:, tt, :], U2[tt * 128:(tt + 1) * 128, :], ["U2"], [uk])
                for k4 in range(0, NK, 4):
                    nk = min(4, NK - k4)
                    p = pT[r % 2]; pk = "pT%d" % (r % 2); r += 1
                    for a in range(nk):
                        kb.tr(p[:, a * 128:(a + 1) * 128], U2TM[:, tt, (k4 + a) * 128:(k4 + a + 1) * 128], idb[:], [uk, "idb"], [pk])
                    kb.cp("act" if (r % 2) else "dve", uT[q][:, k4:k4 + nk, :], p[:, 0:nk * 128].rearrange("p (a c) -> p a c", a=nk), [pk], ["uT%d" % q])
                for k in range(NK):
                    kb.mm(pL[q][:, 0:E], uT[q][:, k, :], wrb[:, k, :], k == 0, k == NK - 1, ["uT%d" % q, "wrb"], ["pL%d" % q])
                lk, sk = "lg%d" % q, "smx%d" % q
                kb.f.op("dve", lambda e, o=sm[q][:, 0:1], i_=pL[q][:, 0:E]: e.tensor_reduce(out=o, in_=i_, axis=mybir.AxisListType.X, op=ALU.max), ["pL%d" % q], [sk])
                kb.ts("dve", sm[q][:, 1:2], sm[q][:, 0:1], -1.0, None, ALU.mult, None, [sk], [sk])
                kb.act(lg[q][:], pL[q][:, 0:E], AF.Exp, ["pL%d" % q, sk], [lk], bias=sm[q][:, 1:2])
                kb.f.op("dve", lambda e, o=sm[q][:, 2:3], i_=lg[q][:]: e.tensor_reduce(out=o, in_=i_, axis=mybir.AxisListType.X, op=ALU.add), [lk], [sk])
                kb.f.op("dve", lambda e, o=sm[q][:, 3:4], i_=sm[q][:, 2:3]: e.reciprocal(out=o, in_=i_), [sk], [sk])
                kb.ts("dve", lg[q][:], lg[q][:], sm[q][:, 3:4], None, ALU.mult, None, [lk, sk], [lk])
                kb.tr(pA[q][0:E, 0:128], lg[q][:], idf[:], [lk, "idf"], ["pA%d" % q])
                kb.cp("act", AFFT[:, tt * 128:(tt + 1) * 128], pA[q][0:E, 0:128], ["pA%d" % q], ["AFFT"])
            for (tok0, ntok, cap, slot0, isctx) in sets:
                src = AFFT[:, tok0:tok0 + ntok]
                srck = "AFFT"
                for rd in range(cap // 8):
                    sl = slice(slot0 + rd * 8, slot0 + rd * 8 + 8)
                    kb.f.op("dve", lambda e, o=GAT[:, sl], i_=src: e.max(out=o, in_=i_), [srck], ["GAT"])
                    kb.f.op("dve", lambda e, o=IDXU[:, sl], m_=GAT[:, sl], v_=src: e.max_index(out=o, in_max=m_, in_values=v_), [srck, "GAT"], ["IDXU"])
                    if rd < cap // 8 - 1:
                        dst = WK[:, 0:ntok]
                        kb.f.op("dve", lambda e, o=dst, m_=GAT[:, sl], v_=src: e.match_replace(out=o, in_to_replace=m_, in_values=v_, imm_value=-1.0), [srck, "GAT"], ["WK"])
                        src = dst
                        srck = "WK"
            kb.cp("dve", IDXF[:], IDXU[:], ["IDXU"], ["IDXF"])
            if with_ctx:
                kb.ts("dve", IDXF[:, CAPL:NSL], IDXF[:, CAPL:NSL], float(L), None, ALU.add, None, ["IDXF"], ["IDXF"])
            for si, (s0, n, _) in enumerate(stiles):
                q = si % 2
                kb.tr(pA[q][0:n, 0:16], IDXF[:, s0:s0 + n], idf[0:16, 0:16], ["IDXF", "idf"], ["pA%d" % q])
                kb.cp("dve", IDXT[0:n, si, :], pA[q][0:n, 0:16], ["pA%d" % q], ["IDXT"])
                kb.tr(pL[q][0:n, 0:16], GAT[:, s0:s0 + n], idf[0:16, 0:16], ["GAT", "idf"], ["pL%d" % q])
                kb.cp("dve", GT[0:n, si, :], pL[q][0:n, 0:16], ["pL%d" % q], ["GT"])
        f.barrier()
        with ExitStack() as es:
            sb = lambda n, s, dt=F32: kb.sb(es, n, s, dt)
            SEL = sb("SEL", [16, 16, 128])
            kb.dma("sp", SEL[:], io["sel"], [], ["SEL"])
            XselT = [sb("XselT0", [128, NK, NSL], BF16)] * 2
            HT = sb("HT", [128, NF, NSL], BF16)
            Sx = [sb("Sx0", [128, ntiles, CAPL], BF16)] * 2
            NWIN = 3
            WIN = [sb("WIN%d" % q, [128, 2, NK, 128], BF16) for q in range(NWIN)]
            WOUT = [sb("WOUT%d" % q, [128, NF, D], BF16) for q in range(2)]
            sg = [sb("sg%d" % q, [128, NSL]) for q in range(2)]
            ygs = [sb("ygs%d" % q, [128, D], BF16) for q in range(2)]
            pb = kb.ps(es, "pb", [128, 512])
            pgx = [kb.ps(es, "pgx%d" % q, [128, 512]) for q in range(2)]
            ph = [kb.ps(es, "ph%d" % q, [128, 512]) for q in range(4)]
            py = kb.ps(es, "py", [128, 512])
            wi_r = 0; wo_r = 0; yg_r = 0; gx_r = 0
            for e_ in range(E):
                q = e_ % 2
                sxk = "Sx0"
                kb.mm(pb[:, 0:NSL], SEL[:, e_, :], IDXF[:, :], True, True, ["SEL", "IDXF"], ["pb"])
                for tt in range(ntiles):
                    isctx = 1 if tt >= NLT else 0
                    s0, cap = (CAPL, CAPC) if isctx else (0, CAPL)
                    kb.ts("dve", Sx[q][:, tt, 0:cap], pb[:, s0:s0 + cap], tokid[:, tt:tt + 1], None, ALU.is_equal, None, ["pb", "tokid"], [sxk])
                xk = "XselT0"
                for k in range(NK):
                    pg_ = pgx[gx_r % 2]; pgk = "pgx%d" % (gx_r % 2); gx_r += 1
                    for tt in range(NLT):
                        kb.mm(pg_[:, 0:CAPL], U2TM[:, tt, k * 128:(k + 1) * 128], Sx[q][:, tt, 0:CAPL], tt == 0, tt == NLT - 1, ["U2TM%d" % tt, sxk], [pgk])
                    if with_ctx:
                        for tt in range(NLT, ntiles):
                            kb.mm(pg_[:, CAPL:NSL], U2TM[:, tt, k * 128:(k + 1) * 128], Sx[q][:, tt, 0:CAPC], tt == NLT, tt == ntiles - 1, ["U2TM%d" % tt, sxk], [pgk])
                    kb.cp("act" if k % 2 else "dve", XselT[q][:, k, :], pg_[:, 0:NSL], [pgk], [xk])
                wo = WOUT[e_ % 2]; wok = "WOUT%d" % (e_ % 2)
                for fc in range(NF):
                    wq = wi_r % NWIN; wi_r += 1
                    for gu in range(2):
                        c0 = gu * F + fc * 128
                        kb.dma("pool", WIN[wq][:, gu, :, :], io["moe_w_in"][i, e_, :, c0:c0 + 128].rearrange("(k p) c -> p k c", p=128), [], ["WIN%d_%d" % (wq, gu)])
                    kb.dma("pool", wo[:, fc, :], io["moe_w_out"][i, e_, fc * 128:(fc + 1) * 128, :], [], [wok + "_%d" % fc])
                    hq = fc % 2
                    phg = ph[hq * 2]; phu = ph[hq * 2 + 1]
                    pgk_, puk_ = "ph%d" % (hq * 2), "ph%d" % (hq * 2 + 1)
                    for k in range(NK):
                        kb.mm(phg[:, 0:NSL], WIN[wq][:, 0, k, :], XselT[q][:, k, :], k == 0, k == NK - 1, ["WIN%d_0" % wq, xk], [pgk_])
                    for k in range(NK):
                        kb.mm(phu[:, 0:NSL], WIN[wq][:, 1, k, :], XselT[q][:, k, :], k == 0, k == NK - 1, ["WIN%d_1" % wq, xk], [puk_])
                    kb.act(sg[hq][:], phg[:, 0:NSL], AF.Silu, [pgk_], ["sg%d" % hq])
                    kb.tt("dve", HT[:, fc, :], phu[:, 0:NSL], sg[hq][:], ALU.mult, [puk_, "sg%d" % hq], ["HT"])
                for si, (s0, n, _) in enumerate(stiles):
                    yq = yg_r % 2; yg_r += 1
                    for nt in range(D // NT):
                        for fk in range(NF):
                            kb.mm(py[0:n, 0:NT], HT[:, fk, s0:s0 + n], wo[:, fk, nt * NT:(nt + 1) * NT], fk == 0, fk == NF - 1, ["HT", wok + "_%d" % fk], ["py"])
                        if nt % 2:
                            kb.ts("dve", ygs[yq][0:n, nt * NT:(nt + 1) * NT], py[0:n, 0:NT], GT[0:n, si, e_:e_ + 1], None, ALU.mult, None, ["py", "GT"], ["ygs%d" % yq])
                        else:
                            kb.act(ygs[yq][0:n, nt * NT:(nt + 1) * NT], py[0:n, 0:NT], AF.Copy, ["py", "GT"], ["ygs%d" % yq], scale=GT[0:n, si, e_:e_ + 1])
                    kb.dma("sp", YG[e_, s0:s0 + n, :], ygs[yq][0:n, :], ["ygs%d" % yq], ["YG"])
        f.barrier()
        esu.close()
        with ExitStack() as es:
            sb = lambda n, s, dt=F32: kb.sb(es, n, s, dt)
            NSTL = (CAPL + 127) // 128
            PL = min(128, CAPL)
            iot = sb("iot", [128, L + LC])
            kb.dma("sp", iot[:], io["iota_f"], [], ["iot"])
            YGL = sb("YGL", [128, E, NSTL, D], BF16)
            for e_ in range(E):
                kb.dma("sp" if e_ % 2 else "pool", YGL[0:PL, e_, :, :], YG[e_, 0:CAPL, :].rearrange("(s p) c -> p s c", p=PL), ["YG"], ["YGL%d" % e_])
            g2t = sb("g2t", [128, 2, D])
            for a_ in range(2 if with_ctx else 1):
                kb.dma("sp", g2t[:, a_, :], bcast_rows(MOD[i, 6 * a_ + 5:6 * a_ + 6, :], 128), [], ["g2t"])
            NEQ = (E * CAPC + 127) // 128 if with_ctx else 0
            EP = 128 // CAPC if with_ctx else 1
            if with_ctx:
                YGC = sb("YGC", [128, NEQ, D], BF16)
                ygv = YG[:, CAPL:NSL, :].rearrange("(eq e4) s c -> e4 s eq c", e4=EP)
                for e4 in range(EP):
                    kb.dma("sp", YGC[e4 * CAPC:(e4 + 1) * CAPC, :, :], ygv[e4], ["YG"], ["YGC"])
                shf = sb("shf", [CAPC, EP, 128]); IDXC = sb("IDXC", [128, NEQ])
                kb.dma("sp", shf[:], io["shift"], [], ["shf"])
                pI = kb.ps(es, "pI", [128, 512])
                for e4 in range(EP):
                    rhs = fview(IDXT, NSTL * 16 + e4, [(EP, NEQ)], pn=CAPC)
                    kb.mm(pI[:, 0:NEQ], shf[:, e4, :], rhs, e4 == 0, e4 == EP - 1, ["shf", "IDXT"], ["pI"])
                kb.cp("dve", IDXC[:], pI[:, 0:NEQ], ["pI"], ["IDXC"])
                STC = [sb("STC%d" % q, [128, NEQ, 128], BF16) for q in range(2)]
            STL = [sb("STL%d" % q, [128, E * NSTL, 128], BF16) for q in range(2)]
            xs = [sb("xs%d" % q, [128, NT]) for q in range(2)]
            ft = [sb("ft%d" % q, [128, NT]) for q in range(2)]
            pf = [kb.ps(es, "pf%d" % q, [128, 512]) for q in range(2)]
            it = 0
            for tt in range(ntiles):
                tq = tt % 2
                isctx = 1 if tt >= NLT else 0
                rows = slice(tt * 128, (tt + 1) * 128)
                if not isctx:
                    terms = [(e_, s_) for e_ in range(E) for s_ in range(NSTL)]
                    for ti, (e_, s_) in enumerate(terms):
                        pl_ = ti % 3 == 2
                        kb.ts("pool" if pl_ else "dve", STL[tq][0:PL, ti, :], iot[0:PL, tt * 128:(tt + 1) * 128], IDXT[0:PL, s_, e_:e_ + 1], None,
                              ALU.is_equal, None, ["iot", "IDXT"], ["STL%d_%d_%d" % (tq, int(pl_), ti)])
                else:
                    for eq in range(NEQ):
                        kb.ts("dve", STC[tq][:, eq, :], iot[:, tt * 128:(tt + 1) * 128], IDXC[:, eq:eq + 1], None, ALU.is_equal, None, ["iot", "IDXC"], ["STC%d_%d" % (tq, eq)])
                for nt in range(D // NT):
                    cs_ = slice(nt * NT, (nt + 1) * NT)
                    q = it % 2; it += 1
                    kb.dma("sp", xs[q][:], XR[rows, cs_], [], ["xs%d" % q])
                    if not isctx:
                        for ti, (e_, s_) in enumerate(terms):
                            kb.mm(pf[q][:, 0:NT], STL[tq][0:PL, ti, :], YGL[0:PL, e_, s_, cs_], ti == 0, ti == len(terms) - 1,
                                  ["STL%d_%d_%d" % (tq, int(ti % 3 == 2), ti), "YGL%d" % e_], ["pf%d" % q])
                    else:
                        for eq in range(NEQ):
                            kb.mm(pf[q][:, 0:NT], STC[tq][:, eq, :], YGC[:, eq, cs_], eq == 0, eq == NEQ - 1, ["STC%d_%d" % (tq, eq), "YGC"], ["pf%d" % q])
                    kb.tt("dve", ft[q][:], pf[q][:, 0:NT], g2t[:, isctx, cs_], ALU.mult, ["pf%d" % q, "g2t"], ["ft%d" % q])
                    kb.stt(ft[q][:], xs[q][:], cfg.ALPHA, ft[q][:], ALU.mult, ALU.add, ["xs%d" % q, "ft%d" % q], ["ft%d" % q])
                    kb.dma("pool", TS[rows, cs_], ft[q][:], ["ft%d" % q], ["TS"])
    f.barrier()


def kernel(**inputs):
    cfg = Cfg()
    nb = int(np.asarray(inputs["x"]).shape[0])
    nc = build(cfg)
    maps = [host_layout(cfg, inputs, b) for b in range(nb)]
    res = run_bass_kernel_spmd(nc, maps, core_ids=list(range(nb)))
    out = np.stack([np.asarray(res.results[b]["out"]) for b in range(nb)])
    return out.astype(np.float32)
```

```python
import math
from contextlib import ExitStack

import numpy as np
import concourse.bass as bass
import concourse.mybir as mybir
from concourse.ap import AP
from concourse.bass_utils import run_bass_kernel_spmd

F32 = mybir.dt.float32
BF16 = mybir.dt.bfloat16
I32 = mybir.dt.int32
U32 = mybir.dt.uint32
AF = mybir.ActivationFunctionType
ALU = mybir.AluOpType

COMPUTE = ("pe", "act", "dve", "pool")
DMAQ = ("sp", "act", "pool")
NDMASEM = 8


class Op:
    __slots__ = ("eng", "fn", "deps", "is_dma", "needed", "semval")

    def __init__(self, eng, fn, is_dma):
        self.eng = eng
        self.fn = fn
        self.deps = []
        self.is_dma = is_dma
        self.needed = False
        self.semval = None


class FW:
    def __init__(self, nc):
        self.nc = nc
        self.streams = {e: [] for e in ("pe", "act", "dve", "pool", "sp")}
        self.last_w = {}
        self.readers = {}
        self.bar_idx = {e: 0 for e in self.streams}

    def op(self, eng, fn, reads=(), writes=(), dma=False):
        o = Op(eng, fn, dma)
        deps = {}
        for k in reads:
            lw = self.last_w.get(k)
            if lw is not None:
                deps[id(lw)] = lw
        for k in writes:
            lw = self.last_w.get(k)
            if lw is not None:
                deps[id(lw)] = lw
            for r in self.readers.get(k, ()):
                deps[id(r)] = r
        for d in deps.values():
            if (not d.is_dma) and (not dma) and d.eng == eng and eng == "pe":
                continue
            o.deps.append(d)
            d.needed = True
        for k in writes:
            self.last_w[k] = o
            self.readers[k] = []
        for k in reads:
            if k in writes:
                continue
            lst = self.readers.setdefault(k, [])
            if not dma:
                lst[:] = [r for r in lst if not (r.eng == eng and not r.is_dma)]
            lst.append(o)
        self.streams[eng].append(o)
        return o

    def barrier(self):
        lastops = []
        for e, st in self.streams.items():
            for o in reversed(st):
                if (not o.is_dma) and o.fn is not None:
                    lastops.append(o)
                    break
            lastops += [o for o in st[self.bar_idx[e]:] if o.is_dma]
        for e in self.streams:
            o = Op(e, None, False)
            o.deps = list(lastops)
            self.streams[e].append(o)
        for d in lastops:
            d.needed = True
        for e in self.streams:
            self.bar_idx[e] = len(self.streams[e])
        self.last_w = {}
        self.readers = {}

    def emit(self, final_wait_ops=()):
        nc = self.nc
        with ExitStack() as es:
            csem = {e: es.enter_context(nc.semaphore("c_" + e)) for e in COMPUTE}
            dsems = {e: [es.enter_context(nc.semaphore("d_%s%d" % (e, i))) for i in range(NDMASEM)]
                     for e in DMAQ}
            for e in COMPUTE:
                cnt = 0
                for o in self.streams[e]:
                    if o.is_dma or o.fn is None:
                        continue
                    if o.needed:
                        cnt += 1
                        o.semval = (csem[e], cnt)
            for e in DMAQ:
                dcount = [0] * NDMASEM
                rr = 0
                for o in self.streams[e]:
                    if not o.is_dma:
                        continue
                    dcount[rr] += 1
                    o.semval = (dsems[e][rr], 16 * dcount[rr])
                    rr = (rr + 1) % NDMASEM
            block = es.enter_context(nc.Block())
            handles = {"pe": block.tensor, "act": block.scalar, "dve": block.vector,
                       "pool": block.gpsimd, "sp": block.sync}
            for e in ("sp", "pool", "act", "dve", "pe"):
                ops = self.streams[e]
                finals = list(final_wait_ops) if e == "sp" else []

                def body(eng, ops=ops, finals=finals):
                    waited = {}

                    def wait(sem, val):
                        k = id(sem)
                        if waited.get(k, 0) >= val:
                            return
                        waited[k] = val
                        eng.wait_ge(sem, val)

                    for o in ops:
                        for d in o.deps:
                            wait(*d.semval)
                        if o.fn is None:
                            continue
                        if o.is_dma:
                            sem, val = o.semval
                            if val > 16:
                                wait(sem, val - 16)
                            o.fn(eng).then_inc(sem, 16)
                        else:
                            ins = o.fn(eng)
                            if o.needed:
                                ins.then_inc(o.semval[0], 1)
                    for o in finals:
                        wait(*o.semval)

                handles[e](body)


class Cfg:
    def __init__(self, D=2048, L=2048, LC=256, DEPTH=4):
        self.D, self.L, self.LC, self.DEPTH = D, L, LC, DEPTH
        self.E = 16
        self.GW = 64
        self.KW = 31
        self.NK = D // 128
        self.G = D // 16
        self.NB = self.G // 16
        self.F = D // 2
        self.NF = self.F // 128
        self.NT = min(512, D)
        self.TALL = L + LC
        self.NLT = L // 128
        self.NCT = LC // 128
        self.NTT = self.TALL // 128
        self.CAPL = 2 * L // 16
        self.CAPC = 2 * LC // 16
        self.NCL = L // 8
        self.NCC = LC // 8
        self.NC = self.NCL + self.NCC
        self.ALPHA = (2.0 * DEPTH) ** 0.25
        self.NS5 = (DEPTH + 1) // 2
        self.NCV = DEPTH // 2


LN_EPS = 1e-5
TWO_PI = 2.0 * math.pi


def fview(t, off, dims, p0=0, pn=None):
    base = t[:]
    pstep, pcount = base.ap[0]
    if pn is None:
        pn = pcount - p0
    return AP(base.tensor, base.offset + p0 * pstep + off, [[pstep, pn]] + [list(d) for d in dims])


class KB:
    def __init__(self, cfg, debug_outs=()):
        self.cfg = cfg
        self.nc = bass.Bass("TRN2", target_bir_lowering=False)
        self.f = FW(self.nc)
        self.debug_outs = set(debug_outs)
        self.uid = 0
        self.out_stores = []

    def din(self, name, shape, dt=F32):
        return self.nc.dram_tensor(name, list(shape), dt, kind="ExternalInput").ap()

    def dscr(self, name, shape, dt=F32):
        kind = "ExternalOutput" if name in self.debug_outs else "Internal"
        return self.nc.dram_tensor(name, list(shape), dt, kind=kind).ap()

    def sb(self, es, name, shape, dt=F32):
        self.uid += 1
        return es.enter_context(self.nc.sbuf_tensor("%s_%d" % (name, self.uid), list(shape), dt))

    def ps(self, es, name, shape, dt=F32):
        self.uid += 1
        return es.enter_context(self.nc.psum_tensor("%s_%d" % (name, self.uid), list(shape), dt))

    SCRATCH = ("XR", "TS", "U2", "ZS", "CV", "MOD", "YG", "OUTF")

    def dma(self, q, out, in_, r, w):
        w2 = []
        for k in w:
            if k in self.SCRATCH:
                self.uid += 1
                k = "%s#%d" % (k, self.uid)
            w2.append(k)
        return self.f.op(q, lambda e: e.dma_start(out=out, in_=in_), r, w2, dma=True)

    def tt(self, eng, out, in0, in1, op, r, w):
        return self.f.op(eng, lambda e: e.tensor_tensor(out=out, in0=in0, in1=in1, op=op), r, w)

    def ts(self, eng, out, in0, s1, s2, op0, op1, r, w):
        if s2 is None:
            return self.f.op(eng, lambda e: e.tensor_scalar(out=out, in0=in0, scalar1=s1, scalar2=None, op0=op0), r, w)
        return self.f.op(eng, lambda e: e.tensor_scalar(out=out, in0=in0, scalar1=s1, scalar2=s2, op0=op0, op1=op1), r, w)

    def stt(self, out, in0, scalar, in1, op0, op1, r, w):
        return self.f.op("dve", lambda e: e.scalar_tensor_tensor(out=out, in0=in0, scalar=scalar, in1=in1, op0=op0, op1=op1), r, w)

    def cp(self, eng, out, in_, r, w):
        if eng == "act":
            return self.f.op(eng, lambda e: e.copy(out=out, in_=in_), r, w)
        return self.f.op(eng, lambda e: e.tensor_copy(out=out, in_=in_), r, w)

    def act(self, out, in_, func, r, w, scale=1.0, bias=None):
        if bias is None:
            return self.f.op("act", lambda e: e.activation(out=out, in_=in_, func=func, scale=scale), r, w)
        return self.f.op("act", lambda e: e.activation(out=out, in_=in_, func=func, scale=scale, bias=bias), r, w)

    def mm(self, out, lhsT, rhs, start, stop, r, w):
        return self.f.op("pe", lambda e: e.matmul(out=out, lhsT=lhsT, rhs=rhs, start=start, stop=stop), r, w)

    def tr(self, out, in_, ident, r, w):
        return self.f.op("pe", lambda e: e.transpose(out=out, in_=in_, identity=ident), r, w)

    def memset(self, eng, ap, val, r, w):
        return self.f.op(eng, lambda e: e.memset(ap, val), r, w)


def s5_paramgen_gen(kb, es, j, io, SW, A8T, A8D):
    cfg, f = kb.cfg, kb.f
    GL = 8
    NB = cfg.NB
    GW = NB * GL
    sb = lambda n, s, dt=F32: kb.sb(es, n, s, dt)
    kt = sb("kt", [128, 2, 4, 2, 8])
    msk = sb("msk", [128, 2, 128])
    idf = sb("idf", [128, 128])
    kb.dma("sp", kt[:], io["s5_kt"], [], ["kt"])
    kb.dma("sp", msk[:], io["s5_mask"], [], ["msk"])
    kb.dma("sp", idf[:], io["ident"], [], ["idf_pg"])
    LM = sb("LM", [128, 3, GW]); Bt = sb("Bt", [128, 2, GW, 16]); Ct = sb("Ct", [128, 2, GW, 16])
    dtv = sb("dtv", [128, GW]); XRI = sb("XRI", [128, 2, GW])
    marg = sb("marg", [128, GW, 8]); mag = sb("mag", [128, GW, 8]); y = sb("y", [128, GW, 8])
    yi = sb("yi", [128, GW, 8], I32); yf = sb("yf", [128, GW, 8]); m1 = sb("m1", [128, GW, 8])
    sinv = sb("sinv", [128, GW, 8]); cosv = sb("cosv", [128, GW, 8])
    P = sb("P", [128, 4, 2, GW, 8])
    sm = sb("sm", [128, 8, GW])
    Bb = sb("Bb", [128, 2, GW, 16]); tb = sb("tb", [128, GW, 16])
    WP = sb("WP", [128, 2, GL, 128]); WE = sb("WE", [128, 2, GL, 128]); CM = sb("CM", [128, 2, GL, 128])
    tw = sb("tw", [128, GL, 128])
    OUT = sb("OUT", [128, GL, 6, 128], BF16)
    pT = [kb.ps(es, "pT%d" % i, [128, 512]) for i in range(2)]
    pM = [kb.ps(es, "pM%d" % i, [128, 512]) for i in range(2)]
    V = "dve"
    for d in range(2):
        for blk in range(NB):
            gs = slice(blk * GL, (blk + 1) * GL)
            kb.dma("sp", LM[:, :, gs], io["s5_lam"][j, d, blk], [], ["LM"])
            kb.dma("sp", Bt[:, :, gs, :], io["s5_B"][j, d, blk], [], ["Bt"])
            kb.dma("sp", Ct[:, :, gs, :], io["s5_C"][j, d, blk], [], ["Ct"])
        kb.act(dtv[:], LM[:, 2, :], AF.Exp, ["LM"], ["dtv"])
        kb.tt(V, XRI[:, 0, :], LM[:, 0, :], dtv[:], ALU.mult, ["LM", "dtv"], ["XRI"])
        kb.tt(V, XRI[:, 1, :], LM[:, 1, :], dtv[:], ALU.mult, ["LM", "dtv"], ["XRI"])
        for ty in range(4):
            ktab = fview(kt, ((d * 4 + ty) * 2 + 0) * 8, [(0, GW), (1, 8)])
            ktab2 = fview(kt, ((d * 4 + ty) * 2 + 1) * 8, [(0, GW), (1, 8)])
            xr_b = fview(XRI, 0, [(1, GW), (0, 8)])
            xi_b = fview(XRI, GW, [(1, GW), (0, 8)])
            kb.tt(V, marg[:], xr_b, ktab, ALU.mult, ["XRI", "kt"], ["marg"])
            kb.act(mag[:], marg[:], AF.Exp, ["marg"], ["mag"])
            kb.tt(V, y[:], xi_b, ktab2, ALU.mult, ["XRI", "kt"], ["y"])
            kb.cp(V, yi[:], y[:], ["y"], ["yi"])
            kb.cp(V, yf[:], yi[:], ["yi"], ["yf"])
            kb.tt(V, y[:], y[:], yf[:], ALU.subtract, ["y", "yf"], ["y"])
            kb.ts(V, m1[:], y[:], 0.5, None, ALU.is_gt, None, ["y"], ["m1"])
            kb.tt(V, y[:], y[:], m1[:], ALU.subtract, ["y", "m1"], ["y"])
            kb.ts(V, m1[:], y[:], -0.5, None, ALU.is_lt, None, ["y"], ["m1"])
            kb.tt(V, y[:], y[:], m1[:], ALU.add, ["y", "m1"], ["y"])
            kb.act(sinv[:], y[:], AF.Sin, ["y"], ["sinv"], scale=TWO_PI)
            kb.ts(V, yf[:], y[:], 0.25, None, ALU.add, None, ["y"], ["yf"])
            kb.ts(V, m1[:], yf[:], 0.5, None, ALU.is_gt, None, ["yf"], ["m1"])
            kb.tt(V, yf[:], yf[:], m1[:], ALU.subtract, ["yf", "m1"], ["yf"])
            kb.act(cosv[:], yf[:], AF.Sin, ["yf"], ["cosv"], scale=TWO_PI)
            kb.tt(V, P[:, ty, 0], mag[:], cosv[:], ALU.mult, ["mag", "cosv"], ["P"])
            kb.tt(V, P[:, ty, 1], mag[:], sinv[:], ALU.mult, ["mag", "sinv"], ["P"])
        yield
        a1re = P[:, 3, 0, :, 0]; a1im = P[:, 3, 1, :, 0]
        a8re = P[:, 3, 0, :, 1]; a8im = P[:, 3, 1, :, 1]
        lre = LM[:, 0, :]; lim = LM[:, 1, :]
        S = lambda i_: sm[:, i_, :]
        kb.ts(V, S(0), a1re, -1.0, None, ALU.add, None, ["P"], ["sm0"])
        kb.tt(V, S(1), lre, lre, ALU.mult, ["LM"], ["sm1"])
        kb.tt(V, S(2), lim, lim, ALU.mult, ["LM"], ["sm2"])
        kb.tt(V, S(1), S(1), S(2), ALU.add, ["sm1", "sm2"], ["sm1"])
        kb.f.op(V, lambda e, o=S(1), i_=S(1): e.reciprocal(out=o, in_=i_), ["sm1"], ["sm1"])
        kb.tt(V, S(2), S(0), lre, ALU.mult, ["sm0", "LM"], ["sm2"])
        kb.tt(V, S(3), a1im, lim, ALU.mult, ["P", "LM"], ["sm3"])
        kb.tt(V, S(2), S(2), S(3), ALU.add, ["sm2", "sm3"], ["sm2"])
        kb.tt(V, S(4), S(2), S(1), ALU.mult, ["sm2", "sm1"], ["sm4"])
        kb.tt(V, S(2), a1im, lre, ALU.mult, ["P", "LM"], ["sm2"])
        kb.tt(V, S(3), S(0), lim, ALU.mult, ["sm0", "LM"], ["sm3"])
        kb.tt(V, S(2), S(2), S(3), ALU.subtract, ["sm2", "sm3"], ["sm2"])
        kb.tt(V, S(5), S(2), S(1), ALU.mult, ["sm2", "sm1"], ["sm5"])
        cre_b = fview(sm, 4 * GW, [(1, GW), (0, 16)]); cim_b = fview(sm, 5 * GW, [(1, GW), (0, 16)])
        kb.tt(V, Bb[:, 0], Bt[:, 0], cre_b, ALU.mult, ["Bt", "sm4"], ["Bb"])
        kb.tt(V, tb[:], Bt[:, 1], cim_b, ALU.mult, ["Bt", "sm5"], ["tb"])
        kb.tt(V, Bb[:, 0], Bb[:, 0], tb[:], ALU.subtract, ["Bb", "tb"], ["Bb"])
        kb.tt(V, Bb[:, 1], Bt[:, 1], cre_b, ALU.mult, ["Bt", "sm4"], ["Bb"])
        kb.tt(V, tb[:], Bt[:, 0], cim_b, ALU.mult, ["Bt", "sm5"], ["tb"])
        kb.tt(V, Bb[:, 1], Bb[:, 1], tb[:], ALU.add, ["Bb", "tb"], ["Bb"])
        kb.cp(V, S(6), a8re, ["P"], ["sm6"])
        kb.cp(V, S(7), a8im, ["P"], ["sm7"])
        BST = 2 * 5 * 2 * 2 * GL
        for lvl in range(5):
            base = ((d * 5 + lvl) * 2) * 2 * GL
            a8t = lambda which, ri: fview(A8T, base + (which * 2 + ri) * GL, [(BST, NB), (1, GL)])
            s6 = fview(sm, 6 * GW, [(GL, NB), (1, GL)]); s7 = fview(sm, 7 * GW, [(GL, NB), (1, GL)])
            kb.cp(V, a8t(0, 0), s6, ["sm6"], ["A8T"])
            kb.cp(V, a8t(0, 1), s6, ["sm6"], ["A8T"])
            kb.ts(V, a8t(1, 0), s7, -1.0, None, ALU.mult, None, ["sm7"], ["A8T"])
            kb.cp(V, a8t(1, 1), s7, ["sm7"], ["A8T"])
            if lvl < 4:
                kb.tt(V, S(2), S(6), S(6), ALU.mult, ["sm6"], ["sm2"])
                kb.tt(V, S(3), S(7), S(7), ALU.mult, ["sm7"], ["sm3"])
                kb.stt(S(7), S(6), 2.0, S(7), ALU.mult, ALU.mult, ["sm6", "sm7"], ["sm7"])
                kb.tt(V, S(6), S(2), S(3), ALU.subtract, ["sm2", "sm3"], ["sm6"])
        yield
        for blk in range(NB):
            g0 = blk * GL

            def cprod(dst, ty, src, key_src, key_dst, g0=g0):
                pre = fview(P, ((ty * 2 + 0) * GW + g0) * 8, [(8, GL), (1, 8), (0, 16)])
                pim = fview(P, ((ty * 2 + 1) * GW + g0) * 8, [(8, GL), (1, 8), (0, 16)])
                sre = fview(src, g0 * 16, [(16, GL), (0, 8), (1, 16)])
                sim = fview(src, (GW + g0) * 16, [(16, GL), (0, 8), (1, 16)])
                dre = fview(dst, 0, [(128, GL), (16, 8), (1, 16)])
                dim_ = fview(dst, GL * 128, [(128, GL), (16, 8), (1, 16)])
                twv = fview(tw, 0, [(128, GL), (16, 8), (1, 16)])
                kb.tt(V, dre, pre, sre, ALU.mult, ["P", key_src], [key_dst])
                kb.tt(V, twv, pim, sim, ALU.mult, ["P", key_src], ["tw"])
                kb.tt(V, dre, dre, twv, ALU.subtract, [key_dst, "tw"], [key_dst])
                kb.tt(V, dim_, pre, sim, ALU.mult, ["P", key_src], [key_dst])
                kb.tt(V, twv, pim, sre, ALU.mult, ["P", key_src], ["tw"])
                kb.tt(V, dim_, dim_, twv, ALU.add, [key_dst, "tw"], [key_dst])

            cprod(CM, 0, Ct, "Ct", "CM")
            cprod(WP, 1, Bb, "Bb", "WP")
            cprod(WE, 2, Bb, "Bb", "WE")
            kb.ts(V, CM[:, 1], CM[:, 1], -1.0, None, ALU.mult, None, ["CM"], ["CM"])
            kb.cp("pool", OUT[:, :, 2, :], CM[:, 0], ["CM"], ["OUT"])
            kb.cp("pool", OUT[:, :, 3, :], CM[:, 1], ["CM"], ["OUT"])
            for ri in range(2):
                for q in range(2):
                    pt = pT[(ri * 2 + q) % 2]
                    ptk = "pT%d" % ((ri * 2 + q) % 2)
                    for i4 in range(4):
                        gl = q * 4 + i4
                        kb.tr(pt[:, i4 * 128:(i4 + 1) * 128], WE[:, ri, gl, :], idf[:], ["WE", "idf_pg"], [ptk])
                    kb.cp("act", OUT[:, q * 4:(q + 1) * 4, ri, :], pt[:, :].rearrange("p (g c) -> p g c", g=4), [ptk], ["OUT"])
            for gh in range(2):
                for q in range(2):
                    pm = pM[(gh * 2 + q) % 2]
                    pmk = "pM%d" % ((gh * 2 + q) % 2)
                    for i4 in range(4):
                        gl = q * 4 + i4
                        lo, hi = gh * 64, (gh + 1) * 64
                        kb.mm(pm[:, i4 * 128:(i4 + 1) * 128], WP[lo:hi, 0, gl, :], CM[lo:hi, 0, gl, :], True, False, ["WP", "CM"], [pmk])
                        kb.mm(pm[:, i4 * 128:(i4 + 1) * 128], WP[lo:hi, 1, gl, :], CM[lo:hi, 1, gl, :], False, True, ["WP", "CM"], [pmk])
                    mk = fview(msk, d * 128, [(0, 4), (1, 128)])
                    kb.tt(V, OUT[:, q * 4:(q + 1) * 4, 4 + gh, :], pm[:, :].rearrange("p (g c) -> p g c", g=4), mk, ALU.mult, [pmk, "msk"], ["OUT"])
            kb.dma("pool", SW[j, d, blk], OUT[:], ["OUT"], ["SW%d_%d_%d" % (j, d, blk)])
            yield
    kb.dma("pool", A8D[j], A8T[:].rearrange("p a b c d e g -> p (a b c d e g)"), ["A8T"], ["A8D%d" % j])
    yield


def bcast_rows(ap2d, n):
    return ap2d.broadcast_to([n, ap2d.shape[1]])


def s5_layer(kb, i, j, io, scr, ctx_out):
    cfg, f = kb.cfg, kb.f
    GL = 8
    NC, NCL, NCC, L = cfg.NC, cfg.NCL, cfg.NCC, cfg.L
    XR, MOD, ZS, SW = scr["XR"], scr["MOD"], scr["ZS"], scr["SW"]
    with ExitStack() as es:
        sb = lambda n, s, dt=F32: kb.sb(es, n, s, dt)
        A8T = sb("A8T", [128, cfg.NB, 2, 5, 2, 2, GL])
        kb.dma("sp", A8T[:].rearrange("p a b c d e g -> p (a b c d e g)"), scr["A8D"][j], [], ["A8T"])
        idb = sb("idb", [128, 128], BF16)
        idf = sb("idf2", [128, 128])
        kb.dma("sp", idf[:], io["ident"], [], ["idf"])
        kb.cp("dve", idb[:], idf[:], ["idf"], ["idb"])
        ctiles = [(t * 1024, 128, t * 128, 0) for t in range(L // 1024)] + [(L, NCC, NCL, 1)]
        nct = len(ctiles)
        W8 = sb("W8", [128, 2, GL, 6, 128], BF16)
        mod4 = sb("mod4", [128, 4, 256]); dsk = sb("dsk", [128, 256])
        U32 = [sb("U32_%d" % c, [128, 8, 256]) for c in range(nct)]
        U8g = [sb("U8g%d" % c, [128, 16, 128], BF16) for c in range(2)]
        X8 = sb("X8", [128, 16, NC], BF16)
        Z = sb("Z", [128, 2, 2, GL, NC])
        NMAX_ = max(NCL, NCC) // 2
        tq1 = [sb("tq1_%d" % d_, [128, 2, GL, NMAX_]) for d_ in range(2)]
        tq2 = [sb("tq2_%d" % d_, [128, 2, GL, NMAX_]) for d_ in range(2)]
        Sbf = sb("Sbf", [128, 2, 2, GL, NC], BF16)
        Y8s = sb("Y8s", [128, 16, NC], BF16)
        ytm = [sb("ytm%d" % c, [128, 8, 256]) for c in range(2)]
        zt = [sb("zt%d" % c, [128, 8, 256], BF16) for c in range(2)]
        pX = [kb.ps(es, "pX%d" % c, [128, 512], BF16) for c in range(2)]
        pZ = [kb.ps(es, "pZ%d" % c, [128, 512]) for c in range(4)]
        pY = [kb.ps(es, "pY%d" % c, [128, 512]) for c in range(2)]
        DS, RS = 2 * GL * NC, GL * NC

        def cf(k):
            return NCL + k if k < NCC else k - NCC

        def cb(k):
            return NCL + (NCC - 1 - k) if k < NCC else NCL - 1 - (k - NCC)

        rot = 0
        for blk in range(cfg.NB):
            c0 = blk * 256
            for q, row in enumerate((1, 0, 7, 6)):
                kb.dma("sp", mod4[:, q, :], bcast_rows(MOD[i, row:row + 1, c0:c0 + 256], 128), [], ["mod4"])
            kb.dma("sp", dsk[:], bcast_rows(io["s5_d"][j, 0:1, c0:c0 + 256], 128), [], ["dsk"])
            for d in range(2):
                kb.dma("sp", W8[:, d], SW[j, d, blk], [], ["W8"])
            for ci, (row0, n, col0, isctx) in enumerate(ctiles):
                uk = "U32_%d" % ci
                kb.dma("sp", U32[ci][0:n], XR[row0:row0 + n * 8, c0:c0 + 256].rearrange("(c s) j -> c s j", s=8), [], [uk])
                scb = fview(mod4, (2 * isctx) * 256, [(0, 8), (1, 256)], pn=n)
                shb = fview(mod4, (2 * isctx + 1) * 256, [(0, 8), (1, 256)], pn=n)
                kb.tt("dve", U32[ci][0:n], U32[ci][0:n], scb, ALU.mult, [uk, "mod4"], [uk])
                kb.tt("dve", U32[ci][0:n], U32[ci][0:n], shb, ALU.add, [uk, "mod4"], [uk])
                ug = U8g[ci % 2]; ugk = "U8g%d" % (ci % 2)
                kb.cp("act", fview(ug, 0, [(16, 8), (128, 16), (1, 16)], pn=n),
                      fview(U32[ci], 0, [(256, 8), (16, 16), (1, 16)], pn=n), [uk], [ugk])
                for g4 in range(4):
                    px = pX[rot % 2]; pxk = "pX%d" % (rot % 2); rot += 1
                    for q in range(4):
                        g = g4 * 4 + q
                        kb.tr(px[:, q * 128:q * 128 + n], ug[0:n, g, :], idb[0:n, 0:n], [ugk, "idb"], [pxk])
                    kb.cp("dve" if g4 % 2 else "act", X8[:, g4 * 4:(g4 + 1) * 4, col0:col0 + n],
                          px[:, :].rearrange("p (g c) -> p g c", g=4)[:, :, 0:n], [pxk], ["X8"])
            zr = 0
            for d in range(2):
                for gl in range(GL):
                    pr = pZ[zr % 4]; prk = "pZ%d" % (zr % 4); zr += 1
                    pi = pZ[zr % 4]; pik = "pZ%d" % (zr % 4); zr += 1
                    for gh in range(2):
                        g = gh * 8 + gl
                        lo, hi = gh * 64, (gh + 1) * 64
                        kb.mm(pr[lo:hi, 0:NC], W8[:, d, gl, 0, lo:hi], X8[:, g, :], True, True, ["W8", "X8"], [prk])
                        kb.mm(pi[lo:hi, 0:NC], W8[:, d, gl, 1, lo:hi], X8[:, g, :], True, True, ["W8", "X8"], [pik])
                    kb.cp("act", Z[:, d, 0, gl, :], pr[:, 0:NC], [prk], ["Z%d" % d])
                    kb.cp("act" if d else "dve", Z[:, d, 1, gl, :], pi[:, 0:NC], [pik], ["Z%d" % d])
            NMAX = max(NCL, NCC) // 2
            ML = 4
            for d, RE in ((0, "dve"), (1, "pool")):
                zk, k1, k2 = "Z%d" % d, "t1_%d" % d, "t2_%d" % d
                segs = [(NCL, 1, NCC), (0, 1, NCL)] if d == 0 else [(NC - 1, -1, NCC), (NCL - 1, -1, NCL)]

                def cacc(dc0, dst_, sc0, sst, n, lvl, d=d, RE=RE, zk=zk, k1=k1, k2=k2):
                    if n <= 0:
                        return
                    ab = (((blk * 2 + d) * 5 + lvl) * 2) * 2 * GL
                    dst = fview(Z, d * DS + dc0, [(RS, 2), (NC, GL), (dst_, n)])
                    src = fview(Z, d * DS + sc0, [(RS, 2), (NC, GL), (sst, n)])
                    srw = fview(Z, d * DS + sc0 + RS, [(-RS, 2), (NC, GL), (sst, n)])
                    AR_ = fview(A8T, ab, [(GL, 2), (1, GL), (0, n)])
                    AI_ = fview(A8T, ab + 2 * GL, [(GL, 2), (1, GL), (0, n)])
                    T1 = fview(tq1[d], 0, [(GL * NMAX, 2), (NMAX, GL), (1, n)])
                    T2 = fview(tq2[d], 0, [(GL * NMAX, 2), (NMAX, GL), (1, n)])
                    kb.tt(RE, T1, src, AR_, ALU.mult, [zk, "A8T"], [k1])
                    kb.tt(RE, T2, srw, AI_, ALU.mult, [zk, "A8T"], [k2])
                    kb.tt(RE, T1, T1, T2, ALU.add, [k1, k2], [k1])
                    kb.tt(RE, dst, dst, T1, ALU.add, [zk, k1], [zk])

                def col(p, segs=segs):
                    (c0a, sga, la), (c0b, sgb, lb) = segs
                    return c0a + sga * p if p < la else c0b + sgb * (p - la)

                for lvl in range(ML):
                    h = 1 << lvl
                    for (sc0_, sg_, ln_) in segs:
                        cacc(sc0_ + sg_ * (2 * h - 1), sg_ * 2 * h, sc0_ + sg_ * (h - 1), sg_ * 2 * h, ln_ // (2 * h), lvl)
                BS = 1 << ML
                for q_ in range(1, NC // BS):
                    cacc(col(q_ * BS + BS - 1), 1, col(q_ * BS - 1), 1, 1, ML)
                for lvl in range(ML - 1, -1, -1):
                    h = 1 << lvl
                    (c0a, sga, la), (c0b, sgb, lb) = segs
                    cacc(c0a + sga * (2 * h + h - 1), sga * 2 * h, c0a + sga * (2 * h - 1), sga * 2 * h, la // (2 * h) - 1, lvl)
                    cacc(c0b + sgb * (h - 1), 1, c0a + sga * (la - 1), 1, 1, lvl)
                    cacc(c0b + sgb * (2 * h + h - 1), sgb * 2 * h, c0b + sgb * (2 * h - 1), sgb * 2 * h, lb // (2 * h) - 1, lvl)
            kb.cp("act", Sbf[:], Z[:], ["Z0", "Z1"], ["Sbf"])
            fw_p = [(1, NCL, 0), (0, 1, NC - 1), (NCL + 1, NC, NCL)]
            bw_p = [(NCL, NC - 1, NCL + 1), (NCL - 1, NCL, NCL), (0, NCL - 1, 1)]
            for gh in range(2):
                lo, hi = gh * 64, (gh + 1) * 64
                for gl in range(GL):
                    g = gh * 8 + gl
                    py = pY[g % 2]; pyk = "pY%d" % (g % 2)
                    kb.mm(py[:, 0:NC], W8[:, 0, gl, 4 + gh, :], X8[:, g, :], True, False, ["W8", "X8"], [pyk])
                    kb.mm(py[:, 0:NC], W8[:, 1, gl, 4 + gh, :], X8[:, g, :], False, False, ["W8", "X8"], [pyk])
                    pieces = [(0, p) for p in fw_p] + [(1, p) for p in bw_p]
                    pieces = [(d, p) for d, p in pieces if p[1] > p[0]]
                    for pi_, (d, (o0, o1, s0)) in enumerate(pieces):
                        for ri in range(2):
                            last = (pi_ == len(pieces) - 1) and ri == 1
                            kb.mm(py[:, o0:o1], W8[lo:hi, d, gl, 2 + ri, :], Sbf[lo:hi, d, ri, gl, s0:s0 + (o1 - o0)],
                                  False, last, ["W8", "Sbf"], [pyk])
                    kb.cp("act" if g % 2 else "dve", Y8s[:, g, :], py[:, 0:NC], [pyk], ["Y8s"])
            for ci, (row0, n, col0, isctx) in enumerate(ctiles):
                if isctx and not ctx_out:
                    continue
                uk = "U32_%d" % ci
                yt = ytm[ci % 2]; ytk = "ytm%d" % (ci % 2)
                for g4 in range(4):
                    px = pX[rot % 2]; pxk = "pX%d" % (rot % 2); rot += 1
                    for q in range(4):
                        g = g4 * 4 + q
                        kb.tr(px[0:n, q * 128:(q + 1) * 128], Y8s[:, g, col0:col0 + n], idb[:], ["Y8s", "idb"], [pxk])
                    kb.cp("act" if g4 % 2 else "dve", fview(yt, g4 * 64, [(16, 4), (256, 8), (1, 16)], pn=n),
                          fview(px, 0, [(128, 4), (16, 8), (1, 16)], pn=n), [pxk], [ytk])
                dkb = fview(dsk, 0, [(0, 8), (1, 256)], pn=n)
                kb.tt("dve", U32[ci][0:n], U32[ci][0:n], dkb, ALU.mult, [uk, "dsk"], [uk])
                kb.tt("dve", yt[0:n], yt[0:n], U32[ci][0:n], ALU.add, [ytk, uk], [ytk])
                z_ = zt[ci % 2]; zk = "zt%d" % (ci % 2)
                kb.act(z_[0:n], yt[0:n], AF.Gelu_apprx_tanh, [ytk], [zk])
                kb.dma("pool", ZS[row0:row0 + n * 8, c0:c0 + 256].rearrange("(c s) j -> c s j", s=8), z_[0:n], [zk], ["ZS"])
    f.barrier()


def adaln_gen(kb, es, io, scr):
    cfg, f = kb.cfg, kb.f
    D, NK, NT = cfg.D, cfg.NK, cfg.NT
    MOD = scr["MOD"]
    if True:
        sb = lambda n, s, dt=F32: kb.sb(es, n, s, dt)
        cond = sb("cond", [128, NK, 2]); cs = sb("cs", [128, NK, 2])
        kb.dma("sp", cond[:], io["cond"], [], ["cond"])
        kb.act(cs[:], cond[:], AF.Silu, ["cond"], ["cs"])
        aw = [sb("aw%d" % q, [128, NK, NT]) for q in range(2)]
        ab = [sb("ab%d" % q, [1, NT]) for q in range(2)]
        res = [sb("res%d" % q, [2, NT]) for q in range(2)]
        ones = sb("ones", [1, 2])
        kb.memset("pool", ones[:], 1.0, [], ["ones"])
        pa = [kb.ps(es, "pa%d" % q, [128, 512]) for q in range(2)]
        it = 0
        for i in range(cfg.DEPTH):
            for n in range(6 * D // NT):
                q = it % 2; it += 1
                kb.dma("sp", aw[q][:], io["ada_w"][i, :, n * NT:(n + 1) * NT].rearrange("(k p) c -> p k c", p=128), [], ["aw%d" % q])
                kb.dma("sp", ab[q][:], io["ada_b"][i, 0:1, n * NT:(n + 1) * NT], [], ["ab%d" % q])
                for k in range(NK):
                    kb.mm(pa[q][0:2, 0:NT], cs[:, k, :], aw[q][:, k, :], k == 0, False, ["cs", "aw%d" % q], ["pa%d" % q])
                kb.mm(pa[q][0:2, 0:NT], ones[0:1, 0:2], ab[q][0:1, :], False, True, ["ones", "ab%d" % q], ["pa%d" % q])
                v = (n * NT) // D
                off = n * NT - v * D
                kb.act(res[q][:], pa[q][0:2, 0:NT], AF.Identity, ["pa%d" % q], ["res%d" % q], bias=(1.0 if v in (1, 4) else 0.0))
                kb.dma("pool", MOD[i].rearrange("(w v) d -> w v d", w=2)[:, v, off:off + NT], res[q][:], ["res%d" % q], ["MOD"])
                yield


def prologue(kb, io, scr):
    cfg, f = kb.cfg, kb.f
    with ExitStack() as es_a:
        A8T = kb.sb(es_a, "A8Tg", [128, cfg.NB, 2, 5, 2, 2, 8])
        ga = adaln_gen(kb, es_a, io, scr)
        a_done = False
        try:
            next(ga)
        except StopIteration:
            a_done = True
        n_a = cfg.DEPTH * 6 * cfg.D // cfg.NT
        n_p = cfg.NS5 * (2 * (cfg.NB + 2) + 1)
        per = max(1, -(-n_a // max(1, n_p)))
        for j in range(cfg.NS5):
            with ExitStack() as es_j:
                for _ in s5_paramgen_gen(kb, es_j, j, io, scr["SW"], A8T, scr["A8D"]):
                    for _k in range(per):
                        if a_done:
                            break
                        try:
                            next(ga)
                        except StopIteration:
                            a_done = True
        if not a_done:
            for _ in ga:
                pass
    f.barrier()


def ln_tile(kb, eng2, t, tk, mv, st, sd, g_bc, b_bc, n=128):
    cfg = kb.cfg
    D = cfg.D
    nch = max(1, D // 512)
    w = D // nch
    for c in range(nch):
        kb.f.op("dve", lambda e, o=st[0:n, c, :], i_=t[0:n, c * w:(c + 1) * w]: e.bn_stats(out=o, in_=i_), [tk], [tk + "_st"])
    kb.f.op("dve", lambda e, o=mv[0:n, :], i_=st[0:n, :, :].rearrange("p c s -> p (c s)"): e.bn_aggr(out=o, in_=i_), [tk + "_st"], [tk + "_mv"])
    kb.ts("dve", sd[0:n, 0:1], mv[0:n, 1:2], LN_EPS, None, ALU.add, None, [tk + "_mv"], [tk + "_sd"])
    kb.act(sd[0:n, 0:1], sd[0:n, 0:1], AF.Sqrt, [tk + "_sd"], [tk + "_sd"])
    kb.f.op("dve", lambda e, o=sd[0:n, 1:2], i_=sd[0:n, 0:1]: e.reciprocal(out=o, in_=i_), [tk + "_sd"], [tk + "_sd"])
    kb.stt(sd[0:n, 2:3], mv[0:n, 0:1], -1.0, sd[0:n, 1:2], ALU.mult, ALU.mult, [tk + "_mv", tk + "_sd"], [tk + "_sd"])
    kb.act(t[0:n, :], t[0:n, :], AF.Identity, [tk, tk + "_sd"], [tk], scale=sd[0:n, 1:2], bias=sd[0:n, 2:3])
    kb.tt("dve", t[0:n, :], t[0:n, :], g_bc[0:n, :], ALU.mult, [tk, "lng"], [tk])
    kb.tt(eng2, t[0:n, :], t[0:n, :], b_bc[0:n, :], ALU.add, [tk, "lnb"], [tk])


def postnorm(kb, i, which, io, scr, with_ctx, mod_rows, dst_u, final_out=None):
    cfg, f = kb.cfg, kb.f
    D = cfg.D
    XR, MOD, TS = scr["XR"], scr["MOD"], scr["TS"]
    ntiles = cfg.NTT if with_ctx else cfg.NLT
    with ExitStack() as es:
        sb = lambda n, s, dt=F32: kb.sb(es, n, s, dt)
        lng = sb("lng", [128, D]); lnb = sb("lnb", [128, D])
        kb.dma("sp", lng[:], bcast_rows(io["ln_g"][2 * i + which:2 * i + which + 1, :], 128), [], ["lng"])
        kb.dma("sp", lnb[:], bcast_rows(io["ln_b"][2 * i + which:2 * i + which + 1, :], 128), [], ["lnb"])
        mods = None
        if mod_rows is not None:
            mods = [[sb("m%d_%d" % (a, b_), [128, D]) for b_ in range(2)] for a in range(2 if with_ctx else 1)]
            for a in range(len(mods)):
                for b_ in range(2):
                    kb.dma("sp", mods[a][b_][:], bcast_rows(MOD[i, 6 * a + mod_rows[b_]:6 * a + mod_rows[b_] + 1, :], 128), [], ["mods"])
        tb = [sb("tb%d" % q, [128, D]) for q in range(3)]
        ub = [sb("ub%d" % q, [128, D], BF16) for q in range(3)]
        st = [sb("st%d" % q, [128, max(1, D // 512), 6]) for q in range(3)]
        mv = [sb("mv%d" % q, [128, 2]) for q in range(3)]
        sd = [sb("sd%d" % q, [128, 4]) for q in range(3)]
        for tt in range(ntiles):
            q = tt % 3
            tk = "tb%d" % q
            rows = slice(tt * 128, (tt + 1) * 128)
            isctx = 1 if tt >= cfg.NLT else 0
            kb.dma("sp", tb[q][:], TS[rows, :], ["TS"], [tk])
            ln_tile(kb, "pool", tb[q], tk, mv[q], st[q], sd[q], lng, lnb)
            if final_out is not None:
                kb.out_stores.append(kb.dma("pool", final_out[rows, :], tb[q][:], [tk], ["OUTF"]))
            else:
                kb.dma("pool", XR[rows, :], tb[q][:], [tk], ["XR"])
            if mods is not None:
                m = mods[isctx]
                kb.tt("dve", tb[q][:], tb[q][:], m[0][:], ALU.mult, [tk, "mods"], [tk])
                kb.tt("dve", ub[q][:], tb[q][:], m[1][:], ALU.add, [tk, "mods"], ["ub%d" % q])
                kb.dma("pool", dst_u[rows, :], ub[q][:], ["ub%d" % q], ["U2"])
    f.barrier()


def glu_stage(kb, i, j, io, scr, with_ctx):
    cfg, f = kb.cfg, kb.f
    D, NK, NT = cfg.D, cfg.NK, cfg.NT
    XR, MOD, TS, ZS = scr["XR"], scr["MOD"], scr["TS"], scr["ZS"]
    ntiles = cfg.NTT if with_ctx else cfg.NLT
    with ExitStack() as es:
        sb = lambda n, s, dt=F32: kb.sb(es, n, s, dt)
        idb = sb("idb", [128, 128], BF16); idf = sb("idf", [128, 128])
        kb.dma("sp", idf[:], io["ident"], [], ["idf"])
        kb.cp("dve", idb[:], idf[:], ["idf"], ["idb"])
        zT = sb("zT", [128, NK, ntiles * 128], BF16)
        zin = [sb("zin%d" % q, [128, D], BF16) for q in range(2)]
        pT = [kb.ps(es, "pT%d" % q, [128, 512], BF16) for q in range(2)]
        r = 0
        for tt in range(ntiles):
            q = tt % 2
            kb.dma("sp", zin[q][:], ZS[tt * 128:(tt + 1) * 128, :], ["ZS"], ["zin%d" % q])
            for k4 in range(0, NK, 4):
                nk = min(4, NK - k4)
                p = pT[r % 2]; pk = "pT%d" % (r % 2); r += 1
                for a in range(nk):
                    kb.tr(p[:, a * 128:(a + 1) * 128], zin[q][:, (k4 + a) * 128:(k4 + a + 1) * 128], idb[:], ["zin%d" % q, "idb"], [pk])
                kb.cp("act" if (r % 2) else "dve", zT[:, k4:k4 + nk, tt * 128:(tt + 1) * 128],
                      p[:, 0:nk * 128].rearrange("p (a c) -> p a c", a=nk), [pk], ["zT"])
        KH = max(1, NK // 2)
        wst = [sb("wst%d" % q, [128, KH, NT]) for q in range(2)]
        wb = [sb("wb%d" % q, [128, NK, NT], BF16) for q in range(2)]
        bvg = sb("bvg", [128, 2, NT]); g1t = sb("g1t", [128, 2, NT])
        xs = [sb("xs%d" % q, [128, NT]) for q in range(2)]
        sg = [sb("sg%d" % q, [128, NT]) for q in range(2)]
        vv = [sb("vv%d" % q, [128, NT]) for q in range(2)]
        pv = [kb.ps(es, "pv%d" % q, [128, 512]) for q in range(2)]
        pg = [kb.ps(es, "pg%d" % q, [128, 512]) for q in range(2)]
        ws = 0
        for np_ in range(D // NT):
            cs_ = slice(np_ * NT, (np_ + 1) * NT)
            for vg in range(2):
                col0 = vg * D + np_ * NT
                for kh in range(0, NK, KH):
                    w_ = wst[ws % 2]; wk = "wst%d" % (ws % 2); ws += 1
                    kb.dma("sp", w_[:], io["s5_w_glu"][j, kh * 128:(kh + KH) * 128, col0:col0 + NT].rearrange("(k p) c -> p k c", p=128), [], [wk])
                    kb.cp("act", wb[vg][:, kh:kh + KH, :], w_[:], [wk], ["wb%d" % vg])
                kb.dma("sp", bvg[:, vg, :], bcast_rows(io["s5_b_glu"][j, 0:1, col0:col0 + NT], 128), [], ["bvg"])
            for a in range(2 if with_ctx else 1):
                kb.dma("sp", g1t[:, a, :], bcast_rows(MOD[i, 6 * a + 2:6 * a + 3, cs_], 128), [], ["g1t"])
            for tt in range(ntiles):
                q = tt % 2
                isctx = 1 if tt >= cfg.NLT else 0
                rows = slice(tt * 128, (tt + 1) * 128)
                kb.dma("sp", xs[q][:], XR[rows, cs_], [], ["xs%d" % q])
                for k in range(NK):
                    kb.mm(pv[q][:, 0:NT], zT[:, k, rows], wb[0][:, k, :], k == 0, k == NK - 1, ["zT", "wb0"], ["pv%d" % q])
                for k in range(NK):
                    kb.mm(pg[q][:, 0:NT], zT[:, k, rows], wb[1][:, k, :], k == 0, k == NK - 1, ["zT", "wb1"], ["pg%d" % q])
                kb.tt("dve", sg[q][:], pg[q][:, 0:NT], bvg[:, 1, :], ALU.add, ["pg%d" % q, "bvg"], ["sg%d" % q])
                kb.act(sg[q][:], sg[q][:], AF.Sigmoid, ["sg%d" % q], ["sg%d" % q])
                kb.tt("dve", vv[q][:], pv[q][:, 0:NT], bvg[:, 0, :], ALU.add, ["pv%d" % q, "bvg"], ["vv%d" % q])
                kb.tt("pool", vv[q][:], vv[q][:], sg[q][:], ALU.mult, ["vv%d" % q, "sg%d" % q], ["vv%d" % q])
                kb.tt("pool", vv[q][:], vv[q][:], g1t[:, isctx, :], ALU.mult, ["vv%d" % q, "g1t"], ["vv%d" % q])
                kb.stt(vv[q][:], xs[q][:], cfg.ALPHA, vv[q][:], ALU.mult, ALU.add, ["xs%d" % q, "vv%d" % q], ["vv%d" % q])
                kb.dma("pool", TS[rows, cs_], vv[q][:], ["vv%d" % q], ["TS"])
    f.barrier()


def conv_layer(kb, i, j, io, scr, with_ctx):
    cfg, f = kb.cfg, kb.f
    D, NK, NT, L, LC, GW = cfg.D, cfg.NK, cfg.NT, cfg.L, cfg.LC, cfg.GW
    XR, MOD, TS, CV = scr["XR"], scr["MOD"], scr["TS"], scr["CV"]
    ntiles = cfg.NTT if with_ctx else cfg.NLT
    NLT = cfg.NLT
    NTOK = ntiles * 128
    with ExitStack() as esl:
        sbl = lambda n, s, dt=F32: kb.sb(esl, n, s, dt)
        idf = sbl("idf", [128, 128]); idb = sbl("idb", [128, 128], BF16)
        kb.dma("sp", idf[:], io["ident"], [], ["idf"])
        kb.cp("dve", idb[:], idf[:], ["idf"], ["idb"])
        uT = sbl("uT", [128, NK, NTOK], BF16)
        with ExitStack() as es:
            sb = lambda n, s, dt=F32: kb.sb(es, n, s, dt)
            mods = [[sb("m%d_%d" % (a, b_), [128, D]) for b_ in range(2)] for a in range(2 if with_ctx else 1)]
            for a in range(len(mods)):
                for b_, row in enumerate((1, 0)):
                    kb.dma("sp", mods[a][b_][:], bcast_rows(MOD[i, 6 * a + row:6 * a + row + 1, :], 128), [], ["mods"])
            xt = [sb("xt%d" % q, [128, D]) for q in range(2)]
            ub = [sb("ub%d" % q, [128, D], BF16) for q in range(2)]
            pT = [kb.ps(es, "pT%d" % q, [128, 512], BF16) for q in range(2)]
            r = 0
            for tt in range(ntiles):
                q = tt % 2
                isctx = 1 if tt >= NLT else 0
                kb.dma("sp", xt[q][:], XR[tt * 128:(tt + 1) * 128, :], [], ["xt%d" % q])
                kb.tt("dve", xt[q][:], xt[q][:], mods[isctx][0][:], ALU.mult, ["xt%d" % q, "mods"], ["xt%d" % q])
                kb.tt("pool", ub[q][:], xt[q][:], mods[isctx][1][:], ALU.add, ["xt%d" % q, "mods"], ["ub%d" % q])
                for k4 in range(0, NK, 4):
                    nk = min(4, NK - k4)
                    p = pT[r % 2]; pk = "pT%d" % (r % 2); r += 1
                    for a in range(nk):
                        kb.tr(p[:, a * 128:(a + 1) * 128], ub[q][:, (k4 + a) * 128:(k4 + a + 1) * 128], idb[:], ["ub%d" % q, "idb"], [pk])
                    kb.cp("act" if (r % 2) else "dve", uT[:, k4:k4 + nk, tt * 128:(tt + 1) * 128],
                          p[:, 0:nk * 128].rearrange("p (a c) -> p a c", a=nk), [pk], ["uT"])
        f.barrier()
        with ExitStack() as es:
            sb = lambda n, s, dt=F32: kb.sb(es, n, s, dt)
            rows = L // GW
            WL = (rows + 30) * GW
            hl = sb("hl", [128, WL], BF16)
            hc = sb("hc", [128, LC + 30], BF16)
            kb.memset("pool", hl[:], 0.0, [], ["hl"])
            kb.memset("pool", hc[:], 0.0, [], ["hc"])
            bp = sb("bp", [128, 2 * NK]); wdw = sb("wdw", [128, NK, 31]); bdw = sb("bdw", [128, NK])
            kb.dma("sp", bp[:], io["cv_b_pw1"][j], [], ["bp"])
            kb.dma("sp", wdw[:], io["cv_w_dw"][j], [], ["wdw"])
            kb.dma("sp", bdw[:], io["cv_b_dw"][j], [], ["bdw"])
            DG = [sb("DG%d" % q, [128, 31, 128], BF16) for q in range(2)]
            wst = [sb("wst%d" % q, [128, NK, 128]) for q in range(2)]
            wab = [[sb("wab%d_%d" % (q, a), [128, NK, 128], BF16) for a in range(2)] for q in range(2)]
            sig = [sb("sig%d" % q, [128, NT]) for q in range(2)]
            cvs = [sb("cvs%d" % q, [128, NT]) for q in range(2)]
            cvt = [sb("cvt%d" % q, [128, NT // 128, 128]) for q in range(2)]
            pa = [kb.ps(es, "pa%d" % q, [128, 512]) for q in range(2)]
            pg = [kb.ps(es, "pg%d" % q, [128, 512]) for q in range(2)]
            pc = [kb.ps(es, "pc%d" % q, [128, 512]) for q in range(2)]
            pt = [kb.ps(es, "pt%d" % q, [128, 512]) for q in range(2)]
            blocks = [(tb * NT, NT, 0, 15 * GW + tb * NT) for tb in range(L // NT)]
            if with_ctx:
                blocks += [(L + tb * NT, min(NT, LC - tb * NT), 1, 15 + tb * NT) for tb in range((LC + NT - 1) // NT)]
            ws = 0; it = 0
            for cc in range(NK):
                cq = cc % 2
                for a in range(2):
                    w_ = wst[ws % 2]; wk = "wst%d" % (ws % 2); ws += 1
                    c0 = a * D + cc * 128
                    kb.dma("sp", w_[:], io["cv_w_pw1"][j, :, c0:c0 + 128].rearrange("(k p) c -> p k c", p=128), [], [wk])
                    kb.cp("act", wab[cq][a][:], w_[:], [wk], ["wab%d_%d" % (cq, a)])
                for k in range(31):
                    kb.ts("dve" if k % 2 else "pool", DG[cq][:, k, :], idf[:], wdw[:, cc, k:k + 1], None, ALU.mult, None, ["idf", "wdw"], ["DG%d" % cq])
                for (t0, n, isctx, hoff) in blocks:
                    q = it % 2; it += 1
                    for k in range(NK):
                        kb.mm(pa[q][:, 0:n], wab[cq][0][:, k, :], uT[:, k, t0:t0 + n], k == 0, k == NK - 1, ["wab%d_0" % cq, "uT"], ["pa%d" % q])
                    for k in range(NK):
                        kb.mm(pg[q][:, 0:n], wab[cq][1][:, k, :], uT[:, k, t0:t0 + n], k == 0, k == NK - 1, ["wab%d_1" % cq, "uT"], ["pg%d" % q])
                    kb.act(sig[q][:, 0:n], pg[q][:, 0:n], AF.Sigmoid, ["pg%d" % q, "bp"], ["sig%d" % q], bias=bp[:, NK + cc:NK + cc + 1])
                    hbuf, hk = (hc, "hc") if isctx else (hl, "hl")
                    kb.stt(hbuf[:, hoff:hoff + n], pa[q][:, 0:n], bp[:, cc:cc + 1], sig[q][:, 0:n], ALU.add, ALU.mult, ["pa%d" % q, "bp", "sig%d" % q], [hk])
                for (t0, n, isctx, hoff) in blocks:
                    q = it % 2; it += 1
                    hbuf, hk = (hc, "hc") if isctx else (hl, "hl")
                    step = 1 if isctx else GW
                    base = (hoff - 15) if isctx else (hoff - 15 * GW)
                    for k in range(31):
                        kb.mm(pc[q][:, 0:n], DG[cq][:, k, :], hbuf[:, base + k * step:base + k * step + n], k == 0, k == 30, ["DG%d" % cq, hk], ["pc%d" % q])
                    kb.act(cvs[q][:, 0:n], pc[q][:, 0:n], AF.Identity, ["pc%d" % q, "bdw"], ["cvs%d" % q], bias=bdw[:, cc:cc + 1])
                    na = n // 128
                    for a in range(na):
                        kb.tr(pt[q][:, a * 128:(a + 1) * 128], cvs[q][:, a * 128:(a + 1) * 128], idf[:], ["cvs%d" % q, "idf"], ["pt%d" % q])
                    kb.cp("dve", cvt[q][:, 0:na, :], pt[q][:, 0:n].rearrange("p (a c) -> p a c", a=na), ["pt%d" % q], ["cvt%d" % q])
                    kb.dma("pool", CV[t0:t0 + n, cc * 128:(cc + 1) * 128].rearrange("(a p) c -> p a c", p=128), cvt[q][:, 0:na, :], ["cvt%d" % q], ["CV"])
        f.barrier()
    with ExitStack() as es:
        sb = lambda n, s, dt=F32: kb.sb(es, n, s, dt)
        idf = sb("idf", [128, 128]); idb = sb("idb", [128, 128], BF16)
        kb.dma("sp", idf[:], io["ident"], [], ["idf"])
        kb.cp("dve", idb[:], idf[:], ["idf"], ["idb"])
        lng = sb("lng", [128, D]); lnb = sb("lnb", [128, D]); b2 = sb("b2", [128, D])
        kb.dma("sp", lng[:], bcast_rows(io["cv_ln_g"][j:j + 1, :], 128), [], ["lng"])
        kb.dma("sp", lnb[:], bcast_rows(io["cv_ln_b"][j:j + 1, :], 128), [], ["lnb"])
        kb.dma("sp", b2[:], bcast_rows(io["cv_b_pw2"][j, 0:1, :], 128), [], ["b2"])
        g1t = [sb("g1t%d" % a, [128, D]) for a in range(2 if with_ctx else 1)]
        for a in range(len(g1t)):
            kb.dma("sp", g1t[a][:], bcast_rows(MOD[i, 6 * a + 2:6 * a + 3, :], 128), [], ["g1t"])
        W2 = sb("W2", [128, NK, D], BF16)
        wst = [sb("wst%d" % q, [128, D]) for q in range(2)]
        for k in range(NK):
            kb.dma("sp", wst[k % 2][:], io["cv_w_pw2"][j, k * 128:(k + 1) * 128, :], [], ["wst%d" % (k % 2)])
            kb.cp("act", W2[:, k, :], wst[k % 2][:], ["wst%d" % (k % 2)], ["W2"])
        cvt = [sb("cvt%d" % q, [128, D]) for q in range(3)]
        sbf = [sb("sbf%d" % q, [128, D], BF16) for q in range(3)]
        sT = [sb("sT%d" % q, [128, NK, 128], BF16) for q in range(3)]
        xt = [sb("xt%d" % q, [128, D]) for q in range(2)]
        yt = [sb("yt%d" % q, [128, D]) for q in range(2)]
        st = [sb("st%d" % q, [128, max(1, D // 512), 6]) for q in range(3)]
        mv = [sb("mv%d" % q, [128, 2]) for q in range(3)]
        sd = [sb("sd%d" % q, [128, 4]) for q in range(3)]
        pT = [kb.ps(es, "pT%d" % q, [128, 512], BF16) for q in range(2)]
        po = [kb.ps(es, "po%d" % q, [128, 512]) for q in range(2)]
        r = 0; pr = 0
        for tt in range(ntiles):
            q = tt % 3
            q2 = tt % 2
            isctx = 1 if tt >= NLT else 0
            rows_ = slice(tt * 128, (tt + 1) * 128)
            kb.dma("sp", cvt[q][:], CV[rows_, :], ["CV"], ["cvt%d" % q])
            kb.dma("sp", xt[q2][:], XR[rows_, :], [], ["xt%d" % q2])
            ln_tile(kb, "pool", cvt[q], "cvt%d" % q, mv[q], st[q], sd[q], lng, lnb)
            kb.act(sbf[q][:], cvt[q][:], AF.Silu, ["cvt%d" % q], ["sbf%d" % q])
            for k4 in range(0, NK, 4):
                nk = min(4, NK - k4)
                p = pT[r % 2]; pk = "pT%d" % (r % 2); r += 1
                for a in range(nk):
                    kb.tr(p[:, a * 128:(a + 1) * 128], sbf[q][:, (k4 + a) * 128:(k4 + a + 1) * 128], idb[:], ["sbf%d" % q, "idb"], [pk])
                kb.cp("act" if (r % 2) else "dve", sT[q][:, k4:k4 + nk, :], p[:, 0:nk * 128].rearrange("p (a c) -> p a c", a=nk), [pk], ["sT%d" % q])
            for nt in range(D // NT):
                cs_ = slice(nt * NT, (nt + 1) * NT)
                pq = pr % 2; pr += 1
                for k in range(NK):
                    kb.mm(po[pq][:, 0:NT], sT[q][:, k, :], W2[:, k, cs_], k == 0, k == NK - 1, ["sT%d" % q, "W2"], ["po%d" % pq])
                kb.tt("dve", yt[q2][:, cs_], po[pq][:, 0:NT], b2[:, cs_], ALU.add, ["po%d" % pq, "b2"], ["yt%d" % q2])
            kb.tt("pool", yt[q2][:], yt[q2][:], g1t[isctx][:], ALU.mult, ["yt%d" % q2, "g1t"], ["yt%d" % q2])
            kb.stt(yt[q2][:], xt[q2][:], cfg.ALPHA, yt[q2][:], ALU.mult, ALU.add, ["xt%d" % q2, "yt%d" % q2], ["yt%d" % q2])
            kb.dma("pool", TS[rows_, :], yt[q2][:], ["yt%d" % q2], ["TS"])
    f.barrier()
    postnorm(kb, i, 0, io, scr, with_ctx, (4, 3), scr["U2"])


def declare_io(kb):
    cfg = kb.cfg
    D, L, LC, DEPTH, NK, NB = cfg.D, cfg.L, cfg.LC, cfg.DEPTH, cfg.NK, cfg.NB
    NS5, NCV, E, F = cfg.NS5, cfg.NCV, cfg.E, cfg.F
    io = {}
    io["x"] = kb.din("x", [L, D]); io["ctx"] = kb.din("ctx", [LC, D])
    io["cond"] = kb.din("cond", [128, NK, 2])
    io["ada_w"] = kb.din("ada_w", [DEPTH, D, 6 * D]); io["ada_b"] = kb.din("ada_b", [DEPTH, 1, 6 * D])
    io["ln_g"] = kb.din("ln_g", [DEPTH * 2, D]); io["ln_b"] = kb.din("ln_b", [DEPTH * 2, D])
    io["ident"] = kb.din("ident", [128, 128])
    io["iota_f"] = kb.din("iota_f", [128, cfg.TALL]); io["tokid"] = kb.din("tokid", [128, 18])
    io["sel"] = kb.din("sel", [16, 16, 128])
    io["shift"] = kb.din("shift", [cfg.CAPC, 128 // cfg.CAPC, 128])
    io["s5_kt"] = kb.din("s5_kt", [128, 2, 4, 2, 8]); io["s5_mask"] = kb.din("s5_mask", [128, 2, 128])
    io["s5_lam"] = kb.din("s5_lam", [NS5, 2, NB, 128, 3, 8])
    io["s5_B"] = kb.din("s5_B", [NS5, 2, NB, 128, 2, 8, 16]); io["s5_C"] = kb.din("s5_C", [NS5, 2, NB, 128, 2, 8, 16])
    io["s5_d"] = kb.din("s5_d", [NS5, 1, D])
    io["s5_w_glu"] = kb.din("s5_w_glu", [NS5, D, 2 * D]); io["s5_b_glu"] = kb.din("s5_b_glu", [NS5, 1, 2 * D])
    nc_ = max(NCV, 1)
    io["cv_w_pw1"] = kb.din("cv_w_pw1", [nc_, D, 2 * D]); io["cv_b_pw1"] = kb.din("cv_b_pw1", [nc_, 128, 2 * NK])
    io["cv_w_dw"] = kb.din("cv_w_dw", [nc_, 128, NK, 31]); io["cv_b_dw"] = kb.din("cv_b_dw", [nc_, 128, NK])
    io["cv_ln_g"] = kb.din("cv_ln_g", [nc_, D]); io["cv_ln_b"] = kb.din("cv_ln_b", [nc_, D])
    io["cv_w_pw2"] = kb.din("cv_w_pw2", [nc_, D, D]); io["cv_b_pw2"] = kb.din("cv_b_pw2", [nc_, 1, D])
    io["moe_w_router"] = kb.din("moe_w_router", [DEPTH, D, E])
    io["moe_w_in"] = kb.din("moe_w_in", [DEPTH, E, D, 2 * F]); io["moe_w_out"] = kb.din("moe_w_out", [DEPTH, E, F, D])
    io["out"] = kb.nc.dram_tensor("out", [L, D], F32, kind="ExternalOutput").ap()
    return io


def declare_scratch(kb):
    cfg = kb.cfg
    D, TALL = cfg.D, cfg.TALL
    scr = {}
    scr["XR"] = kb.dscr("XR", [TALL, D])
    scr["TS"] = kb.dscr("TS", [TALL, D])
    scr["MOD"] = kb.dscr("MOD", [cfg.DEPTH, 12, D])
    scr["ZS"] = kb.dscr("ZS", [TALL, D], BF16)
    scr["U2"] = kb.dscr("U2", [TALL, D], BF16)
    scr["CV"] = kb.dscr("CV", [TALL, D])
    scr["SW"] = kb.dscr("SW", [cfg.NS5, 2, cfg.NB, 128, 8, 6, 128], BF16)
    scr["A8D"] = kb.dscr("A8D", [cfg.NS5, 128, cfg.NB * 2 * 5 * 2 * 2 * 8])
    scr["YG"] = kb.dscr("YG", [cfg.E, cfg.CAPL + cfg.CAPC, D], BF16)
    return scr


def build(cfg, debug_outs=(), stop_after=None):
    kb = KB(cfg, debug_outs)
    io = declare_io(kb)
    scr = declare_scratch(kb)
    f = kb.f
    kb.dma("sp", scr["XR"][0:cfg.L, :], io["x"], [], ["XR"])
    kb.dma("sp", scr["XR"][cfg.L:cfg.TALL, :], io["ctx"], [], ["XR"])
    prologue(kb, io, scr)
    done = False
    for i in range(cfg.DEPTH):
        is_s5 = (i % 2) == 0
        j = i // 2
        ctx_out = any((k % 2) == 0 for k in range(i + 1, cfg.DEPTH))
        last = i == cfg.DEPTH - 1
        if is_s5:
            s5_layer(kb, i, j, io, scr, ctx_out)
            if stop_after == ("s5", i):
                break
            glu_stage(kb, i, j, io, scr, ctx_out)
            if stop_after == ("glu", i):
                break
            postnorm(kb, i, 0, io, scr, ctx_out, (4, 3), scr["U2"])
        else:
            conv_layer(kb, i, j, io, scr, ctx_out)
        if stop_after == ("mix", i):
            break
        moe_layer(kb, i, io, scr, ctx_out)
        if stop_after == ("moe", i):
            break
        postnorm(kb, i, 1, io, scr, ctx_out, None, None, final_out=io["out"] if last else None)
    if not kb.out_stores:
        with ExitStack() as es:
            t = kb.sb(es, "dbg", [128, cfg.D])
            kb.dma("sp", t[:], scr["XR"][0:128, :], ["XR"], ["dbg"])
            kb.out_stores.append(kb.dma("pool", io["out"][0:128, :], t[:], ["dbg"], ["OUTF"]))
    f.barrier()
    f.emit(final_wait_ops=kb.out_stores)
    return kb.nc


def host_consts(cfg):
    c = {}
    c["ident"] = np.eye(128, dtype=np.float32)
    c["iota_f"] = np.tile(np.arange(cfg.TALL, dtype=np.float32)[None, :], (128, 1))
    c["tokid"] = (np.arange(128, dtype=np.float32)[:, None] + 128.0 * np.arange(18, dtype=np.float32)[None, :]).astype(np.float32)
    sel = np.zeros((16, 16, 128), np.float32)
    for e_ in range(16):
        sel[e_, e_, :] = 1.0
    c["sel"] = sel
    ep = 128 // cfg.CAPC
    sh = np.zeros((cfg.CAPC, ep, 128), np.float32)
    for e4 in range(ep):
        for s_ in range(cfg.CAPC):
            sh[s_, e4, e4 * cfg.CAPC + s_] = 1.0
    c["shift"] = sh
    kt = np.zeros((2, 4, 8), np.float64)
    idx = np.arange(8)
    kt[0, 0] = idx + 1; kt[0, 1] = -(idx + 1); kt[0, 2] = 7 - idx
    kt[1, 0] = 8 - idx; kt[1, 1] = idx - 8; kt[1, 2] = idx
    kt[:, 3, 0] = 1; kt[:, 3, 1] = 8
    ktt = np.stack([kt, kt / (2.0 * math.pi)], axis=2)
    c["s5_kt"] = np.ascontiguousarray(np.broadcast_to(ktt[None], (128, 2, 4, 2, 8))).astype(np.float32)
    s_idx = np.arange(128) // 16
    mf = (s_idx[None, :] >= s_idx[:, None]).astype(np.float32)
    mb = (s_idx[None, :] <= s_idx[:, None]).astype(np.float32)
    c["s5_mask"] = np.ascontiguousarray(np.stack([mf, mb], axis=1))
    return c


def host_layout(cfg, inp, b):
    D, NK, NB, NS5, NCV = cfg.D, cfg.NK, cfg.NB, cfg.NS5, cfg.NCV
    f32 = np.float32
    m = {}
    m["x"] = np.ascontiguousarray(inp["x"][b], f32)
    m["ctx"] = np.ascontiguousarray(inp["ctx"][b], f32)
    cond = np.stack([np.asarray(inp["c"][b]).reshape(NK, 128).T, np.asarray(inp["c_ctx"]).reshape(NK, 128).T], axis=2)
    m["cond"] = np.ascontiguousarray(cond, f32)
    m["ada_w"] = np.asarray(inp["ada_w"], f32)
    m["ada_b"] = np.asarray(inp["ada_b"], f32)[:, None, :]
    m["ln_g"] = np.asarray(inp["ln_g"], f32).reshape(-1, D)
    m["ln_b"] = np.asarray(inp["ln_b"], f32).reshape(-1, D)

    def glay(a):
        a = np.asarray(a, f32).reshape(NS5, 2, NB, 2, 8, 64)
        return a.transpose(0, 1, 2, 3, 5, 4).reshape(NS5, 2, NB, 128, 8)
    ldt = np.broadcast_to(np.asarray(inp["s5_log_dt"], f32)[..., None], np.asarray(inp["s5_a_re"]).shape)
    m["s5_lam"] = np.ascontiguousarray(np.stack([glay(inp["s5_a_re"]), glay(inp["s5_a_im"]), glay(ldt)], axis=4))

    def blay(re, im):
        out = []
        for a in (re, im):
            a = np.asarray(a, f32).reshape(NS5, 2, NB, 2, 8, 64, 16)
            out.append(a.transpose(0, 1, 2, 3, 5, 4, 6).reshape(NS5, 2, NB, 128, 8, 16))
        return np.ascontiguousarray(np.stack(out, axis=4))
    m["s5_B"] = blay(inp["s5_b_re"], inp["s5_b_im"])
    cre = np.asarray(inp["s5_c_re"], f32).transpose(0, 1, 2, 4, 3)
    cim = np.asarray(inp["s5_c_im"], f32).transpose(0, 1, 2, 4, 3)
    m["s5_C"] = blay(cre, cim)
    m["s5_d"] = np.asarray(inp["s5_d"], f32)[:, None, :]
    m["s5_w_glu"] = np.asarray(inp["s5_w_glu"], f32)
    m["s5_b_glu"] = np.asarray(inp["s5_b_glu"], f32)[:, None, :]
    n_ = max(NCV, 1)

    def pad0(a, shape):
        a = np.asarray(a, f32)
        if a.shape[0] == 0:
            return np.zeros(shape, f32)
        return np.ascontiguousarray(a.reshape(shape))
    m["cv_w_pw1"] = pad0(inp["cv_w_pw1"], (n_, D, 2 * D))
    bp = np.asarray(inp["cv_b_pw1"], f32)
    m["cv_b_pw1"] = np.ascontiguousarray(bp.reshape(-1, 2 * NK, 128).transpose(0, 2, 1)) if bp.shape[0] else np.zeros((n_, 128, 2 * NK), f32)
    wd = np.asarray(inp["cv_w_dw"], f32)
    m["cv_w_dw"] = np.ascontiguousarray(wd.reshape(-1, 31, NK, 128).transpose(0, 3, 2, 1)) if wd.shape[0] else np.zeros((n_, 128, NK, 31), f32)
    bd = np.asarray(inp["cv_b_dw"], f32)
    m["cv_b_dw"] = np.ascontiguousarray(bd.reshape(-1, NK, 128).transpose(0, 2, 1)) if bd.shape[0] else np.zeros((n_, 128, NK), f32)
    m["cv_ln_g"] = pad0(inp["cv_ln_g"], (n_, D)); m["cv_ln_b"] = pad0(inp["cv_ln_b"], (n_, D))
    m["cv_w_pw2"] = pad0(inp["cv_w_pw2"], (n_, D, D)); m["cv_b_pw2"] = pad0(inp["cv_b_pw2"], (n_, 1, D))
    m["moe_w_router"] = np.asarray(inp["moe_w_router"], f32)
    m["moe_w_in"] = np.asarray(inp["moe_w_in"], f32)
    m["moe_w_out"] = np.asarray(inp["moe_w_out"], f32)
    m.update(host_consts(cfg))
    return m


def moe_layer(kb, i, io, scr, with_ctx):
    cfg, f = kb.cfg, kb.f
    D, NK, NT, E, F, NF, L, LC = cfg.D, cfg.NK, cfg.NT, cfg.E, cfg.F, cfg.NF, cfg.L, cfg.LC
    XR, MOD, TS, U2, YG = scr["XR"], scr["MOD"], scr["TS"], scr["U2"], scr["YG"]
    CAPL, CAPC = cfg.CAPL, (cfg.CAPC if with_ctx else 0)
    NSL = CAPL + CAPC
    ntiles = cfg.NTT if with_ctx else cfg.NLT
    NLT = cfg.NLT
    sets = [(0, L, CAPL, 0, 0)]
    if with_ctx:
        sets.append((L, LC, CAPC, CAPL, 1))
    stiles = [(s * 128, min(128, CAPL - s * 128), 0) for s in range((CAPL + 127) // 128)]
    if with_ctx:
        stiles.append((CAPL, CAPC, 1))
    NST = len(stiles)
    with ExitStack() as esl:
        sbl = lambda n, s, dt=F32: kb.sb(esl, n, s, dt)
        IDXF = sbl("IDXF", [16, NSL]); GAT = sbl("GAT", [16, NSL])
        IDXT = sbl("IDXT", [128, NST, 16]); GT = sbl("GT", [128, NST, 16])
        idf = sbl("idf", [128, 128]); idb = sbl("idb", [128, 128], BF16)
        tokid = sbl("tokid", [128, 16 + 2])
        kb.dma("sp", idf[:], io["ident"], [], ["idf"])
        kb.cp("dve", idb[:], idf[:], ["idf"], ["idb"])
        kb.dma("sp", tokid[:], io["tokid"], [], ["tokid"])
        esu = ExitStack()
        U2TM = kb.sb(esu, "U2TM", [128, ntiles, D], BF16)
        with ExitStack() as es:
            sb = lambda n, s, dt=F32: kb.sb(es, n, s, dt)
            wrf = sb("wrf", [128, NK, E]); wrb = sb("wrb", [128, NK, E], BF16)
            kb.dma("sp", wrf[:], io["moe_w_router"][i].rearrange("(k p) e -> p k e", p=128), [], ["wrf"])
            kb.cp("dve", wrb[:], wrf[:], ["wrf"], ["wrb"])
            AFFT = sb("AFFT", [16, L + LC]); WK = sb("WK", [16, L])
            IDXU = sb("IDXU", [16, NSL], U32)
            uT = [sb("uT%d" % q, [128, NK, 128], BF16) for q in range(2)]
            lg = [sb("lg%d" % q, [128, E]) for q in range(2)]
            sm = [sb("smx%d" % q, [128, 4]) for q in range(2)]
            pT = [kb.ps(es, "pT%d" % q, [128, 512], BF16) for q in range(2)]
            pL = [kb.ps(es, "pL%d" % q, [128, 512]) for q in range(2)]
            pA = [kb.ps(es, "pA%d" % q, [128, 512]) for q in range(2)]
            r = 0
            for tt in range(ntiles):
                q = tt % 2
                uk = "U2TM%d" % tt
                kb.dma("sp", U2TM[:, tt, :], U2[tt * 128:(tt + 1) * 128, :], ["U2"], [uk])
                for k4 in range(0, NK, 4):
                    nk = min(4, NK - k4)
                    p = pT[r % 2]; pk = "pT%d" % (r % 2); r += 1
                    for a in range(nk):
                        kb.tr(p[:, a * 128:(a + 1) * 128], U2TM[:, tt, (k4 + a) * 128:(k4 + a + 1) * 128], idb[:], [uk, "idb"], [pk])
                    kb.cp("act" if (r % 2) else "dve", uT[q][:, k4:k4 + nk, :], p[:, 0:nk * 128].rearrange("p (a c) -> p a c", a=nk), [pk], ["uT%d" % q])
                for k in range(NK):
                    kb.mm(pL[q][:, 0:E], uT[q][:, k, :], wrb[:, k, :], k == 0, k == NK - 1, ["uT%d" % q, "wrb"], ["pL%d" % q])
                lk, sk = "lg%d" % q, "smx%d" % q
                kb.f.op("dve", lambda e, o=sm[q][:, 0:1], i_=pL[q][:, 0:E]: e.tensor_reduce(out=o, in_=i_, axis=mybir.AxisListType.X, op=ALU.max), ["pL%d" % q], [sk])
                kb.ts("dve", sm[q][:, 1:2], sm[q][:, 0:1], -1.0, None, ALU.mult, None, [sk], [sk])
                kb.act(lg[q][:], pL[q][:, 0:E], AF.Exp, ["pL%d" % q, sk], [lk], bias=sm[q][:, 1:2])
                kb.f.op("dve", lambda e, o=sm[q][:, 2:3], i_=lg[q][:]: e.tensor_reduce(out=o, in_=i_, axis=mybir.AxisListType.X, op=ALU.add), [lk], [sk])
                kb.f.op("dve", lambda e, o=sm[q][:, 3:4], i_=sm[q][:, 2:3]: e.reciprocal(out=o, in_=i_), [sk], [sk])
                kb.ts("dve", lg[q][:], lg[q][:], sm[q][:, 3:4], None, ALU.mult, None, [lk, sk], [lk])
                kb.tr(pA[q][0:E, 0:128], lg[q][:], idf[:], [lk, "idf"], ["pA%d" % q])
                kb.cp("act", AFFT[:, tt * 128:(tt + 1) * 128], pA[q][0:E, 0:128], ["pA%d" % q], ["AFFT"])
            for (tok0, ntok, cap, slot0, isctx) in sets:
                src = AFFT[:, tok0:tok0 + ntok]
                srck = "AFFT"
                for rd in range(cap // 8):
                    sl = slice(slot0 + rd * 8, slot0 + rd * 8 + 8)
                    kb.f.op("dve", lambda e, o=GAT[:, sl], i_=src: e.max(out=o, in_=i_), [srck], ["GAT"])
                    kb.f.op("dve", lambda e, o=IDXU[:, sl], m_=GAT[:, sl], v_=src: e.max_index(out=o, in_max=m_, in_values=v_), [srck, "GAT"], ["IDXU"])
                    if rd < cap // 8 - 1:
                        dst = WK[:, 0:ntok]
                        kb.f.op("dve", lambda e, o=dst, m_=GAT[:, sl], v_=src: e.match_replace(out=o, in_to_replace=m_, in_values=v_, imm_value=-1.0), [srck, "GAT"], ["WK"])
                        src = dst
                        srck = "WK"
            kb.cp("dve", IDXF[:], IDXU[:], ["IDXU"], ["IDXF"])
            if with_ctx:
                kb.ts("dve", IDXF[:, CAPL:NSL], IDXF[:, CAPL:NSL], float(L), None, ALU.add, None, ["IDXF"], ["IDXF"])
            for si, (s0, n, _) in enumerate(stiles):
                q = si % 2
                kb.tr(pA[q][0:n, 0:16], IDXF[:, s0:s0 + n], idf[0:16, 0:16], ["IDXF", "idf"], ["pA%d" % q])
                kb.cp("dve", IDXT[0:n, si, :], pA[q][0:n, 0:16], ["pA%d" % q], ["IDXT"])
                kb.tr(pL[q][0:n, 0:16], GAT[:, s0:s0 + n], idf[0:16, 0:16], ["GAT", "idf"], ["pL%d" % q])
                kb.cp("dve", GT[0:n, si, :], pL[q][0:n, 0:16], ["pL%d" % q], ["GT"])
        f.barrier()
        with ExitStack() as es:
            sb = lambda n, s, dt=F32: kb.sb(es, n, s, dt)
            SEL = sb("SEL", [16, 16, 128])
            kb.dma("sp", SEL[:], io["sel"], [], ["SEL"])
            XselT = [sb("XselT0", [128, NK, NSL], BF16)] * 2
            HT = sb("HT", [128, NF, NSL], BF16)
            Sx = [sb("Sx0", [128, ntiles, CAPL], BF16)] * 2
            NWIN = 3
            WIN = [sb("WIN%d" % q, [128, 2, NK, 128], BF16) for q in range(NWIN)]
            WOUT = [sb("WOUT%d" % q, [128, NF, D], BF16) for q in range(2)]
            sg = [sb("sg%d" % q, [128, NSL]) for q in range(2)]
            ygs = [sb("ygs%d" % q, [128, D], BF16) for q in range(2)]
            pb = kb.ps(es, "pb", [128, 512])
            pgx = [kb.ps(es, "pgx%d" % q, [128, 512]) for q in range(2)]
            ph = [kb.ps(es, "ph%d" % q, [128, 512]) for q in range(4)]
            py = kb.ps(es, "py", [128, 512])
            wi_r = 0; wo_r = 0; yg_r = 0; gx_r = 0
            sxk = "Sx0"

            def gen_onehot(ex):
                kb.mm(pb[:, 0:NSL], SEL[:, ex, :], IDXF[:, :], True, True, ["SEL", "IDXF"], ["pb"])
                for tt in range(ntiles):
                    isctx = 1 if tt >= NLT else 0
                    s0, cap = (CAPL, CAPC) if isctx else (0, CAPL)
                    kb.ts("dve", Sx[0][:, tt, 0:cap], pb[:, s0:s0 + cap], tokid[:, tt:tt + 1], None, ALU.is_equal, None, ["pb", "tokid"], [sxk])

            gen_onehot(0)
            for e_ in range(E):
                q = e_ % 2
                xk = "XselT0"
                for k in range(NK):
                    pg_ = pgx[gx_r % 2]; pgk = "pgx%d" % (gx_r % 2); gx_r += 1
                    for tt in range(NLT):
                        kb.mm(pg_[:, 0:CAPL], U2TM[:, tt, k * 128:(k + 1) * 128], Sx[q][:, tt, 0:CAPL], tt == 0, tt == NLT - 1, ["U2TM%d" % tt, sxk], [pgk])
                    if with_ctx:
                        for tt in range(NLT, ntiles):
                            kb.mm(pg_[:, CAPL:NSL], U2TM[:, tt, k * 128:(k + 1) * 128], Sx[q][:, tt, 0:CAPC], tt == NLT, tt == ntiles - 1, ["U2TM%d" % tt, sxk], [pgk])
                    kb.cp("act" if k % 2 else "dve", XselT[q][:, k, :], pg_[:, 0:NSL], [pgk], [xk])
                wo = WOUT[e_ % 2]; wok = "WOUT%d" % (e_ % 2)
                for fc in range(NF):
                    wq = wi_r % NWIN; wi_r += 1
                    for gu in range(2):
                        c0 = gu * F + fc * 128
                        kb.dma("pool", WIN[wq][:, gu, :, :], io["moe_w_in"][i, e_, :, c0:c0 + 128].rearrange("(k p) c -> p k c", p=128), [], ["WIN%d_%d" % (wq, gu)])
                    kb.dma("pool", wo[:, fc, :], io["moe_w_out"][i, e_, fc * 128:(fc + 1) * 128, :], [], [wok + "_%d" % fc])
                    hq = fc % 2
                    phg = ph[hq * 2]; phu = ph[hq * 2 + 1]
                    pgk_, puk_ = "ph%d" % (hq * 2), "ph%d" % (hq * 2 + 1)
                    for k in range(NK):
                        kb.mm(phg[:, 0:NSL], WIN[wq][:, 0, k, :], XselT[q][:, k, :], k == 0, k == NK - 1, ["WIN%d_0" % wq, xk], [pgk_])
                    for k in range(NK):
                        kb.mm(phu[:, 0:NSL], WIN[wq][:, 1, k, :], XselT[q][:, k, :], k == 0, k == NK - 1, ["WIN%d_1" % wq, xk], [puk_])
                    kb.act(sg[hq][:], phg[:, 0:NSL], AF.Silu, [pgk_], ["sg%d" % hq])
                    kb.tt("dve", HT[:, fc, :], phu[:, 0:NSL], sg[hq][:], ALU.mult, [puk_, "sg%d" % hq], ["HT"])
                if e_ + 1 < E:
                    gen_onehot(e_ + 1)
                for si, (s0, n, _) in enumerate(stiles):
                    yq = yg_r % 2; yg_r += 1
                    for nt in range(D // NT):
                        for fk in range(NF):
                            kb.mm(py[0:n, 0:NT], HT[:, fk, s0:s0 + n], wo[:, fk, nt * NT:(nt + 1) * NT], fk == 0, fk == NF - 1, ["HT", wok + "_%d" % fk], ["py"])
                        if nt % 2:
                            kb.ts("dve", ygs[yq][0:n, nt * NT:(nt + 1) * NT], py[0:n, 0:NT], GT[0:n, si, e_:e_ + 1], None, ALU.mult, None, ["py", "GT"], ["ygs%d" % yq])
                        else:
                            kb.act(ygs[yq][0:n, nt * NT:(nt + 1) * NT], py[0:n, 0:NT], AF.Copy, ["py", "GT"], ["ygs%d" % yq], scale=GT[0:n, si, e_:e_ + 1])
                    kb.dma("sp", YG[e_, s0:s0 + n, :], ygs[yq][0:n, :], ["ygs%d" % yq], ["YG"])
        f.barrier()
        esu.close()
        with ExitStack() as es:
            sb = lambda n, s, dt=F32: kb.sb(es, n, s, dt)
            NSTL = (CAPL + 127) // 128
            PL = min(128, CAPL)
            iot = sb("iot", [128, L + LC])
            kb.dma("sp", iot[:], io["iota_f"], [], ["iot"])
            YGL = sb("YGL", [128, E, NSTL, D], BF16)
            for e_ in range(E):
                kb.dma("sp" if e_ % 2 else "pool", YGL[0:PL, e_, :, :], YG[e_, 0:CAPL, :].rearrange("(s p) c -> p s c", p=PL), ["YG"], ["YGL%d" % e_])
            g2t = sb("g2t", [128, 2, D])
            for a_ in range(2 if with_ctx else 1):
                kb.dma("sp", g2t[:, a_, :], bcast_rows(MOD[i, 6 * a_ + 5:6 * a_ + 6, :], 128), [], ["g2t"])
            NEQ = (E * CAPC + 127) // 128 if with_ctx else 0
            EP = 128 // CAPC if with_ctx else 1
            if with_ctx:
                YGC = sb("YGC", [128, NEQ, D], BF16)
                ygv = YG[:, CAPL:NSL, :].rearrange("(eq e4) s c -> e4 s eq c", e4=EP)
                for e4 in range(EP):
                    kb.dma("sp", YGC[e4 * CAPC:(e4 + 1) * CAPC, :, :], ygv[e4], ["YG"], ["YGC"])
                shf = sb("shf", [CAPC, EP, 128]); IDXC = sb("IDXC", [128, NEQ])
                kb.dma("sp", shf[:], io["shift"], [], ["shf"])
                pI = kb.ps(es, "pI", [128, 512])
                for e4 in range(EP):
                    rhs = fview(IDXT, NSTL * 16 + e4, [(EP, NEQ)], pn=CAPC)
                    kb.mm(pI[:, 0:NEQ], shf[:, e4, :], rhs, e4 == 0, e4 == EP - 1, ["shf", "IDXT"], ["pI"])
                kb.cp("dve", IDXC[:], pI[:, 0:NEQ], ["pI"], ["IDXC"])
                STC = [sb("STC%d" % q, [128, NEQ, 128], BF16) for q in range(2)]
            STL = [sb("STL%d" % q, [128, E * NSTL, 128], BF16) for q in range(2)]
            xs = [sb("xs%d" % q, [128, NT]) for q in range(2)]
            ft = [sb("ft%d" % q, [128, NT]) for q in range(2)]
            pf = [kb.ps(es, "pf%d" % q, [128, 512]) for q in range(2)]
            it = 0
            for tt in range(ntiles):
                tq = tt % 2
                isctx = 1 if tt >= NLT else 0
                rows = slice(tt * 128, (tt + 1) * 128)
                if not isctx:
                    terms = [(e_, s_) for e_ in range(E) for s_ in range(NSTL)]
                    for ti, (e_, s_) in enumerate(terms):
                        pl_ = ti % 3 == 2
                        kb.ts("pool" if pl_ else "dve", STL[tq][0:PL, ti, :], iot[0:PL, tt * 128:(tt + 1) * 128], IDXT[0:PL, s_, e_:e_ + 1], None,
                              ALU.is_equal, None, ["iot", "IDXT"], ["STL%d_%d_%d" % (tq, int(pl_), ti)])
                else:
                    for eq in range(NEQ):
                        kb.ts("dve", STC[tq][:, eq, :], iot[:, tt * 128:(tt + 1) * 128], IDXC[:, eq:eq + 1], None, ALU.is_equal, None, ["iot", "IDXC"], ["STC%d_%d" % (tq, eq)])
                for nt in range(D // NT):
                    cs_ = slice(nt * NT, (nt + 1) * NT)
                    q = it % 2; it += 1
                    kb.dma("sp", xs[q][:], XR[rows, cs_], [], ["xs%d" % q])
                    if not isctx:
                        for ti, (e_, s_) in enumerate(terms):
                            kb.mm(pf[q][:, 0:NT], STL[tq][0:PL, ti, :], YGL[0:PL, e_, s_, cs_], ti == 0, ti == len(terms) - 1,
                                  ["STL%d_%d_%d" % (tq, int(ti % 3 == 2), ti), "YGL%d" % e_], ["pf%d" % q])
                    else:
                        for eq in range(NEQ):
                            kb.mm(pf[q][:, 0:NT], STC[tq][:, eq, :], YGC[:, eq, cs_], eq == 0, eq == NEQ - 1, ["STC%d_%d" % (tq, eq), "YGC"], ["pf%d" % q])
                    kb.tt("dve", ft[q][:], pf[q][:, 0:NT], g2t[:, isctx, cs_], ALU.mult, ["pf%d" % q, "g2t"], ["ft%d" % q])
                    kb.stt(ft[q][:], xs[q][:], cfg.ALPHA, ft[q][:], ALU.mult, ALU.add, ["xs%d" % q, "ft%d" % q], ["ft%d" % q])
                    kb.dma("pool", TS[rows, cs_], ft[q][:], ["ft%d" % q], ["TS"])
    f.barrier()


def kernel(**inputs):
    cfg = Cfg()
    nb = int(np.asarray(inputs["x"]).shape[0])
    nc = build(cfg)
    maps = [host_layout(cfg, inputs, b) for b in range(nb)]
    res = run_bass_kernel_spmd(nc, maps, core_ids=list(range(nb)))
    out = np.stack([np.asarray(res.results[b]["out"]) for b in range(nb)])
    return out.astype(np.float32)
```
